# Optimizing a Trainium2 kernel written in Bass

```python
import math
import jax, jax.numpy as jnp
from jax import lax
import numpy as np

D_MODEL = 1024
BATCH = 8
SEQ = 4096
DEPTH = 2

CHUNK = 64
Q_BLOCK = 128
MEM_LEN = 256
ROPE_THETA = 10000.0
NORM_EPS = 1e-6

D_FF = 2816
DIFF_HEADS = 4
DIFF_HEAD_DIM = 64
DIFF_V_DIM = 2 * DIFF_HEAD_DIM
SB_HEADS = 8
SB_HEAD_DIM = 64
A_QK_WIDTH = 2 * DIFF_HEADS * DIFF_HEAD_DIM
A_V_WIDTH = DIFF_HEADS * DIFF_V_DIM
B_WIDTH = SB_HEADS * SB_HEAD_DIM
AB_SPLITS = (A_QK_WIDTH, 2 * A_QK_WIDTH, 2 * A_QK_WIDTH + A_V_WIDTH,
             2 * A_QK_WIDTH + A_V_WIDTH + B_WIDTH, 2 * A_QK_WIDTH + A_V_WIDTH + 2 * B_WIDTH)
AB_IN_WIDTH = 2 * A_QK_WIDTH + A_V_WIDTH + 3 * B_WIDTH
AB_MIX_WIDTH = A_V_WIDTH + B_WIDTH
MLA_HEADS = 16
MLA_Q_RANK = 512
MLA_KV_RANK = 256
MLA_NOPE_DIM = 64
MLA_ROPE_DIM = 32
MLA_V_DIM = 64
MLA_QK_DIM = MLA_NOPE_DIM + MLA_ROPE_DIM
XM_HEADS = 4
XM_HEAD_DIM = 128
XM_WIDTH = XM_HEADS * XM_HEAD_DIM

N_EVEN = (DEPTH + 1) // 2
N_ODD = DEPTH // 2

kernel_name = 'hybrid_diff_stickbreak_mla_macaron'


def rms_norm(x, gain):
    xf = x.astype(jnp.float32)
    y = xf * lax.rsqrt(jnp.mean(xf * xf, axis=-1, keepdims=True) + NORM_EPS)
    return (y * gain.astype(jnp.float32)).astype(x.dtype)


def rope(x):
    s, d = x.shape[-2], x.shape[-1]
    inv = 1.0 / (ROPE_THETA ** (jnp.arange(0, d, 2, dtype=jnp.float32) / d))
    ang = jnp.arange(s, dtype=jnp.float32)[:, None] * inv[None, :]
    cos, sin = jnp.cos(ang), jnp.sin(ang)
    xf = x.astype(jnp.float32)
    x1, x2 = xf[..., : d // 2], xf[..., d // 2:]
    return jnp.concatenate([x1 * cos - x2 * sin, x1 * sin + x2 * cos], axis=-1).astype(x.dtype)


def to_heads(t, n_heads):
    b, s, _ = t.shape
    return t.reshape(b, s, n_heads, -1).transpose(0, 2, 1, 3)


def from_heads(t):
    b, h, s, d = t.shape
    return t.transpose(0, 2, 1, 3).reshape(b, s, h * d)


def chunk_causal_mask(q_pos, k_pos):
    return (k_pos[None, :] // CHUNK) <= (q_pos[:, None] // CHUNK)


def sweep_query_blocks(block_fn, q):
    b, h, s, d = q.shape
    nb = s // Q_BLOCK
    qb = jnp.moveaxis(q.reshape(b, h, nb, Q_BLOCK, d), 2, 0)
    starts = jnp.arange(nb, dtype=jnp.int32) * Q_BLOCK
    out = lax.map(lambda a: block_fn(a[0], a[1]), (qb, starts))
    _, ob, oh, _, dv = out.shape
    return jnp.moveaxis(out, 0, 2).reshape(ob, oh, s, dv)


def chunk_causal_softmax_attention(q, k, v, scale):
    k_pos = jnp.arange(k.shape[2], dtype=jnp.int32)

    def block(qb, start):
        q_pos = start + jnp.arange(Q_BLOCK, dtype=jnp.int32)
        sc = jnp.einsum('bhqd,bhkd->bhqk', qb, k).astype(jnp.float32) * scale
        sc = jnp.where(chunk_causal_mask(q_pos, k_pos), sc, -jnp.inf)
        p = jax.nn.softmax(sc, axis=-1)
        return jnp.einsum('bhqk,bhkd->bhqd', p.astype(v.dtype), v)

    return sweep_query_blocks(block, q)


def differential_attention(q, k, v, lam, scale):
    n_h = v.shape[1]
    k_pos = jnp.arange(k.shape[2], dtype=jnp.int32)

    def block(qb, start):
        q_pos = start + jnp.arange(Q_BLOCK, dtype=jnp.int32)
        sc = jnp.einsum('bhqd,bhkd->bhqk', qb, k).astype(jnp.float32) * scale
        sc = jnp.where(chunk_causal_mask(q_pos, k_pos), sc, -jnp.inf)
        p = jax.nn.softmax(sc, axis=-1)
        p = p[:, :n_h] - lam * p[:, n_h:]
        return jnp.einsum('bhqk,bhkd->bhqd', p.astype(v.dtype), v)

    return sweep_query_blocks(block, q)


def stick_breaking_attention(q, k, v, scale):
    k_pos = jnp.arange(k.shape[2], dtype=jnp.int32)

    def block(qb, start):
        q_pos = start + jnp.arange(Q_BLOCK, dtype=jnp.int32)
        z = jnp.einsum('bhqd,bhkd->bhqk', qb, k).astype(jnp.float32) * scale
        strict = k_pos[None, :] < q_pos[:, None]
        log_beta = jax.nn.log_sigmoid(z)
        log_keep = jnp.where(strict, jax.nn.log_sigmoid(-z), 0.0)
        later = lax.cumsum(log_keep, axis=3, reverse=True) - log_keep
        w = jnp.where(strict, jnp.exp(log_beta + later), 0.0)
        return jnp.einsum('bhqk,bhkd->bhqd', w.astype(v.dtype), v)

    return sweep_query_blocks(block, q)


def swiglu(h, w_gate, w_up, w_down):
    return (jax.nn.silu(h @ w_gate) * (h @ w_up)) @ w_down


def diff_stickbreak_mixer(h, w_in, w_out, q_norm, k_norm, lq1, lk1, lq2, lk2, subln, lambda_init):
    proj = h @ w_in
    qa, ka, va, qb, kb, vb = jnp.split(proj, AB_SPLITS, axis=-1)
    qa = rope(rms_norm(to_heads(qa, 2 * DIFF_HEADS), q_norm))
    ka = rope(rms_norm(to_heads(ka, 2 * DIFF_HEADS), k_norm))
    va = to_heads(va, DIFF_HEADS)
    f32 = jnp.float32
    lam = (jnp.exp(jnp.sum(lq1.astype(f32) * lk1.astype(f32)))
           - jnp.exp(jnp.sum(lq2.astype(f32) * lk2.astype(f32))) + lambda_init)
    oa = differential_attention(qa, ka, va, lam, DIFF_HEAD_DIM ** -0.5)
    oa = rms_norm(oa, subln) * (1.0 - lambda_init)
    qb = to_heads(qb, SB_HEADS)
    kb = to_heads(kb, SB_HEADS)
    vb = to_heads(vb, SB_HEADS)
    ob = stick_breaking_attention(qb, kb, vb, SB_HEAD_DIM ** -0.5)
    mixed = jnp.concatenate([from_heads(oa), from_heads(ob)], axis=-1)
    return mixed @ w_out


def mla_mixer(h, w_dq, q_lat_norm, w_uq, w_dkv, kv_lat_norm, w_ukv, qk_norm_q, qk_norm_k, w_o):
    b, s, _ = h.shape
    c_q = rms_norm(h @ w_dq, q_lat_norm)
    q = to_heads(c_q @ w_uq, MLA_HEADS)
    dkv = h @ w_dkv
    c_kv = rms_norm(dkv[..., :MLA_KV_RANK], kv_lat_norm)
    k_rope = jnp.broadcast_to(dkv[:, None, :, MLA_KV_RANK:], (b, MLA_HEADS, s, MLA_ROPE_DIM))
    kv = to_heads(c_kv @ w_ukv, MLA_HEADS)
    k_nope, v = kv[..., :MLA_NOPE_DIM], kv[..., MLA_NOPE_DIM:]
    k = jnp.concatenate([k_nope, k_rope], axis=-1)
    q = rms_norm(q, qk_norm_q)
    k = rms_norm(k, qk_norm_k)
    q = jnp.concatenate([q[..., :MLA_NOPE_DIM], rope(q[..., MLA_NOPE_DIM:])], axis=-1)
    k = jnp.concatenate([k[..., :MLA_NOPE_DIM], rope(k[..., MLA_NOPE_DIM:])], axis=-1)
    o = chunk_causal_softmax_attention(q, k, v, MLA_QK_DIM ** -0.5)
    return from_heads(o) @ w_o


def memory_cross_attention(h, m, w_q, w_kv, q_norm, k_norm, w_o):
    q = rms_norm(to_heads(h @ w_q, XM_HEADS), q_norm)
    k, v = jnp.split(m @ w_kv, 2, axis=-1)
    k = rms_norm(to_heads(k, XM_HEADS), k_norm)
    v = to_heads(v, XM_HEADS)
    sc = jnp.einsum('bhqd,bhkd->bhqk', q, k).astype(jnp.float32) * (XM_HEAD_DIM ** -0.5)
    p = jax.nn.softmax(sc, axis=-1)
    o = jnp.einsum('bhqk,bhkd->bhqd', p.astype(v.dtype), v)
    return from_heads(o) @ w_o


def setup_inputs(seed: int = 0) -> dict:
    key = jax.random.key(seed)
    keys = jax.random.split(key, 40)
    counter = [0]

    def nk():
        k = keys[counter[0]]
        counter[0] += 1
        return k

    def dense(shape, fan_in):
        return jax.random.normal(nk(), shape, jnp.float32) * (fan_in ** -0.5)

    def gain(shape):
        return 1.0 + 0.02 * jax.random.normal(nk(), shape, jnp.float32)

    def small(shape, scale):
        return scale * jax.random.normal(nk(), shape, jnp.float32)

    inputs = {}
    inputs['x'] = jax.random.normal(nk(), (BATCH, SEQ, D_MODEL), jnp.float32)
    inputs['mem'] = jax.random.normal(nk(), (BATCH, MEM_LEN, D_MODEL), jnp.float32)
    inputs['ffn_norm'] = gain((DEPTH, 2, D_MODEL))
    inputs['ffn_w_gate'] = dense((DEPTH, 2, D_MODEL, D_FF), D_MODEL)
    inputs['ffn_w_up'] = dense((DEPTH, 2, D_MODEL, D_FF), D_MODEL)
    inputs['ffn_w_down'] = dense((DEPTH, 2, D_FF, D_MODEL), D_FF)
    inputs['mix_norm'] = gain((DEPTH, D_MODEL))
    inputs['ab_w_in'] = dense((N_EVEN, D_MODEL, AB_IN_WIDTH), D_MODEL)
    inputs['ab_w_out'] = dense((N_EVEN, AB_MIX_WIDTH, D_MODEL), AB_MIX_WIDTH)
    inputs['diff_q_norm'] = gain((N_EVEN, DIFF_HEAD_DIM))
    inputs['diff_k_norm'] = gain((N_EVEN, DIFF_HEAD_DIM))
    inputs['diff_lambda_q1'] = small((N_EVEN, DIFF_HEAD_DIM), 0.1)
    inputs['diff_lambda_k1'] = small((N_EVEN, DIFF_HEAD_DIM), 0.1)
    inputs['diff_lambda_q2'] = small((N_EVEN, DIFF_HEAD_DIM), 0.1)
    inputs['diff_lambda_k2'] = small((N_EVEN, DIFF_HEAD_DIM), 0.1)
    inputs['diff_subln'] = gain((N_EVEN, DIFF_V_DIM))
    inputs['mla_w_dq'] = dense((N_ODD, D_MODEL, MLA_Q_RANK), D_MODEL)
    inputs['mla_q_norm'] = gain((N_ODD, MLA_Q_RANK))
    inputs['mla_w_uq'] = dense((N_ODD, MLA_Q_RANK, MLA_HEADS * MLA_QK_DIM), MLA_Q_RANK)
    inputs['mla_w_dkv'] = dense((N_ODD, D_MODEL, MLA_KV_RANK + MLA_ROPE_DIM), D_MODEL)
    inputs['mla_kv_norm'] = gain((N_ODD, MLA_KV_RANK))
    inputs['mla_w_ukv'] = dense((N_ODD, MLA_KV_RANK, MLA_HEADS * (MLA_NOPE_DIM + MLA_V_DIM)), MLA_KV_RANK)
    inputs['mla_qk_norm_q'] = gain((N_ODD, MLA_QK_DIM))
    inputs['mla_qk_norm_k'] = gain((N_ODD, MLA_QK_DIM))
    inputs['mla_w_o'] = dense((N_ODD, MLA_HEADS * MLA_V_DIM, D_MODEL), MLA_HEADS * MLA_V_DIM)
    inputs['xm_norm'] = gain((DEPTH, D_MODEL))
    inputs['xm_mem_norm'] = gain((DEPTH, D_MODEL))
    inputs['xm_w_q'] = dense((DEPTH, D_MODEL, XM_WIDTH), D_MODEL)
    inputs['xm_w_kv'] = dense((DEPTH, D_MODEL, 2 * XM_WIDTH), D_MODEL)
    inputs['xm_q_norm'] = gain((DEPTH, XM_HEAD_DIM))
    inputs['xm_k_norm'] = gain((DEPTH, XM_HEAD_DIM))
    inputs['xm_w_o'] = dense((DEPTH, XM_WIDTH, D_MODEL), XM_WIDTH)
    return inputs


def reference(x, mem, ffn_norm, ffn_w_gate, ffn_w_up, ffn_w_down, mix_norm,
              ab_w_in, ab_w_out, diff_q_norm, diff_k_norm, diff_lambda_q1, diff_lambda_k1,
              diff_lambda_q2, diff_lambda_k2, diff_subln,
              mla_w_dq, mla_q_norm, mla_w_uq, mla_w_dkv, mla_kv_norm, mla_w_ukv,
              mla_qk_norm_q, mla_qk_norm_k, mla_w_o,
              xm_norm, xm_mem_norm, xm_w_q, xm_w_kv, xm_q_norm, xm_k_norm, xm_w_o):
    for layer in range(DEPTH):
        i = layer // 2
        x = x + 0.5 * swiglu(rms_norm(x, ffn_norm[layer, 0]),
                             ffn_w_gate[layer, 0], ffn_w_up[layer, 0], ffn_w_down[layer, 0])
        h = rms_norm(x, mix_norm[layer])
        if layer % 2 == 0:
            lambda_init = 0.8 - 0.6 * math.exp(-0.3 * layer)
            x = x + diff_stickbreak_mixer(h, ab_w_in[i], ab_w_out[i], diff_q_norm[i], diff_k_norm[i],
                                          diff_lambda_q1[i], diff_lambda_k1[i],
                                          diff_lambda_q2[i], diff_lambda_k2[i],
                                          diff_subln[i], lambda_init)
        else:
            x = x + mla_mixer(h, mla_w_dq[i], mla_q_norm[i], mla_w_uq[i], mla_w_dkv[i],
                              mla_kv_norm[i], mla_w_ukv[i], mla_qk_norm_q[i], mla_qk_norm_k[i],
                              mla_w_o[i])
        x = x + memory_cross_attention(rms_norm(x, xm_norm[layer]), rms_norm(mem, xm_mem_norm[layer]),
                                       xm_w_q[layer], xm_w_kv[layer], xm_q_norm[layer],
                                       xm_k_norm[layer], xm_w_o[layer])
        x = x + 0.5 * swiglu(rms_norm(x, ffn_norm[layer, 1]),
                             ffn_w_gate[layer, 1], ffn_w_up[layer, 1], ffn_w_down[layer, 1])
    return x
```

```python
import math
import numpy as np
import concourse.bass as bass
import concourse.mybir as mybir
from concourse.bass_utils import run_bass_kernel_spmd

F32 = mybir.dt.float32
BF16 = mybir.dt.bfloat16
AF = mybir.ActivationFunctionType
ALU = mybir.AluOpType
AX = mybir.AxisListType
N_DMA_SEMS = 48
EPS = 1e-6
D = 1024
DFF = 2816
NFF = 22
MEM = 256


class Tr:
    __slots__ = ("lw", "rd")

    def __init__(self):
        self.lw = None
        self.rd = []


class Op:
    __slots__ = ("eng", "fn", "deps", "needed", "cnt", "is_dma", "dslot", "dval")

    def __init__(self, eng, fn, is_dma=False):
        self.eng = eng
        self.fn = fn
        self.deps = set()
        self.needed = False
        self.cnt = 0
        self.is_dma = is_dma
        self.dslot = -1
        self.dval = 0


class Prog:
    ENGS = ("pe", "act", "dve", "pool", "sp")

    def __init__(self, nc):
        self.nc = nc
        self.streams = {e: [] for e in self.ENGS}
        self.n_dma = 0
        self.dma_hist = []
        self.since_barrier = []
        self.order = []

    def _add_deps(self, op, reads, writes):
        for t in reads:
            if t.lw is not None:
                op.deps.add(t.lw)
        for t in writes:
            if t.lw is not None:
                op.deps.add(t.lw)
            for r in t.rd:
                op.deps.add(r)
        for t in reads:
            if not op.is_dma:
                t.rd = [r for r in t.rd if r.is_dma or r.eng != op.eng]
            t.rd.append(op)
        for t in writes:
            t.lw = op
            t.rd = []
        op.deps.discard(op)

    def op(self, eng, fn, reads=(), writes=()):
        o = Op(eng, fn)
        self._add_deps(o, reads, writes)
        self.streams[eng].append(o)
        self.order.append(o)
        return o

    def dma(self, queue, out_ap, in_ap, reads=(), writes=()):
        def fn(e, out_ap=out_ap, in_ap=in_ap):
            return e.dma_start(out=out_ap, in_=in_ap)
        o = Op(queue, fn, is_dma=True)
        i = self.n_dma
        self.n_dma += 1
        o.dslot = i % N_DMA_SEMS
        o.dval = 16 * (i // N_DMA_SEMS + 1)
        if i >= N_DMA_SEMS:
            o.deps.add(self.dma_hist[i - N_DMA_SEMS])
        self.dma_hist.append(o)
        self.since_barrier.append(o)
        self._add_deps(o, reads, writes)
        self.streams[queue].append(o)
        self.order.append(o)
        return o

    def barrier(self):
        lasts = []
        for e in self.ENGS:
            for o in reversed(self.streams[e]):
                if not o.is_dma:
                    lasts.append(o)
                    break
        dmas = list(self.since_barrier)
        self.since_barrier = []
        for e in self.ENGS:
            o = Op(e, lambda eng: eng.nop())
            o.deps.update(lasts)
            o.deps.update(dmas)
            self.streams[e].append(o)
            self.order.append(o)

    def emit(self, final_wait_ops=()):
        nc = self.nc
        for e in self.ENGS:
            for o in self.streams[e]:
                for d in o.deps:
                    if not d.is_dma:
                        if d.eng == "pe" and o.eng == "pe" and not o.is_dma:
                            continue
                        d.needed = True
        for e in self.ENGS:
            c = 0
            for o in self.streams[e]:
                if (not o.is_dma) and o.needed:
                    c += 1
                    o.cnt = c
        esem = {e: nc.semaphore("s_" + e).__enter__() for e in self.ENGS}
        dsem = [nc.semaphore("d%d" % i).__enter__() for i in range(N_DMA_SEMS)]
        engobj = {"pe": nc.tensor, "act": nc.scalar, "dve": nc.vector, "pool": nc.gpsimd, "sp": nc.sync}
        seen = {e: {} for e in self.ENGS}

        def do_waits(ename, deps):
            eng = engobj[ename]
            waits = {}
            for d in deps:
                if d.is_dma:
                    key = ("d", d.dslot)
                    val = d.dval
                else:
                    if d.eng == "pe" and ename == "pe":
                        continue
                    key = ("e", d.eng)
                    val = d.cnt
                if waits.get(key, 0) < val:
                    waits[key] = val
            sn = seen[ename]
            for key, val in waits.items():
                if sn.get(key, 0) >= val:
                    continue
                sn[key] = val
                sm = dsem[key[1]] if key[0] == "d" else esem[key[1]]
                eng.wait_ge(sm, val)
        for o in self.order:
            do_waits(o.eng, o.deps)
            ins = o.fn(engobj[o.eng])
            if o.is_dma:
                ins.then_inc(dsem[o.dslot], 16)
            elif o.needed:
                ins.then_inc(esem[o.eng], 1)
        do_waits("sp", final_wait_ops)
        import os
        if os.environ.get("KDEBUG"):
            print("SEMCOUNTS", {e: max([o.cnt for o in self.streams[e]] + [0]) for e in self.ENGS}, "ndma", self.n_dma, "nops", len(self.order))


class Buf:
    def __init__(self, handle, ntr=1):
        self.h = handle
        self.tr = [Tr() for _ in range(ntr)]

    def __getitem__(self, k):
        return self.h[k]

    @property
    def t(self):
        return self.tr[0]


SB_BASE = 16640
SB_TOP = 229376


class Ctx:
    def __init__(self, nc, S):
        self.nc = nc
        self.S = S
        self.NT = S // 128
        self.NG = S // 512
        self.P = Prog(nc)
        self.off = SB_BASE
        self.uid = 0
        self.psum = [Buf(nc.alloc_psum_tensor("ps%d" % i, [128, 512], F32)) for i in range(8)]
        self.rr = 0

    def sb(self, shape, dtype, ntr=1):
        n = 1
        for s in shape[1:]:
            n *= s
        nbytes = n * (4 if dtype == F32 else 2)
        nbytes = (nbytes + 63) // 64 * 64
        assert self.off + nbytes <= SB_TOP, ("SBUF overflow", self.off, nbytes)
        self.uid += 1
        h = self.nc.alloc_sbuf_tensor_at("t%d" % self.uid, list(shape), dtype, offset=self.off)
        self.off += nbytes
        return Buf(h, ntr)

    def mark(self):
        return self.off

    def release(self, m):
        self.off = m

    def eng_rr(self, engs=("dve", "pool")):
        self.rr += 1
        return engs[self.rr % len(engs)]


def copy_op(C, eng, out_ap, in_ap, reads, writes):
    if eng == "act":
        return C.P.op("act", lambda e: e.copy(out=out_ap, in_=in_ap), reads, writes)
    return C.P.op(eng, lambda e: e.tensor_copy(out=out_ap, in_=in_ap), reads, writes)


def load_weight(C, w_dram, K, N, stg, col0=0, ncols=None, engs=("dve", "pool")):
    ncols = N if ncols is None else ncols
    kc = (K + 127) // 128
    W = C.sb([128, kc, ncols], BF16)
    CH = stg[0].h.shape[1]
    i = 0
    for c in range(kc):
        rows = min(128, K - c * 128)
        for n0 in range(0, ncols, CH):
            n1 = min(ncols, n0 + CH)
            s = stg[C.rr % len(stg)]
            C.P.dma("sp", s[0:rows, 0:n1 - n0], w_dram[c * 128:c * 128 + rows, col0 + n0:col0 + n1], writes=[s.t])
            copy_op(C, C.eng_rr(engs), W[0:rows, c, n0:n1], s[0:rows, 0:n1 - n0], [s.t, W.t], [W.t])
    return W


def load_bcast(C, vec_dram, n):
    b = C.sb([128, n], F32)
    C.P.dma("sp", b[:], vec_dram.partition_broadcast(128), writes=[b.t])
    return b


def rstd_from_ss(C, ss, rs, n, dim):
    P = C.P
    P.op("dve", lambda e: e.tensor_scalar(out=ss[:, 0:n], in0=ss[:, 0:n], scalar1=1.0 / dim, scalar2=EPS,
                                          op0=ALU.mult, op1=ALU.add), [ss.t], [ss.t])
    P.op("act", lambda e: e.activation(out=ss[:, 0:n], in_=ss[:, 0:n], func=AF.Sqrt), [ss.t], [ss.t])
    P.op("dve", lambda e: e.reciprocal(out=rs[:, 0:n], in_=ss[:, 0:n]), [ss.t], [rs.t])


def norm_rows(C, xt, gain, hb, junk, ss, rs):
    P = C.P
    P.op("act", lambda e: e.activation(out=junk[:], in_=xt[:], func=AF.Square, accum_out=ss[:, 0:1]),
         [xt.t], [junk.t, ss.t])
    rstd_from_ss(C, ss, rs, 1, D)
    P.op("dve", lambda e: e.scalar_tensor_tensor(out=hb[:], in0=xt[:], scalar=rs[:, 0:1], in1=gain[:],
                                                 op0=ALU.mult, op1=ALU.mult), [xt.t, rs.t, gain.t], [hb.t])


def transpose_cols(C, src, src_tr, ncol, dst_fn, dst_tr, banks, ceng=("act", "dve")):
    P = C.P
    for b0 in range(0, ncol, 4):
        b1 = min(ncol, b0 + 4)
        ps = C.psum[banks[(b0 // 4) % len(banks)]]
        for c in range(b0, b1):
            P.op("pe", lambda e, c=c, ps=ps, b0=b0: e.matmul(out=ps[:, (c - b0) * 128:(c - b0 + 1) * 128],
                                                           lhsT=src[:, c * 128:(c + 1) * 128], rhs=C.ident[:],
                                                           start=True, stop=True),
                 [src_tr, C.ident.t], [ps.t])
        n = b1 - b0
        copy_op(C, C.eng_rr(ceng), dst_fn(b0, b1), ps[:, 0:n * 128].rearrange("p (c t) -> p c t", c=n),
                [ps.t] + list(dst_tr), list(dst_tr))


def setup_consts(C, ident_d, masks_d, tri_d):
    P = C.P
    C.ident = C.sb([128, 128], BF16)
    C.ones = C.sb([128, 128], BF16)
    C.masks = C.sb([128, 8, 512], BF16)
    C.trineg = C.sb([128, 128], BF16)
    C.onesneg = C.sb([128, 128], BF16)
    m = C.mark()
    idf = C.sb([128, 128], F32)
    trf = C.sb([128, 128], F32)
    mf = C.sb([128, 8, 512], F32)
    P.dma("sp", idf[:], ident_d, writes=[idf.t])
    P.dma("sp", trf[:], tri_d, writes=[trf.t])
    P.dma("sp", mf[:], masks_d.rearrange("m p q -> p m q"), writes=[mf.t])
    copy_op(C, "dve", C.ident[:], idf[:], [idf.t], [C.ident.t])
    P.op("dve", lambda e: e.tensor_scalar(out=C.trineg[:], in0=trf[:], scalar1=-1.0, scalar2=None, op0=ALU.mult),
         [trf.t], [C.trineg.t])
    copy_op(C, "pool", C.masks[:], mf[:], [mf.t], [C.masks.t])
    P.op("pool", lambda e: e.memset(C.ones[:], 1.0), [], [C.ones.t])
    P.op("pool", lambda e: e.memset(C.onesneg[:], -1.0), [], [C.onesneg.t])
    P.barrier()
    C.release(m)


def phase_ffn(C, x_d, gain_d, wg_d, wu_d, wd_d):
    P = C.P
    m = C.mark()
    stg = [C.sb([128, 1024], F32), C.sb([128, 1024], F32)]
    Wg = load_weight(C, wg_d, D, DFF, stg)
    Wu = load_weight(C, wu_d, D, DFF, stg)
    Wd = load_weight(C, wd_d, DFF, D, stg)
    gain = load_bcast(C, gain_d, D)
    xs = [C.sb([128, D], F32) for _ in range(4)]
    hb = C.sb([128, D], BF16)
    junk = hb
    ss = C.sb([128, 8], F32)
    rs = C.sb([128, 8], F32)
    hT = C.sb([128, 8, 512], BF16)
    actT = C.sb([128, NFF, 512], BF16, ntr=NFF)
    sg = [C.sb([128, 512], F32), C.sb([128, 512], F32)]
    for g in range(C.NG):
        for t in range(4):
            r0 = g * 512 + t * 128
            P.dma("sp", xs[t][:], x_d[r0:r0 + 128, :], writes=[xs[t].t])
            norm_rows(C, xs[t], gain, hb, junk, ss, rs)
            transpose_cols(C, hb, hb.t, 8, lambda c0, c1, t=t: hT[:, c0:c1, t * 128:(t + 1) * 128], [hT.t], [0, 1])
        for f in range(NFF):
            pg = C.psum[2 + (f % 2)]
            pu = C.psum[4 + (f % 2)]
            for c in range(8):
                P.op("pe", lambda e, c=c, f=f, pg=pg: e.matmul(out=pg[:], lhsT=Wg[:, c, f * 128:(f + 1) * 128],
                                                             rhs=hT[:, c, :], start=(c == 0), stop=(c == 7)),
                     [Wg.t, hT.t], [pg.t])
            for c in range(8):
                P.op("pe", lambda e, c=c, f=f, pu=pu: e.matmul(out=pu[:], lhsT=Wu[:, c, f * 128:(f + 1) * 128],
                                                             rhs=hT[:, c, :], start=(c == 0), stop=(c == 7)),
                     [Wu.t, hT.t], [pu.t])
            s = sg[f % 2]
            P.op("act", lambda e, s=s, pg=pg: e.activation(out=s[:], in_=pg[:], func=AF.Silu), [pg.t], [s.t])
            P.op("dve", lambda e, s=s, pu=pu, f=f: e.tensor_tensor(out=actT[:, f, :], in0=pu[:], in1=s[:], op=ALU.mult),
                 [pu.t, s.t], [actT.tr[f]])
        for t in range(4):
            for h in range(2):
                py = C.psum[6 + h]
                for f in range(NFF):
                    P.op("pe", lambda e, f=f, t=t, h=h, py=py: e.matmul(out=py[:], lhsT=actT[:, f, t * 128:(t + 1) * 128],
                                                                        rhs=Wd[:, f, h * 512:(h + 1) * 512],
                                                                        start=(f == 0), stop=(f == NFF - 1)),
                         [actT.tr[f], Wd.t], [py.t])
                P.op("dve", lambda e, t=t, h=h, py=py: e.scalar_tensor_tensor(
                    out=xs[t][:, h * 512:(h + 1) * 512], in0=py[:], scalar=0.5, in1=xs[t][:, h * 512:(h + 1) * 512],
                    op0=ALU.mult, op1=ALU.add), [py.t, xs[t].t], [xs[t].t])
            r0 = g * 512 + t * 128
            P.dma("sp", x_d[r0:r0 + 128, :], xs[t][:], reads=[xs[t].t])
    P.barrier()
    C.release(m)


def headnorm_rope(C, raw, nh, dh, gain_b, cs_ap, off, half, outb, w1, w2, ss, rs, cs_tr=None):
    P = C.P
    n = nh * dh
    r3 = raw[:, 0:n].rearrange("p (h d) -> p h d", h=nh)
    o3 = outb[:, 0:n].rearrange("p (h d) -> p h d", h=nh)
    a3 = w1[:, 0:n].rearrange("p (h d) -> p h d", h=nh)
    b3 = w2[:, 0:n].rearrange("p (h d) -> p h d", h=nh)
    P.op("act", lambda e: e.activation(out=w1[:, 0:n], in_=raw[:, 0:n], func=AF.Square), [raw.t], [w1.t])
    P.op("dve", lambda e: e.tensor_reduce(out=ss[:, 0:nh], in_=a3, axis=AX.X, op=ALU.add), [w1.t], [ss.t])
    rstd_from_ss(C, ss, rs, nh, dh)
    P.op("dve", lambda e: e.tensor_tensor(out=r3, in0=r3, in1=rs[:, 0:nh].unsqueeze(2).to_broadcast([128, nh, dh]),
                                          op=ALU.mult), [raw.t, rs.t], [raw.t])
    P.op("pool", lambda e: e.tensor_tensor(out=r3, in0=r3, in1=gain_b[:, 0:dh].unsqueeze(1).to_broadcast([128, nh, dh]),
                                           op=ALU.mult), [raw.t, gain_b.t], [raw.t])
    if cs_ap is None:
        copy_op(C, "act", outb[:, 0:n], raw[:, 0:n], [raw.t], [outb.t])
        return
    cos = cs_ap[:, 0:half].unsqueeze(1).to_broadcast([128, nh, half])
    sin = cs_ap[:, half:2 * half].unsqueeze(1).to_broadcast([128, nh, half])
    x1 = r3[:, :, off:off + half]
    x2 = r3[:, :, off + half:off + 2 * half]
    if off > 0:
        copy_op(C, "act", o3[:, :, 0:off], r3[:, :, 0:off], [raw.t, outb.t], [outb.t])
    P.op("pool", lambda e: e.tensor_tensor(out=a3[:, :, 0:half], in0=x1, in1=cos, op=ALU.mult), [raw.t, w1.t, cs_tr], [w1.t])
    P.op("dve", lambda e: e.tensor_tensor(out=b3[:, :, 0:half], in0=x2, in1=sin, op=ALU.mult), [raw.t, w2.t, cs_tr], [w2.t])
    P.op("pool", lambda e: e.tensor_tensor(out=a3[:, :, half:2 * half], in0=x1, in1=sin, op=ALU.mult), [raw.t, w1.t, cs_tr], [w1.t])
    P.op("dve", lambda e: e.tensor_tensor(out=b3[:, :, half:2 * half], in0=x2, in1=cos, op=ALU.mult), [raw.t, w2.t, cs_tr], [w2.t])
    P.op("dve", lambda e: e.tensor_tensor(out=o3[:, :, off:off + half], in0=a3[:, :, 0:half], in1=b3[:, :, 0:half],
                                          op=ALU.subtract), [w1.t, w2.t, outb.t], [outb.t])
    P.op("pool", lambda e: e.tensor_tensor(out=o3[:, :, off + half:off + 2 * half], in0=a3[:, :, half:2 * half],
                                           in1=b3[:, :, half:2 * half], op=ALU.add), [w1.t, w2.t, outb.t], [outb.t])


def store_T(C, outb, ncols, dstT, t, tbuf):
    import os
    nchunk = ncols // 128
    transpose_cols(C, outb, outb.t, nchunk, lambda c0, c1: tbuf[:, c0:c1, :], [tbuf.t], [0, 1])
    if "s" in os.environ.get("MSKIP", ""):
        return
    for c0 in range(0, nchunk, 4):
        C.P.dma("sp", dstT.rearrange("(c p) s -> p c s", p=128)[:, c0:c0 + 4, t * 128:(t + 1) * 128],
                tbuf[:, c0:c0 + 4, :], reads=[tbuf.t])


def proj(C, hT, kc, W, c0, n, bank, tok0=0, ntok=128):
    ps = C.psum[bank]
    for c in range(kc):
        C.P.op("pe", lambda e, c=c: e.matmul(out=ps[0:ntok, 0:n], lhsT=hT[:, c, tok0:tok0 + ntok], rhs=W[:, c, c0:c0 + n],
                                             start=(c == 0), stop=(c == kc - 1)), [hT.t, W.t], [ps.t])
    return ps


def scaled_gain(C, vec_d, n, scale):
    g = load_bcast(C, vec_d, n)
    if scale != 1.0:
        C.P.op("dve", lambda e: e.tensor_scalar(out=g[:], in0=g[:], scalar1=float(scale), scalar2=None, op0=ALU.mult),
               [g.t], [g.t])
    return g


def phase_ab_proj(C, x_d, gain_d, w_in_d, qn_d, kn_d, cs_d, scr):
    P = C.P
    S = C.S
    m = C.mark()
    stg = [C.sb([128, 1024], F32), C.sb([128, 1024], F32)]
    W = load_weight(C, w_in_d, D, 3072, stg)
    gain = load_bcast(C, gain_d, D)
    gq = scaled_gain(C, qn_d, 64, 0.125)
    gk = scaled_gain(C, kn_d, 64, 1.0)
    cs = C.sb([128, C.NT, 64], F32)
    P.dma("sp", cs[:], cs_d.rearrange("(t p) c -> p t c", p=128), writes=[cs.t])
    xt = [C.sb([128, D], F32) for _ in range(2)]
    hb = C.sb([128, D], BF16)
    ss = C.sb([128, 16], F32)
    rs = C.sb([128, 16], F32)
    hT = C.sb([128, 8, 128], BF16)
    raw = C.sb([128, 512], F32)
    w1 = C.sb([128, 512], F32)
    w2 = C.sb([128, 512], F32)
    outb = [C.sb([128, 512], BF16) for _ in range(2)]
    tbuf = [C.sb([128, 4, 128], BF16) for _ in range(2)]
    vb = [C.sb([128, 512], BF16) for _ in range(2)]
    qaT, kaT, va, qbT, kbT, vbd = scr["qaT"], scr["kaT"], scr["va"], scr["qbT"], scr["kbT"], scr["vb"]
    k = 0
    for t in range(C.NT):
        x = xt[t % 2]
        P.dma("sp", x[:], x_d[t * 128:(t + 1) * 128, :], writes=[x.t])
        norm_rows(C, x, gain, hb, hb, ss, rs)
        transpose_cols(C, hb, hb.t, 8, lambda c0, c1: hT[:, c0:c1, :], [hT.t], [0, 1])
        for grp in range(6):
            ps = proj(C, hT, 8, W, grp * 512, 512, 2 + (k % 6))
            k += 1
            if grp in (0, 1):
                copy_op(C, "act", raw[:], ps[:], [ps.t], [raw.t])
                ob = outb[grp]
                headnorm_rope(C, raw, 8, 64, gq if grp == 0 else gk, cs[:, t, :], 0, 32, ob, w1, w2, ss, rs, cs.t)
                store_T(C, ob, 512, qaT if grp == 0 else kaT, t, tbuf[grp])
            elif grp in (2, 5):
                v = vb[0 if grp == 2 else 1]
                copy_op(C, "act", v[:], ps[:], [ps.t, v.t], [v.t])
                P.dma("sp", (va if grp == 2 else vbd)[t * 128:(t + 1) * 128, :], v[:], reads=[v.t])
            else:
                ob = outb[grp - 3]
                if grp == 3:
                    P.op("act", lambda e, ob=ob, ps=ps: e.activation(out=ob[:], in_=ps[:], func=AF.Copy, scale=0.125),
                         [ps.t, ob.t], [ob.t])
                else:
                    copy_op(C, "dve", ob[:], ps[:], [ps.t, ob.t], [ob.t])
                store_T(C, ob, 512, qbT if grp == 3 else kbT, t, tbuf[grp - 3])
    P.barrier()
    C.release(m)


def load_head(C, qT_d, kT_d, dk, QT, KT):
    C.P.dma("sp", QT[0:dk, :], qT_d, reads=[QT.t], writes=[QT.t])
    C.P.dma("sp", KT[0:dk, :], kT_d, reads=[KT.t], writes=[KT.t])


def softmax_block(C, QT, KT, dk, Vl, qs, mask0, pbufs, bank_o, bank_d, k_ctr):
    P = C.P
    nj = 4 * qs + 4
    po = C.psum[bank_o]
    pd = C.psum[bank_d] if bank_d is not None else None
    for j in range(nj):
        ps = C.psum[k_ctr[0] % 2]
        pt = pbufs[k_ctr[0] % len(pbufs)]
        k_ctr[0] += 1
        P.op("pe", lambda e, j=j, ps=ps: e.matmul(out=ps[:], lhsT=KT[0:dk, j * 128:(j + 1) * 128],
                                                 rhs=QT[0:dk, qs * 512:(qs + 1) * 512], start=True, stop=True),
             [KT.t, QT.t], [ps.t])
        P.op("act", lambda e, ps=ps, pt=pt: e.activation(out=pt[:], in_=ps[:], func=AF.Exp), [ps.t], [pt.t])
        if j >= 4 * qs:
            mi = mask0 + j - 4 * qs
            P.op("pool", lambda e, pt=pt, mi=mi: e.tensor_tensor(out=pt[:], in0=pt[:], in1=C.masks[:, mi, :], op=ALU.mult),
                 [pt.t, C.masks.t], [pt.t])
        P.op("pe", lambda e, j=j, pt=pt: e.matmul(out=po[:], lhsT=Vl(j), rhs=pt[:], start=(j == 0), stop=(j == nj - 1)),
             [pt.t, Vl.tr], [po.t])
        if pd is not None:
            P.op("pe", lambda e, j=j, pt=pt: e.matmul(out=pd[:], lhsT=C.ones[:], rhs=pt[:], start=(j == 0),
                                                     stop=(j == nj - 1)), [pt.t, C.ones.t], [pd.t])


class VL:
    def __init__(self, fn, tr):
        self.fn = fn
        self.tr = tr

    def __call__(self, j):
        return self.fn(j)


def phase_diff_attn(C, scr, lam_d, subln_d, lambda_init):
    P = C.P
    S = C.S
    m = C.mark()
    QT = [C.sb([128, S], BF16) for _ in range(2)]
    KT = [C.sb([128, S], BF16) for _ in range(2)]
    V = [C.sb([128, C.NT, 128], BF16) for _ in range(2)]
    pb = [C.sb([128, 512], BF16) for _ in range(3)]
    A = C.sb([128, 512], F32)
    B = C.sb([128, 512], F32)
    R = C.sb([128, 512], F32)
    ob = [C.sb([128, 512], BF16) for _ in range(2)]
    Bh = C.sb([128, 512], BF16)
    Bl = C.sb([128, 512], BF16)
    lv = C.sb([128, 4, 64], F32)
    P.dma("sp", lv[:].rearrange("p a b -> p (a b)"), lam_d.partition_broadcast(128), writes=[lv.t])
    lt = C.sb([128, 2, 64], F32)
    ls = C.sb([128, 4], F32)
    P.op("dve", lambda e: e.tensor_tensor(out=lt[:], in0=lv[:, 0:4:2, :], in1=lv[:, 1:4:2, :], op=ALU.mult), [lv.t], [lt.t])
    P.op("dve", lambda e: e.tensor_reduce(out=ls[:, 0:2], in_=lt[:], axis=AX.X, op=ALU.add), [lt.t], [ls.t])
    P.op("act", lambda e: e.activation(out=ls[:, 0:2], in_=ls[:, 0:2], func=AF.Exp), [ls.t], [ls.t])
    P.op("dve", lambda e: e.tensor_tensor(out=ls[:, 2:3], in0=ls[:, 1:2], in1=ls[:, 0:1], op=ALU.subtract), [ls.t], [ls.t])
    P.op("dve", lambda e: e.tensor_scalar(out=ls[:, 3:4], in0=ls[:, 2:3], scalar1=-float(lambda_init), scalar2=None,
                                          op0=ALU.add), [ls.t], [ls.t])
    sub = C.sb([128, 1], F32)
    P.dma("sp", sub[:], subln_d.rearrange("(p o) -> p o", o=1), writes=[sub.t])
    P.op("dve", lambda e: e.tensor_scalar(out=sub[:], in0=sub[:], scalar1=float(1.0 - lambda_init), scalar2=None,
                                          op0=ALU.mult), [sub.t], [sub.t])
    kc = [0]
    for h in range(4):
        Vh = V[h % 2]
        for t0 in range(0, C.NT, 8):
            P.dma("sp", Vh[:, t0:min(t0 + 8, C.NT), :], scr["va"].rearrange("(t p) c -> p t c", p=128)[:, t0:min(t0 + 8, C.NT), h * 128:(h + 1) * 128],
                  reads=[Vh.t], writes=[Vh.t])
        for mp in range(2):
            hh = h + 4 * mp
            load_head(C, scr["qaT"][hh * 64:(hh + 1) * 64, :], scr["kaT"][hh * 64:(hh + 1) * 64, :], 64, QT[mp], KT[mp])
        for qs in range(C.NG):
            for mp in range(2):
                softmax_block(C, QT[mp], KT[mp], 64, VL(lambda j, Vh=Vh: Vh[:, j, :], Vh.t), qs, 0, pb, 2 + 2 * mp, 3 + 2 * mp, kc)
                po, pd = C.psum[2 + 2 * mp], C.psum[3 + 2 * mp]
                P.op("dve", lambda e, pd=pd: e.reciprocal(out=R[:], in_=pd[:]), [pd.t], [R.t])
                dst = A if mp == 0 else B
                P.op("dve", lambda e, po=po, dst=dst: e.tensor_tensor(out=dst[:], in0=po[:], in1=R[:], op=ALU.mult),
                     [po.t, R.t], [dst.t])
            P.op("dve", lambda e: e.scalar_tensor_tensor(out=A[:], in0=B[:], scalar=ls[:, 3:4], in1=A[:], op0=ALU.mult,
                                                         op1=ALU.add), [A.t, B.t, ls.t], [A.t])
            P.op("act", lambda e: e.activation(out=B[:], in_=A[:], func=AF.Square), [A.t], [B.t])
            pq = C.psum[6]
            P.op("dve", lambda e: e.tensor_copy(out=Bh[:], in_=B[:]), [B.t, Bh.t], [Bh.t])
            P.op("dve", lambda e: e.tensor_tensor(out=Bl[:], in0=B[:], in1=Bh[:], op=ALU.subtract), [B.t, Bh.t, Bl.t], [Bl.t])
            P.op("pe", lambda e, pq=pq: e.matmul(out=pq[:], lhsT=C.ones[:], rhs=Bh[:], start=True, stop=False),
                 [C.ones.t, Bh.t], [pq.t])
            P.op("pe", lambda e, pq=pq: e.matmul(out=pq[:], lhsT=C.ones[:], rhs=Bl[:], start=False, stop=True),
                 [C.ones.t, Bl.t], [pq.t])
            P.op("dve", lambda e, pq=pq: e.tensor_scalar(out=R[:], in0=pq[:], scalar1=1.0 / 128, scalar2=EPS, op0=ALU.mult,
                                                         op1=ALU.add), [pq.t], [R.t])
            P.op("act", lambda e: e.activation(out=R[:], in_=R[:], func=AF.Sqrt), [R.t], [R.t])
            P.op("dve", lambda e: e.reciprocal(out=R[:], in_=R[:]), [R.t], [R.t])
            o = ob[qs % 2]
            P.op("dve", lambda e, o=o: e.scalar_tensor_tensor(out=o[:], in0=A[:], scalar=sub[:, 0:1], in1=R[:], op0=ALU.mult,
                                                              op1=ALU.mult), [A.t, R.t, sub.t, o.t], [o.t])
            P.dma("sp", scr["mixT"][h * 128:(h + 1) * 128, qs * 512:(qs + 1) * 512], o[:], reads=[o.t])
    P.barrier()
    C.release(m)


def phase_sb_attn(C, scr):
    P = C.P
    S = C.S
    m = C.mark()
    QT = [C.sb([128, S], BF16) for _ in range(2)]
    KT = [C.sb([128, S], BF16) for _ in range(2)]
    V = [C.sb([128, C.NT, 128], BF16) for _ in range(2)]
    for b_ in QT + KT:
        P.op("pool", lambda e, b_=b_: e.memset(b_[64:128, :], 0.0), [], [b_.t])
    for v in V:
        P.op("pool", lambda e, v=v: e.memset(v[:, :, 64:128], 0.0), [], [v.t])
    ef = [C.sb([128, 512], F32) for _ in range(2)]
    sp = [C.sb([128, 512], BF16) for _ in range(2)]
    wb = [C.sb([128, 512], BF16) for _ in range(2)]
    Ls = C.sb([128, 512], F32)
    Lb = [C.sb([128, 512], BF16) for _ in range(2)]
    ob = [C.sb([128, 512], BF16) for _ in range(2)]
    k = 0
    for h in range(8):
        Vh, Q, K = V[h % 2], QT[h % 2], KT[h % 2]
        for t0 in range(0, C.NT, 8):
            P.dma("sp", Vh[:, t0:min(t0 + 8, C.NT), 0:64], scr["vb"].rearrange("(t p) c -> p t c", p=128)[:, t0:min(t0 + 8, C.NT), h * 64:(h + 1) * 64],
                  reads=[Vh.t], writes=[Vh.t])
        load_head(C, scr["qbT"][h * 64:(h + 1) * 64, :], scr["kbT"][h * 64:(h + 1) * 64, :], 64, Q, K)
        for qs in range(C.NG):
            nj = 4 * qs + 4
            po = C.psum[4 + (qs % 2)]
            first = True
            for j in range(nj - 1, -1, -1):
                pz = C.psum[k % 2]
                pc = C.psum[2 + (k % 2)]
                e_, s_, w_, lb = ef[k % 2], sp[k % 2], wb[k % 2], Lb[k % 2]
                k += 1
                diag = j >= 4 * qs
                mi = 4 + j - 4 * qs
                for pp in (pz, pc):
                    P.op("pe", lambda e, j=j, pp=pp, last=(pp is pz), K=K, Q=Q, qs=qs: e.matmul(
                        out=pp[:], lhsT=K[:, j * 128:(j + 1) * 128], rhs=Q[:, qs * 512:(qs + 1) * 512],
                        start=True, stop=last), [K.t, Q.t], [pp.t])
                P.op("act", lambda e, pz=pz, e_=e_: e.activation(out=e_[:], in_=pz[:], func=AF.Exp), [pz.t], [e_.t])
                P.op("act", lambda e, e_=e_, s_=s_: e.activation(out=s_[:], in_=e_[:], func=AF.Ln, bias=1.0, scale=1.0),
                     [e_.t], [s_.t])
                if diag:
                    P.op("pool", lambda e, s_=s_, mi=mi: e.tensor_tensor(out=s_[:], in0=s_[:], in1=C.masks[:, mi, :],
                                                                        op=ALU.mult), [s_.t, C.masks.t], [s_.t])
                P.op("pe", lambda e, pc=pc, s_=s_, first=first: e.matmul(out=pc[:], lhsT=C.trineg[:], rhs=s_[:], start=False,
                                                                        stop=first), [C.trineg.t, s_.t], [pc.t])
                if not first:
                    P.op("pe", lambda e, pc=pc, lb=lb: e.matmul(out=pc[:], lhsT=C.onesneg[:], rhs=lb[:], start=False, stop=True),
                         [C.onesneg.t, lb.t], [pc.t])
                P.op("act", lambda e, pc=pc, w_=w_: e.activation(out=w_[:], in_=pc[:], func=AF.Exp), [pc.t], [w_.t])
                if diag:
                    P.op("pool", lambda e, w_=w_, mi=mi: e.tensor_tensor(out=w_[:], in0=w_[:], in1=C.masks[:, mi, :],
                                                                        op=ALU.mult), [w_.t, C.masks.t], [w_.t])
                P.op("pe", lambda e, j=j, w_=w_, first=first, po=po, Vh=Vh: e.matmul(out=po[:], lhsT=Vh[:, j, :], rhs=w_[:],
                                                                      start=first, stop=(j == 0)), [Vh.t, w_.t], [po.t])
                if j > 0:
                    nlb = Lb[k % 2]
                    if first:
                        copy_op(C, "dve", Ls[:], s_[:], [s_.t, Ls.t], [Ls.t])
                    else:
                        P.op("dve", lambda e, s_=s_: e.tensor_tensor(out=Ls[:], in0=Ls[:], in1=s_[:], op=ALU.add),
                             [Ls.t, s_.t], [Ls.t])
                    copy_op(C, "pool", nlb[:], Ls[:], [Ls.t, nlb.t], [nlb.t])
                first = False
            o = ob[qs % 2]
            copy_op(C, "dve", o[0:64, :], po[0:64, :], [po.t, o.t], [o.t])
            P.dma("sp", scr["mixT"][512 + h * 64:512 + (h + 1) * 64, qs * 512:(qs + 1) * 512], o[0:64, :], reads=[o.t])
    P.barrier()
    C.release(m)


def phase_outproj(C, x_d, mixT_d, w_d):
    P = C.P
    m = C.mark()
    stg = [C.sb([128, 1024], F32), C.sb([128, 1024], F32)]
    W = load_weight(C, w_d, D, D, stg)
    mT = [C.sb([128, 8, 512], BF16) for _ in range(2)]
    xs = [C.sb([128, D], F32) for _ in range(3)]
    k = 0
    for g in range(C.NG):
        mt = mT[g % 2]
        P.dma("sp", mt[:], mixT_d.rearrange("(c p) s -> p c s", p=128)[:, :, g * 512:(g + 1) * 512], writes=[mt.t])
        for t in range(4):
            r0 = g * 512 + t * 128
            x = xs[k % 3]
            k += 1
            P.dma("sp", x[:], x_d[r0:r0 + 128, :], writes=[x.t])
            for h in range(2):
                py = C.psum[2 * (k % 2) + h]
                for c in range(8):
                    P.op("pe", lambda e, c=c, t=t, h=h, py=py, mt=mt: e.matmul(
                        out=py[:], lhsT=mt[:, c, t * 128:(t + 1) * 128], rhs=W[:, c, h * 512:(h + 1) * 512],
                        start=(c == 0), stop=(c == 7)), [mt.t, W.t], [py.t])
                P.op("dve", lambda e, h=h, py=py, x=x: e.tensor_tensor(out=x[:, h * 512:(h + 1) * 512], in0=py[:],
                                                                     in1=x[:, h * 512:(h + 1) * 512], op=ALU.add),
                     [py.t, x.t], [x.t])
            P.dma("sp", x_d[r0:r0 + 128, :], x[:], reads=[x.t])
    P.barrier()
    C.release(m)


def phase_xm(C, x_d, mem_d, xn_d, mn_d, wq_d, wkv_d, qn_d, kn_d, wo_d):
    P = C.P
    m = C.mark()
    stg = [C.sb([128, 1024], F32), C.sb([128, 1024], F32)]
    Wkv = load_weight(C, wkv_d, D, D, stg)
    gm = load_bcast(C, mn_d, D)
    gk = scaled_gain(C, kn_d, 128, 1.0)
    gq = scaled_gain(C, qn_d, 128, 128 ** -0.5)
    hb = C.sb([128, D], BF16)
    ss = C.sb([128, 8], F32)
    rs = C.sb([128, 8], F32)
    hT = C.sb([128, 8, 128], BF16)
    raw = C.sb([128, 512], F32)
    w1 = C.sb([128, 512], F32)
    outb = C.sb([128, 512], BF16)
    KT = C.sb([128, 4, MEM], BF16)
    Vm = C.sb([128, 2, 512], BF16)
    xs = [C.sb([128, D], F32) for _ in range(4)]
    for t in range(2):
        x = xs[t]
        P.dma("sp", x[:], mem_d[t * 128:(t + 1) * 128, :], writes=[x.t])
        norm_rows(C, x, gm, hb, hb, ss, rs)
        transpose_cols(C, hb, hb.t, 8, lambda c0, c1: hT[:, c0:c1, :], [hT.t], [0, 1])
        ps = proj(C, hT, 8, Wkv, 0, 512, 2)
        copy_op(C, "act", raw[:], ps[:], [ps.t], [raw.t])
        headnorm_rope(C, raw, 4, 128, gk, None, 0, 0, outb, w1, w1, ss, rs)
        transpose_cols(C, outb, outb.t, 4, lambda c0, c1, t=t: KT[:, c0:c1, t * 128:(t + 1) * 128], [KT.t], [0, 1])
        ps = proj(C, hT, 8, Wkv, 512, 512, 3)
        copy_op(C, "act", Vm[:, t, :], ps[:], [ps.t, Vm.t], [Vm.t])
    Wq = load_weight(C, wq_d, D, 512, stg)
    Wo = load_weight(C, wo_d, 512, D, stg)
    gx = load_bcast(C, xn_d, D)
    hT4 = C.sb([128, 8, 512], BF16)
    qT = C.sb([128, 4, 512], BF16)
    xoT = C.sb([128, 4, 512], BF16, ntr=4)
    pb = [C.sb([128, 512], BF16) for _ in range(3)]
    R = C.sb([128, 512], F32)
    kk = 0
    for g in range(C.NG):
        for t in range(4):
            r0 = g * 512 + t * 128
            x = xs[t]
            P.dma("sp", x[:], x_d[r0:r0 + 128, :], writes=[x.t])
            norm_rows(C, x, gx, hb, hb, ss, rs)
            transpose_cols(C, hb, hb.t, 8, lambda c0, c1, t=t: hT4[:, c0:c1, t * 128:(t + 1) * 128], [hT4.t], [0, 1])
            ps = proj(C, hT4, 8, Wq, 0, 512, 2 + (t % 2), tok0=t * 128)
            copy_op(C, "act", raw[:], ps[:], [ps.t], [raw.t])
            headnorm_rope(C, raw, 4, 128, gq, None, 0, 0, outb, w1, w1, ss, rs)
            transpose_cols(C, outb, outb.t, 4, lambda c0, c1, t=t: qT[:, c0:c1, t * 128:(t + 1) * 128], [qT.t], [0, 1])
        for h in range(4):
            po, pd = C.psum[4], C.psum[5]
            for mt in range(2):
                ps = C.psum[2 + (kk % 2)]
                pt = pb[kk % 3]
                kk += 1
                P.op("pe", lambda e, h=h, mt=mt, ps=ps: e.matmul(out=ps[:], lhsT=KT[:, h, mt * 128:(mt + 1) * 128],
                                                               rhs=qT[:, h, :], start=True, stop=True), [KT.t, qT.t], [ps.t])
                P.op("act", lambda e, ps=ps, pt=pt: e.activation(out=pt[:], in_=ps[:], func=AF.Exp), [ps.t], [pt.t])
                P.op("pe", lambda e, h=h, mt=mt, pt=pt: e.matmul(out=po[:], lhsT=Vm[:, mt, h * 128:(h + 1) * 128], rhs=pt[:],
                                                               start=(mt == 0), stop=(mt == 1)), [Vm.t, pt.t], [po.t])
                P.op("pe", lambda e, mt=mt, pt=pt: e.matmul(out=pd[:], lhsT=C.ones[:], rhs=pt[:], start=(mt == 0),
                                                          stop=(mt == 1)), [C.ones.t, pt.t], [pd.t])
            P.op("dve", lambda e, pd=pd: e.reciprocal(out=R[:], in_=pd[:]), [pd.t], [R.t])
            P.op("dve", lambda e, po=po, h=h: e.tensor_tensor(out=xoT[:, h, :], in0=po[:], in1=R[:], op=ALU.mult),
                 [po.t, R.t], [xoT.tr[h]])
        for t in range(4):
            r0 = g * 512 + t * 128
            x = xs[t]
            for hf in range(2):
                py = C.psum[6 + hf]
                for h in range(4):
                    P.op("pe", lambda e, h=h, t=t, hf=hf, py=py: e.matmul(
                        out=py[:], lhsT=xoT[:, h, t * 128:(t + 1) * 128], rhs=Wo[:, h, hf * 512:(hf + 1) * 512],
                        start=(h == 0), stop=(h == 3)), [xoT.tr[h], Wo.t], [py.t])
                P.op("dve", lambda e, hf=hf, py=py, x=x: e.tensor_tensor(out=x[:, hf * 512:(hf + 1) * 512], in0=py[:],
                                                                       in1=x[:, hf * 512:(hf + 1) * 512], op=ALU.add),
                     [py.t, x.t], [x.t])
            P.dma("sp", x_d[r0:r0 + 128, :], x[:], reads=[x.t])
    P.barrier()
    C.release(m)


def phase_mla_proj(C, x_d, gain_d, wdq_d, qln_d, wuq_d, wdkv_d, kvln_d, wukv_d, nq_d, nk_d, cs_d, scr):
    P = C.P
    m = C.mark()
    stg = [C.sb([128, 1024], F32), C.sb([128, 1024], F32)]
    Wdq = load_weight(C, wdq_d, D, 512, stg)
    Wuq = load_weight(C, wuq_d, 512, 1536, stg)
    Wdkv = load_weight(C, wdkv_d, D, 288, stg)
    Wukv = load_weight(C, wukv_d, 256, 2048, stg)
    gain = load_bcast(C, gain_d, D)
    gql = load_bcast(C, qln_d, 512)
    gkvl = load_bcast(C, kvln_d, 256)
    gq = scaled_gain(C, nq_d, 96, 96 ** -0.5)
    gk = scaled_gain(C, nk_d, 96, 1.0)
    import os
    MS = os.environ.get("MSKIP", "")
    cs = C.sb([128, C.NT, 32], F32)
    if "c" not in MS:
        P.dma("sp", cs[:], cs_d.rearrange("(t p) c -> p t c", p=128), writes=[cs.t])
    xt = [C.sb([128, D], F32) for _ in range(2)]
    hb = C.sb([128, D], BF16)
    ss = C.sb([128, 16], F32)
    rs = C.sb([128, 16], F32)
    hT = C.sb([128, 8, 128], BF16)
    cq = C.sb([128, 512], F32)
    cqb = C.sb([128, 512], BF16)
    cqT = C.sb([128, 4, 128], BF16)
    dkv = C.sb([128, 288], F32)
    ckb = C.sb([128, 256], BF16)
    ckT = C.sb([128, 2, 128], BF16)
    raw = C.sb([128, 1536], F32)
    w1 = C.sb([128, 1536], F32)
    w2 = C.sb([128, 1536], F32)
    outb = [C.sb([128, 1536], BF16) for _ in range(2)]
    tbuf = [C.sb([128, 12, 128], BF16) for _ in range(2)]
    vb = C.sb([128, 16, 64], BF16)
    kvs = [C.sb([128, 512], F32) for _ in range(2)]
    qT_d, kT_d, v_d = scr["mqT"], scr["mkT"], scr["mv"]
    for t in range(C.NT):
        x = xt[t % 2]
        P.dma("sp", x[:], x_d[t * 128:(t + 1) * 128, :], writes=[x.t])
        norm_rows(C, x, gain, hb, hb, ss, rs)
        transpose_cols(C, hb, hb.t, 8, lambda c0, c1: hT[:, c0:c1, :], [hT.t], [0, 1])
        ps = proj(C, hT, 8, Wdq, 0, 512, 2)
        copy_op(C, "act", cq[:], ps[:], [ps.t], [cq.t])
        P.op("act", lambda e: e.activation(out=w1[:, 0:512], in_=cq[:], func=AF.Square, accum_out=ss[:, 0:1]),
             [cq.t], [w1.t, ss.t])
        rstd_from_ss(C, ss, rs, 1, 512)
        P.op("dve", lambda e: e.scalar_tensor_tensor(out=cqb[:], in0=cq[:], scalar=rs[:, 0:1], in1=gql[:], op0=ALU.mult,
                                                     op1=ALU.mult), [cq.t, rs.t, gql.t], [cqb.t])
        transpose_cols(C, cqb, cqb.t, 4, lambda c0, c1: cqT[:, c0:c1, :], [cqT.t], [0, 1])
        ps = proj(C, hT, 8, Wdkv, 0, 288, 3)
        copy_op(C, "act", dkv[:], ps[:, 0:288], [ps.t], [dkv.t])
        P.op("act", lambda e: e.activation(out=w1[:, 0:256], in_=dkv[:, 0:256], func=AF.Square, accum_out=ss[:, 0:1]),
             [dkv.t, ss.t], [w1.t, ss.t])
        rstd_from_ss(C, ss, rs, 1, 256)
        P.op("dve", lambda e: e.scalar_tensor_tensor(out=ckb[:], in0=dkv[:, 0:256], scalar=rs[:, 0:1], in1=gkvl[:],
                                                     op0=ALU.mult, op1=ALU.mult), [dkv.t, rs.t, gkvl.t], [ckb.t])
        transpose_cols(C, ckb, ckb.t, 2, lambda c0, c1: ckT[:, c0:c1, :], [ckT.t], [0, 1])
        r3 = raw[:].rearrange("p (h d) -> p h d", h=16)
        if "Q" in MS:
            continue
        for g4 in range(4):
            ps = proj(C, cqT, 4, Wuq, g4 * 384, 384, 4 + g4)
            copy_op(C, "act" if g4 % 2 else "dve", raw[:, g4 * 384:(g4 + 1) * 384], ps[:, 0:384], [ps.t, raw.t], [raw.t])
        import os
        MS = os.environ.get("MSKIP", "")
        headnorm_rope(C, raw, 16, 96, gq, None if "r" in MS else cs[:, t, :], 64, 16, outb[0], w1, w2, ss, rs, cs.t)
        store_T(C, outb[0], 1536, qT_d, t, tbuf[0])
        if "K" in MS:
            continue
        for g4 in range(4):
            ps = proj(C, ckT, 2, Wukv, g4 * 512, 512, 4 + g4)
            kv = kvs[g4 % 2]
            copy_op(C, "act", kv[:], ps[:], [ps.t], [kv.t])
            p3 = kv[:].rearrange("p (h d) -> p h d", h=4)
            copy_op(C, "pool", r3[:, g4 * 4:(g4 + 1) * 4, 0:64], p3[:, :, 0:64], [kv.t, raw.t], [raw.t])
            copy_op(C, "dve", vb[:, g4 * 4:(g4 + 1) * 4, :], p3[:, :, 64:128], [kv.t, vb.t], [vb.t])
        copy_op(C, "dve" if "b" in MS else "pool", r3[:, :, 64:96], dkv[:, 256:288].unsqueeze(1).to_broadcast([128, 16, 32]), [dkv.t, raw.t], [raw.t])
        if "v" not in MS:
            P.dma("sp", v_d[t * 128:(t + 1) * 128, :], vb[:].rearrange("p h d -> p (h d)"), reads=[vb.t])
        headnorm_rope(C, raw, 16, 96, gk, None if "r" in MS else cs[:, t, :], 64, 16, outb[1], w1, w2, ss, rs, cs.t)
        store_T(C, outb[1], 1536, kT_d, t, tbuf[1])
    P.barrier()
    C.release(m)


def phase_mla_attn(C, scr):
    P = C.P
    S = C.S
    m = C.mark()
    QT = [C.sb([128, S], BF16) for _ in range(2)]
    KT = [C.sb([128, S], BF16) for _ in range(2)]
    V = [C.sb([128, C.NT, 128], BF16) for _ in range(2)]
    for v in V:
        P.op("pool", lambda e, v=v: e.memset(v[:, :, 64:128], 1.0), [], [v.t])
    pb = [C.sb([128, 512], BF16) for _ in range(3)]
    R = C.sb([128, 512], F32)
    ob = [C.sb([128, 512], BF16) for _ in range(2)]
    kc = [0]
    k = 0
    for h in range(16):
        Vh, Q, K = V[h % 2], QT[h % 2], KT[h % 2]
        for t0 in range(0, C.NT, 8):
            P.dma("sp", Vh[:, t0:min(t0 + 8, C.NT), 0:64], scr["mv"].rearrange("(t p) c -> p t c", p=128)[:, t0:min(t0 + 8, C.NT), h * 64:(h + 1) * 64],
                  reads=[Vh.t], writes=[Vh.t])
        load_head(C, scr["mqT"][h * 96:(h + 1) * 96, :], scr["mkT"][h * 96:(h + 1) * 96, :], 96, Q, K)
        for qs in range(C.NG):
            bo = 2 + (k % 2)
            k += 1
            softmax_block(C, Q, K, 96, VL(lambda j, Vh=Vh: Vh[:, j, :], Vh.t), qs, 0, pb, bo, None, kc)
            po = C.psum[bo]
            P.op("dve", lambda e, po=po: e.reciprocal(out=R[0:64, :], in_=po[64:128, :]), [po.t], [R.t])
            o = ob[k % 2]
            P.op("dve", lambda e, po=po, o=o: e.tensor_tensor(out=o[0:64, :], in0=po[0:64, :], in1=R[0:64, :], op=ALU.mult),
                 [po.t, R.t, o.t], [o.t])
            P.dma("sp", scr["mixT"][h * 64:(h + 1) * 64, qs * 512:(qs + 1) * 512], o[0:64, :], reads=[o.t])
    P.barrier()
    C.release(m)


WNAMES = ["ffn_norm", "ffn_w_gate", "ffn_w_up", "ffn_w_down", "mix_norm", "ab_w_in", "ab_w_out", "diff_q_norm",
          "diff_k_norm", "diff_subln", "mla_w_dq", "mla_q_norm", "mla_w_uq", "mla_w_dkv", "mla_kv_norm", "mla_w_ukv",
          "mla_qk_norm_q", "mla_qk_norm_k", "mla_w_o", "xm_norm", "xm_mem_norm", "xm_w_q", "xm_w_kv", "xm_q_norm",
          "xm_k_norm", "xm_w_o"]


def build(S, shapes, phases=None):
    nc = bass.Bass("TRN2", target_bir_lowering=False)
    ins = {}
    for name, shp in shapes.items():
        ins[name] = nc.dram_tensor(name, list(shp), F32, kind="ExternalInput").ap()
    out = nc.dram_tensor("out", [S, D], F32, kind="ExternalOutput").ap()
    scr_t = nc.dram_tensor("scr", [5120 * S], BF16).ap()

    def sv(off, rows, cols):
        return scr_t[off * S:(off + rows * cols // S) * S].rearrange("(r c) -> r c", c=cols)
    scr = {"qaT": sv(0, 512, S), "kaT": sv(512, 512, S), "va": sv(1024, S, 512), "qbT": sv(1536, 512, S),
           "kbT": sv(2048, 512, S), "vb": sv(2560, S, 512),
           "mqT": sv(0, 1536, S), "mkT": sv(1536, 1536, S), "mv": sv(3072, S, 1024), "mixT": sv(4096, 1024, S)}
    C = Ctx(nc, S)
    P = C.P
    setup_consts(C, ins["c_ident"], ins["c_masks"], ins["c_tri"])
    xin = Tr()
    P.dma("sp", out, ins["x"], writes=[xin])
    P.barrier()
    w = ins
    allp = ["ffn00", "abproj", "diff", "sb", "out0", "xm0", "ffn01", "ffn10", "mlaproj", "mlaattn", "out1", "xm1", "ffn11"]
    for ph in (allp if phases is None else phases):
        if ph.startswith("ffn"):
            l, i = int(ph[3]), int(ph[4])
            phase_ffn(C, out, w["ffn_norm"][l, i], w["ffn_w_gate"][l, i], w["ffn_w_up"][l, i], w["ffn_w_down"][l, i])
        elif ph == "abproj":
            phase_ab_proj(C, out, w["mix_norm"][0], w["ab_w_in"][0], w["diff_q_norm"][0], w["diff_k_norm"][0], w["c_cs64"], scr)
        elif ph == "diff":
            phase_diff_attn(C, scr, w["c_lam"], w["diff_subln"][0], 0.8 - 0.6 * math.exp(0.0))
        elif ph == "sb":
            phase_sb_attn(C, scr)
        elif ph == "out0":
            phase_outproj(C, out, scr["mixT"], w["ab_w_out"][0])
        elif ph == "out1":
            phase_outproj(C, out, scr["mixT"], w["mla_w_o"][0])
        elif ph.startswith("xm"):
            l = int(ph[2])
            phase_xm(C, out, w["mem"], w["xm_norm"][l], w["xm_mem_norm"][l], w["xm_w_q"][l], w["xm_w_kv"][l],
                     w["xm_q_norm"][l], w["xm_k_norm"][l], w["xm_w_o"][l])
        elif ph == "mlaproj":
            phase_mla_proj(C, out, w["mix_norm"][1], w["mla_w_dq"][0], w["mla_q_norm"][0], w["mla_w_uq"][0], w["mla_w_dkv"][0],
                           w["mla_kv_norm"][0], w["mla_w_ukv"][0], w["mla_qk_norm_q"][0], w["mla_qk_norm_k"][0], w["c_cs32"], scr)
        elif ph == "mlaattn":
            phase_mla_attn(C, scr)
    P.emit(final_wait_ops=list(C.P.dma_hist[-N_DMA_SEMS:]))
    return nc


def make_consts(S):
    c = {}
    c["c_ident"] = np.eye(128, dtype=np.float32)
    kk = np.arange(128)[:, None]
    qq = np.arange(512)[None, :]
    masks = np.zeros((8, 128, 512), np.float32)
    for j in range(4):
        k = 128 * j + kk
        masks[j] = ((k // 64) <= (qq // 64)).astype(np.float32)
        masks[4 + j] = (k < qq).astype(np.float32)
    c["c_masks"] = masks
    c["c_tri"] = (np.arange(128)[:, None] >= np.arange(128)[None, :]).astype(np.float32)
    pos = np.arange(S, dtype=np.float32)[:, None]
    for d, nm in ((64, "c_cs64"), (32, "c_cs32")):
        inv = (1.0 / (np.float32(10000.0) ** (np.arange(0, d, 2, dtype=np.float32) / np.float32(d)))).astype(np.float32)
        ang = (pos * inv[None, :]).astype(np.float32)
        c[nm] = np.concatenate([np.cos(ang), np.sin(ang)], axis=1).astype(np.float32)
    return c


def prep_inputs(inputs, S):
    shared = {k: np.ascontiguousarray(np.asarray(inputs[k], dtype=np.float32)) for k in WNAMES}
    shared["c_lam"] = np.ascontiguousarray(np.concatenate([np.asarray(inputs[k], np.float32).reshape(-1) for k in
                                           ("diff_lambda_q1", "diff_lambda_k1", "diff_lambda_q2", "diff_lambda_k2")]))
    shared.update(make_consts(S))
    x = np.asarray(inputs["x"], np.float32)
    mem = np.asarray(inputs["mem"], np.float32)
    maps = []
    for b in range(x.shape[0]):
        mmap = dict(shared)
        mmap["x"] = np.ascontiguousarray(x[b])
        mmap["mem"] = np.ascontiguousarray(mem[b])
        maps.append(mmap)
    return maps


def kernel(**inputs):
    S = inputs["x"].shape[1]
    maps = prep_inputs(inputs, S)
    shapes = {k: v.shape for k, v in maps[0].items()}
    nc = build(S, shapes)
    res = run_bass_kernel_spmd(nc, maps, core_ids=list(range(len(maps))))
    return np.stack([np.asarray(r["out"], dtype=np.float32) for r in res.results], axis=0)
```

```python
import math
import numpy as np
import concourse.bass as bass
import concourse.mybir as mybir
from concourse.bass_utils import run_bass_kernel_spmd

F32 = mybir.dt.float32
BF16 = mybir.dt.bfloat16
AF = mybir.ActivationFunctionType
ALU = mybir.AluOpType
AX = mybir.AxisListType
N_DMA_SEMS = 48
EPS = 1e-6
D = 1024
DFF = 2816
NFF = 22
MEM = 256


class Tr:
    __slots__ = ("lw", "rd")

    def __init__(self):
        self.lw = None
        self.rd = []


class Op:
    __slots__ = ("eng", "fn", "deps", "needed", "cnt", "is_dma", "dslot", "dval")

    def __init__(self, eng, fn, is_dma=False):
        self.eng = eng
        self.fn = fn
        self.deps = set()
        self.needed = False
        self.cnt = 0
        self.is_dma = is_dma
        self.dslot = -1
        self.dval = 0


class Prog:
    ENGS = ("pe", "act", "dve", "pool", "sp")

    def __init__(self, nc):
        self.nc = nc
        self.streams = {e: [] for e in self.ENGS}
        self.n_dma = 0
        self.dma_hist = []
        self.since_barrier = []
        self.order = []

    def _add_deps(self, op, reads, writes):
        for t in reads:
            if t.lw is not None:
                op.deps.add(t.lw)
        for t in writes:
            if t.lw is not None:
                op.deps.add(t.lw)
            for r in t.rd:
                op.deps.add(r)
        for t in reads:
            if not op.is_dma:
                t.rd = [r for r in t.rd if r.is_dma or r.eng != op.eng]
            t.rd.append(op)
        for t in writes:
            t.lw = op
            t.rd = []
        op.deps.discard(op)

    def op(self, eng, fn, reads=(), writes=()):
        o = Op(eng, fn)
        self._add_deps(o, reads, writes)
        self.streams[eng].append(o)
        self.order.append(o)
        return o

    def dma(self, queue, out_ap, in_ap, reads=(), writes=()):
        def fn(e, out_ap=out_ap, in_ap=in_ap):
            return e.dma_start(out=out_ap, in_=in_ap)
        o = Op(queue, fn, is_dma=True)
        i = self.n_dma
        self.n_dma += 1
        o.dslot = i % N_DMA_SEMS
        o.dval = 16 * (i // N_DMA_SEMS + 1)
        if i >= N_DMA_SEMS:
            o.deps.add(self.dma_hist[i - N_DMA_SEMS])
        self.dma_hist.append(o)
        self.since_barrier.append(o)
        self._add_deps(o, reads, writes)
        self.streams[queue].append(o)
        self.order.append(o)
        return o

    def barrier(self):
        lasts = []
        for e in self.ENGS:
            for o in reversed(self.streams[e]):
                if not o.is_dma:
                    lasts.append(o)
                    break
        dmas = list(self.since_barrier)
        self.since_barrier = []
        for e in self.ENGS:
            o = Op(e, lambda eng: eng.nop())
            o.deps.update(lasts)
            o.deps.update(dmas)
            self.streams[e].append(o)
            self.order.append(o)

    def emit(self, final_wait_ops=()):
        nc = self.nc
        for e in self.ENGS:
            for o in self.streams[e]:
                for d in o.deps:
                    if not d.is_dma:
                        if d.eng == "pe" and o.eng == "pe" and not o.is_dma:
                            continue
                        d.needed = True
        for e in self.ENGS:
            c = 0
            for o in self.streams[e]:
                if (not o.is_dma) and o.needed:
                    c += 1
                    o.cnt = c
        esem = {e: nc.semaphore("s_" + e).__enter__() for e in self.ENGS}
        dsem = [nc.semaphore("d%d" % i).__enter__() for i in range(N_DMA_SEMS)]
        engobj = {"pe": nc.tensor, "act": nc.scalar, "dve": nc.vector, "pool": nc.gpsimd, "sp": nc.sync}
        seen = {e: {} for e in self.ENGS}

        def do_waits(ename, deps):
            eng = engobj[ename]
            waits = {}
            for d in deps:
                if d.is_dma:
                    key = ("d", d.dslot)
                    val = d.dval
                else:
                    if d.eng == "pe" and ename == "pe":
                        continue
                    key = ("e", d.eng)
                    val = d.cnt
                if waits.get(key, 0) < val:
                    waits[key] = val
            sn = seen[ename]
            for key, val in waits.items():
                if sn.get(key, 0) >= val:
                    continue
                sn[key] = val
                sm = dsem[key[1]] if key[0] == "d" else esem[key[1]]
                eng.wait_ge(sm, val)
        for o in self.order:
            do_waits(o.eng, o.deps)
            ins = o.fn(engobj[o.eng])
            if o.is_dma:
                ins.then_inc(dsem[o.dslot], 16)
            elif o.needed:
                ins.then_inc(esem[o.eng], 1)
        do_waits("sp", final_wait_ops)
        import os
        if os.environ.get("KDEBUG"):
            print("SEMCOUNTS", {e: max([o.cnt for o in self.streams[e]] + [0]) for e in self.ENGS}, "ndma", self.n_dma, "nops", len(self.order))


class Buf:
    def __init__(self, handle, ntr=1):
        self.h = handle
        self.tr = [Tr() for _ in range(ntr)]

    def __getitem__(self, k):
        return self.h[k]

    @property
    def t(self):
        return self.tr[0]


SB_BASE = 16640
SB_TOP = 229376


class Ctx:
    def __init__(self, nc, S):
        self.nc = nc
        self.S = S
        self.NT = S // 128
        self.NG = S // 512
        self.P = Prog(nc)
        self.off = SB_BASE
        self.uid = 0
        self.psum = [Buf(nc.alloc_psum_tensor("ps%d" % i, [128, 512], F32)) for i in range(8)]
        self.rr = 0

    def sb(self, shape, dtype, ntr=1):
        n = 1
        for s in shape[1:]:
            n *= s
        nbytes = n * (4 if dtype == F32 else 2)
        nbytes = (nbytes + 63) // 64 * 64
        assert self.off + nbytes <= SB_TOP, ("SBUF overflow", self.off, nbytes)
        self.uid += 1
        h = self.nc.alloc_sbuf_tensor_at("t%d" % self.uid, list(shape), dtype, offset=self.off)
        self.off += nbytes
        return Buf(h, ntr)

    def mark(self):
        return self.off

    def release(self, m):
        self.off = m

    def eng_rr(self, engs=("dve", "pool")):
        self.rr += 1
        return engs[self.rr % len(engs)]


def copy_op(C, eng, out_ap, in_ap, reads, writes):
    if eng == "act":
        return C.P.op("act", lambda e: e.copy(out=out_ap, in_=in_ap), reads, writes)
    return C.P.op(eng, lambda e: e.tensor_copy(out=out_ap, in_=in_ap), reads, writes)


def load_weight(C, w_dram, K, N, stg, col0=0, ncols=None, engs=("dve", "pool")):
    ncols = N if ncols is None else ncols
    kc = (K + 127) // 128
    W = C.sb([128, kc, ncols], BF16)
    CH = stg[0].h.shape[1]
    i = 0
    for c in range(kc):
        rows = min(128, K - c * 128)
        for n0 in range(0, ncols, CH):
            n1 = min(ncols, n0 + CH)
            s = stg[C.rr % len(stg)]
            C.P.dma("sp", s[0:rows, 0:n1 - n0], w_dram[c * 128:c * 128 + rows, col0 + n0:col0 + n1], writes=[s.t])
            copy_op(C, C.eng_rr(engs), W[0:rows, c, n0:n1], s[0:rows, 0:n1 - n0], [s.t, W.t], [W.t])
    return W


def load_bcast(C, vec_dram, n):
    b = C.sb([128, n], F32)
    C.P.dma("sp", b[:], vec_dram.partition_broadcast(128), writes=[b.t])
    return b


def rstd_from_ss(C, ss, rs, n, dim):
    P = C.P
    P.op("dve", lambda e: e.tensor_scalar(out=ss[:, 0:n], in0=ss[:, 0:n], scalar1=1.0 / dim, scalar2=EPS,
                                          op0=ALU.mult, op1=ALU.add), [ss.t], [ss.t])
    P.op("act", lambda e: e.activation(out=ss[:, 0:n], in_=ss[:, 0:n], func=AF.Sqrt), [ss.t], [ss.t])
    P.op("dve", lambda e: e.reciprocal(out=rs[:, 0:n], in_=ss[:, 0:n]), [ss.t], [rs.t])


def norm_rows(C, xt, gain, hb, junk, ss, rs):
    P = C.P
    P.op("act", lambda e: e.activation(out=junk[:], in_=xt[:], func=AF.Square, accum_out=ss[:, 0:1]),
         [xt.t], [junk.t, ss.t])
    rstd_from_ss(C, ss, rs, 1, D)
    P.op("dve", lambda e: e.scalar_tensor_tensor(out=hb[:], in0=xt[:], scalar=rs[:, 0:1], in1=gain[:],
                                                 op0=ALU.mult, op1=ALU.mult), [xt.t, rs.t, gain.t], [hb.t])


def transpose_cols(C, src, src_tr, ncol, dst_fn, dst_tr, banks, ceng=("act", "dve")):
    P = C.P
    for b0 in range(0, ncol, 4):
        b1 = min(ncol, b0 + 4)
        ps = C.psum[banks[(b0 // 4) % len(banks)]]
        for c in range(b0, b1):
            P.op("pe", lambda e, c=c, ps=ps, b0=b0: e.matmul(out=ps[:, (c - b0) * 128:(c - b0 + 1) * 128],
                                                           lhsT=src[:, c * 128:(c + 1) * 128], rhs=C.ident[:],
                                                           start=True, stop=True),
                 [src_tr, C.ident.t], [ps.t])
        n = b1 - b0
        copy_op(C, C.eng_rr(ceng), dst_fn(b0, b1), ps[:, 0:n * 128].rearrange("p (c t) -> p c t", c=n),
                [ps.t] + list(dst_tr), list(dst_tr))


def setup_consts(C, ident_d, masks_d, tri_d):
    P = C.P
    C.ident = C.sb([128, 128], BF16)
    C.ones = C.sb([128, 128], BF16)
    C.masks = C.sb([128, 8, 512], BF16)
    C.trineg = C.sb([128, 128], BF16)
    C.onesneg = C.sb([128, 128], BF16)
    m = C.mark()
    idf = C.sb([128, 128], F32)
    trf = C.sb([128, 128], F32)
    mf = C.sb([128, 8, 512], F32)
    P.dma("sp", idf[:], ident_d, writes=[idf.t])
    P.dma("sp", trf[:], tri_d, writes=[trf.t])
    P.dma("sp", mf[:], masks_d.rearrange("m p q -> p m q"), writes=[mf.t])
    copy_op(C, "dve", C.ident[:], idf[:], [idf.t], [C.ident.t])
    P.op("dve", lambda e: e.tensor_scalar(out=C.trineg[:], in0=trf[:], scalar1=-1.0, scalar2=None, op0=ALU.mult),
         [trf.t], [C.trineg.t])
    copy_op(C, "pool", C.masks[:], mf[:], [mf.t], [C.masks.t])
    P.op("pool", lambda e: e.memset(C.ones[:], 1.0), [], [C.ones.t])
    P.op("pool", lambda e: e.memset(C.onesneg[:], -1.0), [], [C.onesneg.t])
    P.barrier()
    C.release(m)


def phase_ffn(C, x_d, gain_d, wg_d, wu_d, wd_d):
    P = C.P
    m = C.mark()
    stg = [C.sb([128, 1024], F32), C.sb([128, 1024], F32)]
    Wg = load_weight(C, wg_d, D, DFF, stg)
    Wu = load_weight(C, wu_d, D, DFF, stg)
    Wd = load_weight(C, wd_d, DFF, D, stg)
    gain = load_bcast(C, gain_d, D)
    xs = [C.sb([128, D], F32) for _ in range(4)]
    hb = C.sb([128, D], BF16)
    junk = hb
    ss = C.sb([128, 8], F32)
    rs = C.sb([128, 8], F32)
    hT = C.sb([128, 8, 512], BF16)
    actT = C.sb([128, NFF, 512], BF16, ntr=NFF)
    sg = [C.sb([128, 512], F32), C.sb([128, 512], F32)]
    for g in range(C.NG):
        for t in range(4):
            r0 = g * 512 + t * 128
            P.dma("sp", xs[t][:], x_d[r0:r0 + 128, :], writes=[xs[t].t])
            norm_rows(C, xs[t], gain, hb, junk, ss, rs)
            transpose_cols(C, hb, hb.t, 8, lambda c0, c1, t=t: hT[:, c0:c1, t * 128:(t + 1) * 128], [hT.t], [0, 1])
        for f in range(NFF):
            pg = C.psum[2 + (f % 2)]
            pu = C.psum[4 + (f % 2)]
            for c in range(8):
                P.op("pe", lambda e, c=c, f=f, pg=pg: e.matmul(out=pg[:], lhsT=Wg[:, c, f * 128:(f + 1) * 128],
                                                             rhs=hT[:, c, :], start=(c == 0), stop=(c == 7)),
                     [Wg.t, hT.t], [pg.t])
            for c in range(8):
                P.op("pe", lambda e, c=c, f=f, pu=pu: e.matmul(out=pu[:], lhsT=Wu[:, c, f * 128:(f + 1) * 128],
                                                             rhs=hT[:, c, :], start=(c == 0), stop=(c == 7)),
                     [Wu.t, hT.t], [pu.t])
            s = sg[f % 2]
            P.op("act", lambda e, s=s, pg=pg: e.activation(out=s[:], in_=pg[:], func=AF.Silu), [pg.t], [s.t])
            P.op("dve", lambda e, s=s, pu=pu, f=f: e.tensor_tensor(out=actT[:, f, :], in0=pu[:], in1=s[:], op=ALU.mult),
                 [pu.t, s.t], [actT.tr[f]])
        for t in range(4):
            for h in range(2):
                py = C.psum[6 + h]
                for f in range(NFF):
                    P.op("pe", lambda e, f=f, t=t, h=h, py=py: e.matmul(out=py[:], lhsT=actT[:, f, t * 128:(t + 1) * 128],
                                                                        rhs=Wd[:, f, h * 512:(h + 1) * 512],
                                                                        start=(f == 0), stop=(f == NFF - 1)),
                         [actT.tr[f], Wd.t], [py.t])
                P.op("dve", lambda e, t=t, h=h, py=py: e.scalar_tensor_tensor(
                    out=xs[t][:, h * 512:(h + 1) * 512], in0=py[:], scalar=0.5, in1=xs[t][:, h * 512:(h + 1) * 512],
                    op0=ALU.mult, op1=ALU.add), [py.t, xs[t].t], [xs[t].t])
            r0 = g * 512 + t * 128
            P.dma("sp", x_d[r0:r0 + 128, :], xs[t][:], reads=[xs[t].t])
    P.barrier()
    C.release(m)


def headnorm_rope(C, raw, nh, dh, gain_b, cs_ap, off, half, outb, w1, w2, ss, rs, cs_tr=None):
    P = C.P
    n = nh * dh
    r3 = raw[:, 0:n].rearrange("p (h d) -> p h d", h=nh)
    o3 = outb[:, 0:n].rearrange("p (h d) -> p h d", h=nh)
    a3 = w1[:, 0:n].rearrange("p (h d) -> p h d", h=nh)
    b3 = w2[:, 0:n].rearrange("p (h d) -> p h d", h=nh)
    P.op("act", lambda e: e.activation(out=w1[:, 0:n], in_=raw[:, 0:n], func=AF.Square), [raw.t], [w1.t])
    P.op("dve", lambda e: e.tensor_reduce(out=ss[:, 0:nh], in_=a3, axis=AX.X, op=ALU.add), [w1.t], [ss.t])
    rstd_from_ss(C, ss, rs, nh, dh)
    P.op("dve", lambda e: e.tensor_tensor(out=r3, in0=r3, in1=rs[:, 0:nh].unsqueeze(2).to_broadcast([128, nh, dh]),
                                          op=ALU.mult), [raw.t, rs.t], [raw.t])
    P.op("pool", lambda e: e.tensor_tensor(out=r3, in0=r3, in1=gain_b[:, 0:dh].unsqueeze(1).to_broadcast([128, nh, dh]),
                                           op=ALU.mult), [raw.t, gain_b.t], [raw.t])
    if cs_ap is None:
        copy_op(C, "act", outb[:, 0:n], raw[:, 0:n], [raw.t], [outb.t])
        return
    cos = cs_ap[:, 0:half].unsqueeze(1).to_broadcast([128, nh, half])
    sin = cs_ap[:, half:2 * half].unsqueeze(1).to_broadcast([128, nh, half])
    x1 = r3[:, :, off:off + half]
    x2 = r3[:, :, off + half:off + 2 * half]
    if off > 0:
        copy_op(C, "act", o3[:, :, 0:off], r3[:, :, 0:off], [raw.t, outb.t], [outb.t])
    P.op("pool", lambda e: e.tensor_tensor(out=a3[:, :, 0:half], in0=x1, in1=cos, op=ALU.mult), [raw.t, w1.t, cs_tr], [w1.t])
    P.op("dve", lambda e: e.tensor_tensor(out=b3[:, :, 0:half], in0=x2, in1=sin, op=ALU.mult), [raw.t, w2.t, cs_tr], [w2.t])
    P.op("pool", lambda e: e.tensor_tensor(out=a3[:, :, half:2 * half], in0=x1, in1=sin, op=ALU.mult), [raw.t, w1.t, cs_tr], [w1.t])
    P.op("dve", lambda e: e.tensor_tensor(out=b3[:, :, half:2 * half], in0=x2, in1=cos, op=ALU.mult), [raw.t, w2.t, cs_tr], [w2.t])
    P.op("dve", lambda e: e.tensor_tensor(out=o3[:, :, off:off + half], in0=a3[:, :, 0:half], in1=b3[:, :, 0:half],
                                          op=ALU.subtract), [w1.t, w2.t, outb.t], [outb.t])
    P.op("pool", lambda e: e.tensor_tensor(out=o3[:, :, off + half:off + 2 * half], in0=a3[:, :, half:2 * half],
                                           in1=b3[:, :, half:2 * half], op=ALU.add), [w1.t, w2.t, outb.t], [outb.t])


def store_T(C, outb, ncols, dstT, t, tbuf):
    import os
    nchunk = ncols // 128
    transpose_cols(C, outb, outb.t, nchunk, lambda c0, c1: tbuf[:, c0:c1, :], [tbuf.t], [0, 1])
    if "s" in os.environ.get("MSKIP", ""):
        return
    for c0 in range(0, nchunk, 4):
        C.P.dma("sp", dstT.rearrange("(c p) s -> p c s", p=128)[:, c0:c0 + 4, t * 128:(t + 1) * 128],
                tbuf[:, c0:c0 + 4, :], reads=[tbuf.t])


def proj(C, hT, kc, W, c0, n, bank, tok0=0, ntok=128):
    ps = C.psum[bank]
    for c in range(kc):
        C.P.op("pe", lambda e, c=c: e.matmul(out=ps[0:ntok, 0:n], lhsT=hT[:, c, tok0:tok0 + ntok], rhs=W[:, c, c0:c0 + n],
                                             start=(c == 0), stop=(c == kc - 1)), [hT.t, W.t], [ps.t])
    return ps


def scaled_gain(C, vec_d, n, scale):
    g = load_bcast(C, vec_d, n)
    if scale != 1.0:
        C.P.op("dve", lambda e: e.tensor_scalar(out=g[:], in0=g[:], scalar1=float(scale), scalar2=None, op0=ALU.mult),
               [g.t], [g.t])
    return g


def phase_ab_proj(C, x_d, gain_d, w_in_d, qn_d, kn_d, cs_d, scr):
    P = C.P
    S = C.S
    m = C.mark()
    stg = [C.sb([128, 1024], F32), C.sb([128, 1024], F32)]
    W = load_weight(C, w_in_d, D, 3072, stg)
    gain = load_bcast(C, gain_d, D)
    gq = scaled_gain(C, qn_d, 64, 0.125)
    gk = scaled_gain(C, kn_d, 64, 1.0)
    cs = C.sb([128, C.NT, 64], F32)
    P.dma("sp", cs[:], cs_d.rearrange("(t p) c -> p t c", p=128), writes=[cs.t])
    xt = [C.sb([128, D], F32) for _ in range(2)]
    hb = C.sb([128, D], BF16)
    ss = C.sb([128, 16], F32)
    rs = C.sb([128, 16], F32)
    hT = C.sb([128, 8, 128], BF16)
    raw = C.sb([128, 512], F32)
    w1 = C.sb([128, 512], F32)
    w2 = C.sb([128, 512], F32)
    outb = [C.sb([128, 512], BF16) for _ in range(2)]
    tbuf = [C.sb([128, 4, 128], BF16) for _ in range(2)]
    vb = [C.sb([128, 512], BF16) for _ in range(2)]
    qaT, kaT, va, qbT, kbT, vbd = scr["qaT"], scr["kaT"], scr["va"], scr["qbT"], scr["kbT"], scr["vb"]
    k = 0
    for t in range(C.NT):
        x = xt[t % 2]
        P.dma("sp", x[:], x_d[t * 128:(t + 1) * 128, :], writes=[x.t])
        norm_rows(C, x, gain, hb, hb, ss, rs)
        transpose_cols(C, hb, hb.t, 8, lambda c0, c1: hT[:, c0:c1, :], [hT.t], [0, 1])
        for grp in range(6):
            ps = proj(C, hT, 8, W, grp * 512, 512, 2 + (k % 6))
            k += 1
            if grp in (0, 1):
                copy_op(C, "act", raw[:], ps[:], [ps.t], [raw.t])
                ob = outb[grp]
                headnorm_rope(C, raw, 8, 64, gq if grp == 0 else gk, cs[:, t, :], 0, 32, ob, w1, w2, ss, rs, cs.t)
                store_T(C, ob, 512, qaT if grp == 0 else kaT, t, tbuf[grp])
            elif grp in (2, 5):
                v = vb[0 if grp == 2 else 1]
                copy_op(C, "act", v[:], ps[:], [ps.t, v.t], [v.t])
                P.dma("sp", (va if grp == 2 else vbd)[t * 128:(t + 1) * 128, :], v[:], reads=[v.t])
            else:
                ob = outb[grp - 3]
                if grp == 3:
                    P.op("act", lambda e, ob=ob, ps=ps: e.activation(out=ob[:], in_=ps[:], func=AF.Copy, scale=0.125),
                         [ps.t, ob.t], [ob.t])
                else:
                    copy_op(C, "dve", ob[:], ps[:], [ps.t, ob.t], [ob.t])
                store_T(C, ob, 512, qbT if grp == 3 else kbT, t, tbuf[grp - 3])
    P.barrier()
    C.release(m)


def load_head(C, qT_d, kT_d, dk, QT, KT):
    C.P.dma("sp", QT[0:dk, :], qT_d, reads=[QT.t], writes=[QT.t])
    C.P.dma("sp", KT[0:dk, :], kT_d, reads=[KT.t], writes=[KT.t])


def softmax_block(C, QT, KT, dk, Vl, qs, mask0, pbufs, bank_o, bank_d, k_ctr, sbanks=(0, 1)):
    P = C.P
    nj = 4 * qs + 4
    po = C.psum[bank_o]
    pd = C.psum[bank_d] if bank_d is not None else None
    st = {}

    def stage_a(j):
        ps = C.psum[sbanks[k_ctr[0] % len(sbanks)]]
        pt = pbufs[k_ctr[0] % len(pbufs)]
        k_ctr[0] += 1
        st[j] = (ps, pt)
        P.op("pe", lambda e, j=j, ps=ps: e.matmul(out=ps[:], lhsT=KT[0:dk, j * 128:(j + 1) * 128],
                                                 rhs=QT[0:dk, qs * 512:(qs + 1) * 512], start=True, stop=True),
             [KT.t, QT.t], [ps.t])

    def stage_b(j):
        ps, pt = st.pop(j)
        P.op("act", lambda e, ps=ps, pt=pt: e.activation(out=pt[:], in_=ps[:], func=AF.Exp), [ps.t], [pt.t])
        if j >= 4 * qs:
            mi = mask0 + j - 4 * qs
            P.op("pool", lambda e, pt=pt, mi=mi: e.tensor_tensor(out=pt[:], in0=pt[:], in1=C.masks[:, mi, :], op=ALU.mult),
                 [pt.t, C.masks.t], [pt.t])
        P.op("pe", lambda e, j=j, pt=pt: e.matmul(out=po[:], lhsT=Vl(j), rhs=pt[:], start=(j == 0), stop=(j == nj - 1)),
             [pt.t, Vl.tr], [po.t])
        if pd is not None:
            P.op("pe", lambda e, j=j, pt=pt: e.matmul(out=pd[:], lhsT=C.ones[:], rhs=pt[:], start=(j == 0),
                                                     stop=(j == nj - 1)), [pt.t, C.ones.t], [pd.t])
    la = min(2, len(sbanks) - 1)
    for j in range(min(la, nj)):
        stage_a(j)
    for j in range(nj):
        if j + la < nj:
            stage_a(j + la)
        stage_b(j)


class VL:
    def __init__(self, fn, tr):
        self.fn = fn
        self.tr = tr

    def __call__(self, j):
        return self.fn(j)


def phase_diff_attn(C, scr, lam_d, subln_d, lambda_init):
    P = C.P
    S = C.S
    m = C.mark()
    QT = [C.sb([128, S], BF16) for _ in range(2)]
    KT = [C.sb([128, S], BF16) for _ in range(2)]
    V = [C.sb([128, C.NT, 128], BF16) for _ in range(2)]
    pb = [C.sb([128, 512], BF16) for _ in range(4)]
    A = C.sb([128, 512], F32)
    B = C.sb([128, 512], F32)
    R = C.sb([128, 512], F32)
    ob = [C.sb([128, 512], BF16) for _ in range(2)]
    Bh = C.sb([128, 512], BF16)
    Bl = C.sb([128, 512], BF16)
    lv = C.sb([128, 4, 64], F32)
    P.dma("sp", lv[:].rearrange("p a b -> p (a b)"), lam_d.partition_broadcast(128), writes=[lv.t])
    lt = C.sb([128, 2, 64], F32)
    ls = C.sb([128, 4], F32)
    P.op("dve", lambda e: e.tensor_tensor(out=lt[:], in0=lv[:, 0:4:2, :], in1=lv[:, 1:4:2, :], op=ALU.mult), [lv.t], [lt.t])
    P.op("dve", lambda e: e.tensor_reduce(out=ls[:, 0:2], in_=lt[:], axis=AX.X, op=ALU.add), [lt.t], [ls.t])
    P.op("act", lambda e: e.activation(out=ls[:, 0:2], in_=ls[:, 0:2], func=AF.Exp), [ls.t], [ls.t])
    P.op("dve", lambda e: e.tensor_tensor(out=ls[:, 2:3], in0=ls[:, 1:2], in1=ls[:, 0:1], op=ALU.subtract), [ls.t], [ls.t])
    P.op("dve", lambda e: e.tensor_scalar(out=ls[:, 3:4], in0=ls[:, 2:3], scalar1=-float(lambda_init), scalar2=None,
                                          op0=ALU.add), [ls.t], [ls.t])
    sub = C.sb([128, 1], F32)
    P.dma("sp", sub[:], subln_d.rearrange("(p o) -> p o", o=1), writes=[sub.t])
    P.op("dve", lambda e: e.tensor_scalar(out=sub[:], in0=sub[:], scalar1=float(1.0 - lambda_init), scalar2=None,
                                          op0=ALU.mult), [sub.t], [sub.t])
    kc = [0]
    for h in range(4):
        Vh = V[h % 2]
        for t0 in range(0, C.NT, 8):
            P.dma("sp", Vh[:, t0:min(t0 + 8, C.NT), :], scr["va"].rearrange("(t p) c -> p t c", p=128)[:, t0:min(t0 + 8, C.NT), h * 128:(h + 1) * 128],
                  reads=[Vh.t], writes=[Vh.t])
        for mp in range(2):
            hh = h + 4 * mp
            load_head(C, scr["qaT"][hh * 64:(hh + 1) * 64, :], scr["kaT"][hh * 64:(hh + 1) * 64, :], 64, QT[mp], KT[mp])
        for qs in range(C.NG):
            for mp in range(2):
                softmax_block(C, QT[mp], KT[mp], 64, VL(lambda j, Vh=Vh: Vh[:, j, :], Vh.t), qs, 0, pb, 2 + 2 * mp, 3 + 2 * mp, kc, sbanks=(0, 1, 7))
                po, pd = C.psum[2 + 2 * mp], C.psum[3 + 2 * mp]
                P.op("dve", lambda e, pd=pd: e.reciprocal(out=R[:], in_=pd[:]), [pd.t], [R.t])
                dst = A if mp == 0 else B
                P.op("dve", lambda e, po=po, dst=dst: e.tensor_tensor(out=dst[:], in0=po[:], in1=R[:], op=ALU.mult),
                     [po.t, R.t], [dst.t])
            P.op("dve", lambda e: e.scalar_tensor_tensor(out=A[:], in0=B[:], scalar=ls[:, 3:4], in1=A[:], op0=ALU.mult,
                                                         op1=ALU.add), [A.t, B.t, ls.t], [A.t])
            P.op("act", lambda e: e.activation(out=B[:], in_=A[:], func=AF.Square), [A.t], [B.t])
            pq = C.psum[6]
            P.op("dve", lambda e: e.tensor_copy(out=Bh[:], in_=B[:]), [B.t, Bh.t], [Bh.t])
            P.op("dve", lambda e: e.tensor_tensor(out=Bl[:], in0=B[:], in1=Bh[:], op=ALU.subtract), [B.t, Bh.t, Bl.t], [Bl.t])
            P.op("pe", lambda e, pq=pq: e.matmul(out=pq[:], lhsT=C.ones[:], rhs=Bh[:], start=True, stop=False),
                 [C.ones.t, Bh.t], [pq.t])
            P.op("pe", lambda e, pq=pq: e.matmul(out=pq[:], lhsT=C.ones[:], rhs=Bl[:], start=False, stop=True),
                 [C.ones.t, Bl.t], [pq.t])
            P.op("dve", lambda e, pq=pq: e.tensor_scalar(out=R[:], in0=pq[:], scalar1=1.0 / 128, scalar2=EPS, op0=ALU.mult,
                                                         op1=ALU.add), [pq.t], [R.t])
            P.op("act", lambda e: e.activation(out=R[:], in_=R[:], func=AF.Sqrt), [R.t], [R.t])
            P.op("dve", lambda e: e.reciprocal(out=R[:], in_=R[:]), [R.t], [R.t])
            o = ob[qs % 2]
            P.op("dve", lambda e, o=o: e.scalar_tensor_tensor(out=o[:], in0=A[:], scalar=sub[:, 0:1], in1=R[:], op0=ALU.mult,
                                                              op1=ALU.mult), [A.t, R.t, sub.t, o.t], [o.t])
            P.dma("sp", scr["mixT"][h * 128:(h + 1) * 128, qs * 512:(qs + 1) * 512], o[:], reads=[o.t])
    P.barrier()
    C.release(m)


def phase_sb_attn(C, scr):
    P = C.P
    S = C.S
    m = C.mark()
    QT = [C.sb([128, S], BF16) for _ in range(2)]
    KT = [C.sb([128, S], BF16) for _ in range(2)]
    V = [C.sb([128, C.NT, 128], BF16) for _ in range(2)]
    for b_ in QT + KT:
        P.op("pool", lambda e, b_=b_: e.memset(b_[64:128, :], 0.0), [], [b_.t])
    for v in V:
        P.op("pool", lambda e, v=v: e.memset(v[:, :, 64:128], 0.0), [], [v.t])
    ef = [C.sb([128, 512], F32) for _ in range(2)]
    sp = [C.sb([128, 512], BF16) for _ in range(2)]
    wb = [C.sb([128, 512], BF16) for _ in range(2)]
    Ls = C.sb([128, 512], F32)
    Lb = [C.sb([128, 512], BF16) for _ in range(3)]
    ob = [C.sb([128, 512], BF16) for _ in range(2)]
    k = 0
    for h in range(8):
        Vh, Q, K = V[h % 2], QT[h % 2], KT[h % 2]
        for t0 in range(0, C.NT, 8):
            P.dma("sp", Vh[:, t0:min(t0 + 8, C.NT), 0:64], scr["vb"].rearrange("(t p) c -> p t c", p=128)[:, t0:min(t0 + 8, C.NT), h * 64:(h + 1) * 64],
                  reads=[Vh.t], writes=[Vh.t])
        load_head(C, scr["qbT"][h * 64:(h + 1) * 64, :], scr["kbT"][h * 64:(h + 1) * 64, :], 64, Q, K)
        for qs in range(C.NG):
            nj = 4 * qs + 4
            po = C.psum[4 + (qs % 2)]
            js = list(range(nj - 1, -1, -1))
            st = {}

            def stage_a(idx, js=js, qs=qs, Q=Q, K=K, st=st):
                nonlocal k
                j = js[idx]
                pz = C.psum[k % 2]
                pc = C.psum[2 + (k % 2)]
                e_, s_, w_ = ef[k % 2], sp[k % 2], wb[k % 2]
                lb_in = Lb[idx % 3]
                lb_out = Lb[(idx + 1) % 3]
                k += 1
                st[idx] = (j, pc, s_, w_, lb_in)
                diag = j >= 4 * qs
                mi = 4 + j - 4 * qs
                for pp in (pz, pc):
                    P.op("pe", lambda e, j=j, pp=pp, last=(pp is pz): e.matmul(
                        out=pp[:], lhsT=K[:, j * 128:(j + 1) * 128], rhs=Q[:, qs * 512:(qs + 1) * 512],
                        start=True, stop=last), [K.t, Q.t], [pp.t])
                P.op("act", lambda e, pz=pz, e_=e_: e.activation(out=e_[:], in_=pz[:], func=AF.Exp), [pz.t], [e_.t])
                P.op("act", lambda e, e_=e_, s_=s_: e.activation(out=s_[:], in_=e_[:], func=AF.Ln, bias=1.0, scale=1.0),
                     [e_.t], [s_.t])
                if diag:
                    P.op("pool", lambda e, s_=s_, mi=mi: e.tensor_tensor(out=s_[:], in0=s_[:], in1=C.masks[:, mi, :],
                                                                        op=ALU.mult), [s_.t, C.masks.t], [s_.t])
                if idx + 1 < len(js):
                    if idx == 0:
                        copy_op(C, "dve", Ls[:], s_[:], [s_.t, Ls.t], [Ls.t])
                    else:
                        P.op("dve", lambda e, s_=s_: e.tensor_tensor(out=Ls[:], in0=Ls[:], in1=s_[:], op=ALU.add),
                             [Ls.t, s_.t], [Ls.t])
                    copy_op(C, "pool", lb_out[:], Ls[:], [Ls.t, lb_out.t], [lb_out.t])

            def stage_b(idx, js=js, qs=qs, st=st, po=po, Vh=Vh):
                j, pc, s_, w_, lb = st.pop(idx)
                first = idx == 0
                diag = j >= 4 * qs
                mi = 4 + j - 4 * qs
                P.op("pe", lambda e, pc=pc, s_=s_, first=first: e.matmul(out=pc[:], lhsT=C.trineg[:], rhs=s_[:], start=False,
                                                                        stop=first), [C.trineg.t, s_.t], [pc.t])
                if not first:
                    P.op("pe", lambda e, pc=pc, lb=lb: e.matmul(out=pc[:], lhsT=C.onesneg[:], rhs=lb[:], start=False, stop=True),
                         [C.onesneg.t, lb.t], [pc.t])
                P.op("act", lambda e, pc=pc, w_=w_: e.activation(out=w_[:], in_=pc[:], func=AF.Exp), [pc.t], [w_.t])
                if diag:
                    P.op("pool", lambda e, w_=w_, mi=mi: e.tensor_tensor(out=w_[:], in0=w_[:], in1=C.masks[:, mi, :],
                                                                        op=ALU.mult), [w_.t, C.masks.t], [w_.t])
                P.op("pe", lambda e, j=j, w_=w_, first=first, po=po, Vh=Vh: e.matmul(out=po[:], lhsT=Vh[:, j, :], rhs=w_[:],
                                                                                    start=first, stop=(j == 0)), [Vh.t, w_.t], [po.t])
            stage_a(0)
            for idx in range(nj):
                if idx + 1 < nj:
                    stage_a(idx + 1)
                stage_b(idx)
            o = ob[qs % 2]
            copy_op(C, "dve", o[0:64, :], po[0:64, :], [po.t, o.t], [o.t])
            P.dma("sp", scr["mixT"][512 + h * 64:512 + (h + 1) * 64, qs * 512:(qs + 1) * 512], o[0:64, :], reads=[o.t])
    P.barrier()
    C.release(m)


def phase_outproj(C, x_d, mixT_d, w_d):
    P = C.P
    m = C.mark()
    stg = [C.sb([128, 1024], F32), C.sb([128, 1024], F32)]
    W = load_weight(C, w_d, D, D, stg)
    mT = [C.sb([128, 8, 512], BF16) for _ in range(2)]
    xs = [C.sb([128, D], F32) for _ in range(3)]
    k = 0
    for g in range(C.NG):
        mt = mT[g % 2]
        P.dma("sp", mt[:], mixT_d.rearrange("(c p) s -> p c s", p=128)[:, :, g * 512:(g + 1) * 512], writes=[mt.t])
        for t in range(4):
            r0 = g * 512 + t * 128
            x = xs[k % 3]
            k += 1
            P.dma("sp", x[:], x_d[r0:r0 + 128, :], writes=[x.t])
            for h in range(2):
                py = C.psum[2 * (k % 2) + h]
                for c in range(8):
                    P.op("pe", lambda e, c=c, t=t, h=h, py=py, mt=mt: e.matmul(
                        out=py[:], lhsT=mt[:, c, t * 128:(t + 1) * 128], rhs=W[:, c, h * 512:(h + 1) * 512],
                        start=(c == 0), stop=(c == 7)), [mt.t, W.t], [py.t])
                P.op("dve", lambda e, h=h, py=py, x=x: e.tensor_tensor(out=x[:, h * 512:(h + 1) * 512], in0=py[:],
                                                                     in1=x[:, h * 512:(h + 1) * 512], op=ALU.add),
                     [py.t, x.t], [x.t])
            P.dma("sp", x_d[r0:r0 + 128, :], x[:], reads=[x.t])
    P.barrier()
    C.release(m)


def phase_xm(C, x_d, mem_d, xn_d, mn_d, wq_d, wkv_d, qn_d, kn_d, wo_d):
    P = C.P
    m = C.mark()
    stg = [C.sb([128, 1024], F32), C.sb([128, 1024], F32)]
    Wkv = load_weight(C, wkv_d, D, D, stg)
    gm = load_bcast(C, mn_d, D)
    gk = scaled_gain(C, kn_d, 128, 1.0)
    gq = scaled_gain(C, qn_d, 128, 128 ** -0.5)
    hb = C.sb([128, D], BF16)
    ss = C.sb([128, 8], F32)
    rs = C.sb([128, 8], F32)
    hT = C.sb([128, 8, 128], BF16)
    raw = C.sb([128, 512], F32)
    w1 = C.sb([128, 512], F32)
    outb = C.sb([128, 512], BF16)
    KT = C.sb([128, 4, MEM], BF16)
    Vm = C.sb([128, 2, 512], BF16)
    xs = [C.sb([128, D], F32) for _ in range(4)]
    for t in range(2):
        x = xs[t]
        P.dma("sp", x[:], mem_d[t * 128:(t + 1) * 128, :], writes=[x.t])
        norm_rows(C, x, gm, hb, hb, ss, rs)
        transpose_cols(C, hb, hb.t, 8, lambda c0, c1: hT[:, c0:c1, :], [hT.t], [0, 1])
        ps = proj(C, hT, 8, Wkv, 0, 512, 2)
        copy_op(C, "act", raw[:], ps[:], [ps.t], [raw.t])
        headnorm_rope(C, raw, 4, 128, gk, None, 0, 0, outb, w1, w1, ss, rs)
        transpose_cols(C, outb, outb.t, 4, lambda c0, c1, t=t: KT[:, c0:c1, t * 128:(t + 1) * 128], [KT.t], [0, 1])
        ps = proj(C, hT, 8, Wkv, 512, 512, 3)
        copy_op(C, "act", Vm[:, t, :], ps[:], [ps.t, Vm.t], [Vm.t])
    Wq = load_weight(C, wq_d, D, 512, stg)
    Wo = load_weight(C, wo_d, 512, D, stg)
    gx = load_bcast(C, xn_d, D)
    hT4 = C.sb([128, 8, 512], BF16)
    qT = C.sb([128, 4, 512], BF16)
    xoT = C.sb([128, 4, 512], BF16, ntr=4)
    pb = [C.sb([128, 512], BF16) for _ in range(3)]
    R = C.sb([128, 512], F32)
    kk = 0
    for g in range(C.NG):
        for t in range(4):
            r0 = g * 512 + t * 128
            x = xs[t]
            P.dma("sp", x[:], x_d[r0:r0 + 128, :], writes=[x.t])
            norm_rows(C, x, gx, hb, hb, ss, rs)
            transpose_cols(C, hb, hb.t, 8, lambda c0, c1, t=t: hT4[:, c0:c1, t * 128:(t + 1) * 128], [hT4.t], [0, 1])
            ps = proj(C, hT4, 8, Wq, 0, 512, 2 + (t % 2), tok0=t * 128)
            copy_op(C, "act", raw[:], ps[:], [ps.t], [raw.t])
            headnorm_rope(C, raw, 4, 128, gq, None, 0, 0, outb, w1, w1, ss, rs)
            transpose_cols(C, outb, outb.t, 4, lambda c0, c1, t=t: qT[:, c0:c1, t * 128:(t + 1) * 128], [qT.t], [0, 1])
        for h in range(4):
            po, pd = C.psum[4], C.psum[5]
            for mt in range(2):
                ps = C.psum[2 + (kk % 2)]
                pt = pb[kk % 3]
                kk += 1
                P.op("pe", lambda e, h=h, mt=mt, ps=ps: e.matmul(out=ps[:], lhsT=KT[:, h, mt * 128:(mt + 1) * 128],
                                                               rhs=qT[:, h, :], start=True, stop=True), [KT.t, qT.t], [ps.t])
                P.op("act", lambda e, ps=ps, pt=pt: e.activation(out=pt[:], in_=ps[:], func=AF.Exp), [ps.t], [pt.t])
                P.op("pe", lambda e, h=h, mt=mt, pt=pt: e.matmul(out=po[:], lhsT=Vm[:, mt, h * 128:(h + 1) * 128], rhs=pt[:],
                                                               start=(mt == 0), stop=(mt == 1)), [Vm.t, pt.t], [po.t])
                P.op("pe", lambda e, mt=mt, pt=pt: e.matmul(out=pd[:], lhsT=C.ones[:], rhs=pt[:], start=(mt == 0),
                                                          stop=(mt == 1)), [C.ones.t, pt.t], [pd.t])
            P.op("dve", lambda e, pd=pd: e.reciprocal(out=R[:], in_=pd[:]), [pd.t], [R.t])
            P.op("dve", lambda e, po=po, h=h: e.tensor_tensor(out=xoT[:, h, :], in0=po[:], in1=R[:], op=ALU.mult),
                 [po.t, R.t], [xoT.tr[h]])
        for t in range(4):
            r0 = g * 512 + t * 128
            x = xs[t]
            for hf in range(2):
                py = C.psum[6 + hf]
                for h in range(4):
                    P.op("pe", lambda e, h=h, t=t, hf=hf, py=py: e.matmul(
                        out=py[:], lhsT=xoT[:, h, t * 128:(t + 1) * 128], rhs=Wo[:, h, hf * 512:(hf + 1) * 512],
                        start=(h == 0), stop=(h == 3)), [xoT.tr[h], Wo.t], [py.t])
                P.op("dve", lambda e, hf=hf, py=py, x=x: e.tensor_tensor(out=x[:, hf * 512:(hf + 1) * 512], in0=py[:],
                                                                       in1=x[:, hf * 512:(hf + 1) * 512], op=ALU.add),
                     [py.t, x.t], [x.t])
            P.dma("sp", x_d[r0:r0 + 128, :], x[:], reads=[x.t])
    P.barrier()
    C.release(m)


def phase_mla_proj(C, x_d, gain_d, wdq_d, qln_d, wuq_d, wdkv_d, kvln_d, wukv_d, nq_d, nk_d, cs_d, scr):
    P = C.P
    m = C.mark()
    stg = [C.sb([128, 1024], F32), C.sb([128, 1024], F32)]
    Wdq = load_weight(C, wdq_d, D, 512, stg)
    Wuq = load_weight(C, wuq_d, 512, 1536, stg)
    Wdkv = load_weight(C, wdkv_d, D, 288, stg)
    Wukv = load_weight(C, wukv_d, 256, 2048, stg)
    gain = load_bcast(C, gain_d, D)
    gql = load_bcast(C, qln_d, 512)
    gkvl = load_bcast(C, kvln_d, 256)
    gq = scaled_gain(C, nq_d, 96, 96 ** -0.5)
    gk = scaled_gain(C, nk_d, 96, 1.0)
    import os
    MS = os.environ.get("MSKIP", "")
    cs = C.sb([128, C.NT, 32], F32)
    if "c" not in MS:
        P.dma("sp", cs[:], cs_d.rearrange("(t p) c -> p t c", p=128), writes=[cs.t])
    xt = [C.sb([128, D], F32) for _ in range(2)]
    hb = C.sb([128, D], BF16)
    ss = C.sb([128, 16], F32)
    rs = C.sb([128, 16], F32)
    hT = C.sb([128, 8, 128], BF16)
    cq = C.sb([128, 512], F32)
    cqb = C.sb([128, 512], BF16)
    cqT = C.sb([128, 4, 128], BF16)
    dkv = C.sb([128, 288], F32)
    ckb = C.sb([128, 256], BF16)
    ckT = C.sb([128, 2, 128], BF16)
    raw = C.sb([128, 1536], F32)
    w1 = C.sb([128, 1536], F32)
    w2 = C.sb([128, 1536], F32)
    outb = [C.sb([128, 1536], BF16) for _ in range(2)]
    tbuf = [C.sb([128, 12, 128], BF16) for _ in range(2)]
    vb = C.sb([128, 16, 64], BF16)
    kvs = [C.sb([128, 512], F32) for _ in range(2)]
    qT_d, kT_d, v_d = scr["mqT"], scr["mkT"], scr["mv"]
    for t in range(C.NT):
        x = xt[t % 2]
        P.dma("sp", x[:], x_d[t * 128:(t + 1) * 128, :], writes=[x.t])
        norm_rows(C, x, gain, hb, hb, ss, rs)
        transpose_cols(C, hb, hb.t, 8, lambda c0, c1: hT[:, c0:c1, :], [hT.t], [0, 1])
        ps = proj(C, hT, 8, Wdq, 0, 512, 2)
        copy_op(C, "act", cq[:], ps[:], [ps.t], [cq.t])
        P.op("act", lambda e: e.activation(out=w1[:, 0:512], in_=cq[:], func=AF.Square, accum_out=ss[:, 0:1]),
             [cq.t], [w1.t, ss.t])
        rstd_from_ss(C, ss, rs, 1, 512)
        P.op("dve", lambda e: e.scalar_tensor_tensor(out=cqb[:], in0=cq[:], scalar=rs[:, 0:1], in1=gql[:], op0=ALU.mult,
                                                     op1=ALU.mult), [cq.t, rs.t, gql.t], [cqb.t])
        transpose_cols(C, cqb, cqb.t, 4, lambda c0, c1: cqT[:, c0:c1, :], [cqT.t], [0, 1])
        ps = proj(C, hT, 8, Wdkv, 0, 288, 3)
        copy_op(C, "act", dkv[:], ps[:, 0:288], [ps.t], [dkv.t])
        P.op("act", lambda e: e.activation(out=w1[:, 0:256], in_=dkv[:, 0:256], func=AF.Square, accum_out=ss[:, 0:1]),
             [dkv.t, ss.t], [w1.t, ss.t])
        rstd_from_ss(C, ss, rs, 1, 256)
        P.op("dve", lambda e: e.scalar_tensor_tensor(out=ckb[:], in0=dkv[:, 0:256], scalar=rs[:, 0:1], in1=gkvl[:],
                                                     op0=ALU.mult, op1=ALU.mult), [dkv.t, rs.t, gkvl.t], [ckb.t])
        transpose_cols(C, ckb, ckb.t, 2, lambda c0, c1: ckT[:, c0:c1, :], [ckT.t], [0, 1])
        r3 = raw[:].rearrange("p (h d) -> p h d", h=16)
        if "Q" in MS:
            continue
        for g4 in range(4):
            ps = proj(C, cqT, 4, Wuq, g4 * 384, 384, 4 + g4)
            copy_op(C, "act" if g4 % 2 else "dve", raw[:, g4 * 384:(g4 + 1) * 384], ps[:, 0:384], [ps.t, raw.t], [raw.t])
        import os
        MS = os.environ.get("MSKIP", "")
        headnorm_rope(C, raw, 16, 96, gq, None if "r" in MS else cs[:, t, :], 64, 16, outb[0], w1, w2, ss, rs, cs.t)
        store_T(C, outb[0], 1536, qT_d, t, tbuf[0])
        if "K" in MS:
            continue
        for g4 in range(4):
            ps = proj(C, ckT, 2, Wukv, g4 * 512, 512, 4 + g4)
            kv = kvs[g4 % 2]
            copy_op(C, "act", kv[:], ps[:], [ps.t], [kv.t])
            p3 = kv[:].rearrange("p (h d) -> p h d", h=4)
            copy_op(C, "pool", r3[:, g4 * 4:(g4 + 1) * 4, 0:64], p3[:, :, 0:64], [kv.t, raw.t], [raw.t])
            copy_op(C, "dve", vb[:, g4 * 4:(g4 + 1) * 4, :], p3[:, :, 64:128], [kv.t, vb.t], [vb.t])
        copy_op(C, "dve" if "b" in MS else "pool", r3[:, :, 64:96], dkv[:, 256:288].unsqueeze(1).to_broadcast([128, 16, 32]), [dkv.t, raw.t], [raw.t])
        if "v" not in MS:
            P.dma("sp", v_d[t * 128:(t + 1) * 128, :], vb[:].rearrange("p h d -> p (h d)"), reads=[vb.t])
        headnorm_rope(C, raw, 16, 96, gk, None if "r" in MS else cs[:, t, :], 64, 16, outb[1], w1, w2, ss, rs, cs.t)
        store_T(C, outb[1], 1536, kT_d, t, tbuf[1])
    P.barrier()
    C.release(m)


def phase_mla_attn(C, scr):
    P = C.P
    S = C.S
    m = C.mark()
    QT = [C.sb([128, S], BF16) for _ in range(2)]
    KT = [C.sb([128, S], BF16) for _ in range(2)]
    V = [C.sb([128, C.NT, 128], BF16) for _ in range(2)]
    for v in V:
        P.op("pool", lambda e, v=v: e.memset(v[:, :, 64:128], 1.0), [], [v.t])
    pb = [C.sb([128, 512], BF16) for _ in range(5)]
    R = C.sb([128, 512], F32)
    ob = [C.sb([128, 512], BF16) for _ in range(2)]
    kc = [0]
    k = 0
    for h in range(16):
        Vh, Q, K = V[h % 2], QT[h % 2], KT[h % 2]
        for t0 in range(0, C.NT, 8):
            P.dma("sp", Vh[:, t0:min(t0 + 8, C.NT), 0:64], scr["mv"].rearrange("(t p) c -> p t c", p=128)[:, t0:min(t0 + 8, C.NT), h * 64:(h + 1) * 64],
                  reads=[Vh.t], writes=[Vh.t])
        load_head(C, scr["mqT"][h * 96:(h + 1) * 96, :], scr["mkT"][h * 96:(h + 1) * 96, :], 96, Q, K)
        for qs in range(C.NG):
            bo = 2 + (k % 2)
            k += 1
            softmax_block(C, Q, K, 96, VL(lambda j, Vh=Vh: Vh[:, j, :], Vh.t), qs, 0, pb, bo, None, kc, sbanks=(0, 1, 4, 5))
            po = C.psum[bo]
            P.op("dve", lambda e, po=po: e.reciprocal(out=R[0:64, :], in_=po[64:128, :]), [po.t], [R.t])
            o = ob[k % 2]
            P.op("dve", lambda e, po=po, o=o: e.tensor_tensor(out=o[0:64, :], in0=po[0:64, :], in1=R[0:64, :], op=ALU.mult),
                 [po.t, R.t, o.t], [o.t])
            P.dma("sp", scr["mixT"][h * 64:(h + 1) * 64, qs * 512:(qs + 1) * 512], o[0:64, :], reads=[o.t])
    P.barrier()
    C.release(m)


WNAMES = ["ffn_norm", "ffn_w_gate", "ffn_w_up", "ffn_w_down", "mix_norm", "ab_w_in", "ab_w_out", "diff_q_norm",
          "diff_k_norm", "diff_subln", "mla_w_dq", "mla_q_norm", "mla_w_uq", "mla_w_dkv", "mla_kv_norm", "mla_w_ukv",
          "mla_qk_norm_q", "mla_qk_norm_k", "mla_w_o", "xm_norm", "xm_mem_norm", "xm_w_q", "xm_w_kv", "xm_q_norm",
          "xm_k_norm", "xm_w_o"]


def build(S, shapes, phases=None):
    nc = bass.Bass("TRN2", target_bir_lowering=False)
    ins = {}
    for name, shp in shapes.items():
        ins[name] = nc.dram_tensor(name, list(shp), F32, kind="ExternalInput").ap()
    out = nc.dram_tensor("out", [S, D], F32, kind="ExternalOutput").ap()
    scr_t = nc.dram_tensor("scr", [5120 * S], BF16).ap()

    def sv(off, rows, cols):
        return scr_t[off * S:(off + rows * cols // S) * S].rearrange("(r c) -> r c", c=cols)
    scr = {"qaT": sv(0, 512, S), "kaT": sv(512, 512, S), "va": sv(1024, S, 512), "qbT": sv(1536, 512, S),
           "kbT": sv(2048, 512, S), "vb": sv(2560, S, 512),
           "mqT": sv(0, 1536, S), "mkT": sv(1536, 1536, S), "mv": sv(3072, S, 1024), "mixT": sv(4096, 1024, S)}
    C = Ctx(nc, S)
    P = C.P
    setup_consts(C, ins["c_ident"], ins["c_masks"], ins["c_tri"])
    xin = Tr()
    P.dma("sp", out, ins["x"], writes=[xin])
    P.barrier()
    w = ins
    allp = ["ffn00", "abproj", "diff", "sb", "out0", "xm0", "ffn01", "ffn10", "mlaproj", "mlaattn", "out1", "xm1", "ffn11"]
    for ph in (allp if phases is None else phases):
        if ph.startswith("ffn"):
            l, i = int(ph[3]), int(ph[4])
            phase_ffn(C, out, w["ffn_norm"][l, i], w["ffn_w_gate"][l, i], w["ffn_w_up"][l, i], w["ffn_w_down"][l, i])
        elif ph == "abproj":
            phase_ab_proj(C, out, w["mix_norm"][0], w["ab_w_in"][0], w["diff_q_norm"][0], w["diff_k_norm"][0], w["c_cs64"], scr)
        elif ph == "diff":
            phase_diff_attn(C, scr, w["c_lam"], w["diff_subln"][0], 0.8 - 0.6 * math.exp(0.0))
        elif ph == "sb":
            phase_sb_attn(C, scr)
        elif ph == "out0":
            phase_outproj(C, out, scr["mixT"], w["ab_w_out"][0])
        elif ph == "out1":
            phase_outproj(C, out, scr["mixT"], w["mla_w_o"][0])
        elif ph.startswith("xm"):
            l = int(ph[2])
            phase_xm(C, out, w["mem"], w["xm_norm"][l], w["xm_mem_norm"][l], w["xm_w_q"][l], w["xm_w_kv"][l],
                     w["xm_q_norm"][l], w["xm_k_norm"][l], w["xm_w_o"][l])
        elif ph == "mlaproj":
            phase_mla_proj(C, out, w["mix_norm"][1], w["mla_w_dq"][0], w["mla_q_norm"][0], w["mla_w_uq"][0], w["mla_w_dkv"][0],
                           w["mla_kv_norm"][0], w["mla_w_ukv"][0], w["mla_qk_norm_q"][0], w["mla_qk_norm_k"][0], w["c_cs32"], scr)
        elif ph == "mlaattn":
            phase_mla_attn(C, scr)
    P.emit(final_wait_ops=list(C.P.dma_hist[-N_DMA_SEMS:]))
    return nc


def make_consts(S):
    c = {}
    c["c_ident"] = np.eye(128, dtype=np.float32)
    kk = np.arange(128)[:, None]
    qq = np.arange(512)[None, :]
    masks = np.zeros((8, 128, 512), np.float32)
    for j in range(4):
        k = 128 * j + kk
        masks[j] = ((k // 64) <= (qq // 64)).astype(np.float32)
        masks[4 + j] = (k < qq).astype(np.float32)
    c["c_masks"] = masks
    c["c_tri"] = (np.arange(128)[:, None] >= np.arange(128)[None, :]).astype(np.float32)
    pos = np.arange(S, dtype=np.float32)[:, None]
    for d, nm in ((64, "c_cs64"), (32, "c_cs32")):
        inv = (1.0 / (np.float32(10000.0) ** (np.arange(0, d, 2, dtype=np.float32) / np.float32(d)))).astype(np.float32)
        ang = (pos * inv[None, :]).astype(np.float32)
        c[nm] = np.concatenate([np.cos(ang), np.sin(ang)], axis=1).astype(np.float32)
    return c


def prep_inputs(inputs, S):
    shared = {k: np.ascontiguousarray(np.asarray(inputs[k], dtype=np.float32)) for k in WNAMES}
    shared["c_lam"] = np.ascontiguousarray(np.concatenate([np.asarray(inputs[k], np.float32).reshape(-1) for k in
                                           ("diff_lambda_q1", "diff_lambda_k1", "diff_lambda_q2", "diff_lambda_k2")]))
    shared.update(make_consts(S))
    x = np.asarray(inputs["x"], np.float32)
    mem = np.asarray(inputs["mem"], np.float32)
    maps = []
    for b in range(x.shape[0]):
        mmap = dict(shared)
        mmap["x"] = np.ascontiguousarray(x[b])
        mmap["mem"] = np.ascontiguousarray(mem[b])
        maps.append(mmap)
    return maps


def kernel(**inputs):
    S = inputs["x"].shape[1]
    maps = prep_inputs(inputs, S)
    shapes = {k: v.shape for k, v in maps[0].items()}
    nc = build(S, shapes)
    res = run_bass_kernel_spmd(nc, maps, core_ids=list(range(len(maps))))
    return np.stack([np.asarray(r["out"], dtype=np.float32) for r in res.results], axis=0)
```

```python
import math
import numpy as np
import concourse.bass as bass
import concourse.mybir as mybir
from concourse.bass_utils import run_bass_kernel_spmd

F32 = mybir.dt.float32
BF16 = mybir.dt.bfloat16
AF = mybir.ActivationFunctionType
ALU = mybir.AluOpType
AX = mybir.AxisListType
N_DMA_SEMS = 48
EPS = 1e-6
D = 1024
DFF = 2816
NFF = 22
MEM = 256


class Tr:
    __slots__ = ("lw", "rd")

    def __init__(self):
        self.lw = None
        self.rd = []


class Op:
    __slots__ = ("eng", "fn", "deps", "needed", "cnt", "is_dma", "dslot", "dval")

    def __init__(self, eng, fn, is_dma=False):
        self.eng = eng
        self.fn = fn
        self.deps = set()
        self.needed = False
        self.cnt = 0
        self.is_dma = is_dma
        self.dslot = -1
        self.dval = 0


class Prog:
    ENGS = ("pe", "act", "dve", "pool", "sp")

    def __init__(self, nc):
        self.nc = nc
        self.streams = {e: [] for e in self.ENGS}
        self.n_dma = 0
        self.dma_hist = []
        self.since_barrier = []
        self.order = []

    def _add_deps(self, op, reads, writes):
        for t in reads:
            if t.lw is not None:
                op.deps.add(t.lw)
        for t in writes:
            if t.lw is not None:
                op.deps.add(t.lw)
            for r in t.rd:
                op.deps.add(r)
        for t in reads:
            if not op.is_dma:
                t.rd = [r for r in t.rd if r.is_dma or r.eng != op.eng]
            t.rd.append(op)
        for t in writes:
            t.lw = op
            t.rd = []
        op.deps.discard(op)

    def op(self, eng, fn, reads=(), writes=()):
        o = Op(eng, fn)
        self._add_deps(o, reads, writes)
        self.streams[eng].append(o)
        self.order.append(o)
        return o

    def dma(self, queue, out_ap, in_ap, reads=(), writes=()):
        def fn(e, out_ap=out_ap, in_ap=in_ap):
            return e.dma_start(out=out_ap, in_=in_ap)
        o = Op(queue, fn, is_dma=True)
        i = self.n_dma
        self.n_dma += 1
        o.dslot = i % N_DMA_SEMS
        o.dval = 16 * (i // N_DMA_SEMS + 1)
        if i >= N_DMA_SEMS:
            o.deps.add(self.dma_hist[i - N_DMA_SEMS])
        self.dma_hist.append(o)
        self.since_barrier.append(o)
        self._add_deps(o, reads, writes)
        self.streams[queue].append(o)
        self.order.append(o)
        return o

    def barrier(self):
        lasts = []
        for e in self.ENGS:
            for o in reversed(self.streams[e]):
                if not o.is_dma:
                    lasts.append(o)
                    break
        dmas = list(self.since_barrier)
        self.since_barrier = []
        for e in self.ENGS:
            o = Op(e, lambda eng: eng.nop())
            o.deps.update(lasts)
            o.deps.update(dmas)
            self.streams[e].append(o)
            self.order.append(o)

    def emit(self, final_wait_ops=()):
        nc = self.nc
        for e in self.ENGS:
            for o in self.streams[e]:
                for d in o.deps:
                    if not d.is_dma:
                        if d.eng == "pe" and o.eng == "pe" and not o.is_dma:
                            continue
                        d.needed = True
        for e in self.ENGS:
            c = 0
            for o in self.streams[e]:
                if (not o.is_dma) and o.needed:
                    c += 1
                    o.cnt = c
        esem = {e: nc.semaphore("s_" + e).__enter__() for e in self.ENGS}
        dsem = [nc.semaphore("d%d" % i).__enter__() for i in range(N_DMA_SEMS)]
        engobj = {"pe": nc.tensor, "act": nc.scalar, "dve": nc.vector, "pool": nc.gpsimd, "sp": nc.sync}
        seen = {e: {} for e in self.ENGS}

        def do_waits(ename, deps):
            eng = engobj[ename]
            waits = {}
            for d in deps:
                if d.is_dma:
                    key = ("d", d.dslot)
                    val = d.dval
                else:
                    if d.eng == "pe" and ename == "pe":
                        continue
                    key = ("e", d.eng)
                    val = d.cnt
                if waits.get(key, 0) < val:
                    waits[key] = val
            sn = seen[ename]
            for key, val in waits.items():
                if sn.get(key, 0) >= val:
                    continue
                sn[key] = val
                sm = dsem[key[1]] if key[0] == "d" else esem[key[1]]
                eng.wait_ge(sm, val)
        for o in self.order:
            do_waits(o.eng, o.deps)
            ins = o.fn(engobj[o.eng])
            if o.is_dma:
                ins.then_inc(dsem[o.dslot], 16)
            elif o.needed:
                ins.then_inc(esem[o.eng], 1)
        do_waits("sp", final_wait_ops)
        import os
        if os.environ.get("KDEBUG"):
            print("SEMCOUNTS", {e: max([o.cnt for o in self.streams[e]] + [0]) for e in self.ENGS}, "ndma", self.n_dma, "nops", len(self.order))


class Buf:
    def __init__(self, handle, ntr=1):
        self.h = handle
        self.tr = [Tr() for _ in range(ntr)]

    def __getitem__(self, k):
        return self.h[k]

    @property
    def t(self):
        return self.tr[0]


SB_BASE = 16640
SB_TOP = 229376


class Ctx:
    def __init__(self, nc, S):
        self.nc = nc
        self.S = S
        self.NT = S // 128
        self.NG = S // 512
        self.P = Prog(nc)
        self.off = SB_BASE
        self.uid = 0
        self.psum = [Buf(nc.alloc_psum_tensor("ps%d" % i, [128, 512], F32)) for i in range(8)]
        self.rr = 0

    def sb(self, shape, dtype, ntr=1):
        n = 1
        for s in shape[1:]:
            n *= s
        nbytes = n * (4 if dtype == F32 else 2)
        nbytes = (nbytes + 63) // 64 * 64
        assert self.off + nbytes <= SB_TOP, ("SBUF overflow", self.off, nbytes)
        self.uid += 1
        h = self.nc.alloc_sbuf_tensor_at("t%d" % self.uid, list(shape), dtype, offset=self.off)
        self.off += nbytes
        return Buf(h, ntr)

    def mark(self):
        return self.off

    def release(self, m):
        self.off = m

    def eng_rr(self, engs=("dve", "pool")):
        self.rr += 1
        return engs[self.rr % len(engs)]


def copy_op(C, eng, out_ap, in_ap, reads, writes):
    if eng == "act":
        return C.P.op("act", lambda e: e.copy(out=out_ap, in_=in_ap), reads, writes)
    return C.P.op(eng, lambda e: e.tensor_copy(out=out_ap, in_=in_ap), reads, writes)


def load_weight(C, w_dram, K, N, stg, col0=0, ncols=None, engs=("dve", "pool")):
    ncols = N if ncols is None else ncols
    kc = (K + 127) // 128
    W = C.sb([128, kc, ncols], BF16)
    CH = stg[0].h.shape[1]
    i = 0
    for c in range(kc):
        rows = min(128, K - c * 128)
        for n0 in range(0, ncols, CH):
            n1 = min(ncols, n0 + CH)
            s = stg[C.rr % len(stg)]
            C.P.dma("sp", s[0:rows, 0:n1 - n0], w_dram[c * 128:c * 128 + rows, col0 + n0:col0 + n1], writes=[s.t])
            copy_op(C, C.eng_rr(engs), W[0:rows, c, n0:n1], s[0:rows, 0:n1 - n0], [s.t, W.t], [W.t])
    return W


def load_bcast(C, vec_dram, n):
    b = C.sb([128, n], F32)
    C.P.dma("sp", b[:], vec_dram.partition_broadcast(128), writes=[b.t])
    return b


def rstd_from_ss(C, ss, rs, n, dim):
    P = C.P
    P.op("dve", lambda e: e.tensor_scalar(out=ss[:, 0:n], in0=ss[:, 0:n], scalar1=1.0 / dim, scalar2=EPS,
                                          op0=ALU.mult, op1=ALU.add), [ss.t], [ss.t])
    P.op("act", lambda e: e.activation(out=ss[:, 0:n], in_=ss[:, 0:n], func=AF.Sqrt), [ss.t], [ss.t])
    P.op("dve", lambda e: e.reciprocal(out=rs[:, 0:n], in_=ss[:, 0:n]), [ss.t], [rs.t])


def norm_rows(C, xt, gain, hb, junk, ss, rs):
    P = C.P
    P.op("act", lambda e: e.activation(out=junk[:], in_=xt[:], func=AF.Square, accum_out=ss[:, 0:1]),
         [xt.t], [junk.t, ss.t])
    rstd_from_ss(C, ss, rs, 1, D)
    P.op("dve", lambda e: e.scalar_tensor_tensor(out=hb[:], in0=xt[:], scalar=rs[:, 0:1], in1=gain[:],
                                                 op0=ALU.mult, op1=ALU.mult), [xt.t, rs.t, gain.t], [hb.t])


def transpose_cols(C, src, src_tr, ncol, dst_fn, dst_tr, banks, ceng=("act", "dve")):
    P = C.P
    for b0 in range(0, ncol, 4):
        b1 = min(ncol, b0 + 4)
        ps = C.psum[banks[(b0 // 4) % len(banks)]]
        for c in range(b0, b1):
            P.op("pe", lambda e, c=c, ps=ps, b0=b0: e.matmul(out=ps[:, (c - b0) * 128:(c - b0 + 1) * 128],
                                                           lhsT=src[:, c * 128:(c + 1) * 128], rhs=C.ident[:],
                                                           start=True, stop=True),
                 [src_tr, C.ident.t], [ps.t])
        n = b1 - b0
        copy_op(C, C.eng_rr(ceng), dst_fn(b0, b1), ps[:, 0:n * 128].rearrange("p (c t) -> p c t", c=n),
                [ps.t] + list(dst_tr), list(dst_tr))


def setup_consts(C, ident_d, masks_d, tri_d):
    P = C.P
    C.ident = C.sb([128, 128], BF16)
    C.ones = C.sb([128, 128], BF16)
    C.masks = C.sb([128, 8, 512], BF16)
    C.trineg = C.sb([128, 128], BF16)
    C.onesneg = C.sb([128, 128], BF16)
    m = C.mark()
    idf = C.sb([128, 128], F32)
    trf = C.sb([128, 128], F32)
    mf = C.sb([128, 8, 512], F32)
    P.dma("sp", idf[:], ident_d, writes=[idf.t])
    P.dma("sp", trf[:], tri_d, writes=[trf.t])
    P.dma("sp", mf[:], masks_d.rearrange("m p q -> p m q"), writes=[mf.t])
    copy_op(C, "dve", C.ident[:], idf[:], [idf.t], [C.ident.t])
    P.op("dve", lambda e: e.tensor_scalar(out=C.trineg[:], in0=trf[:], scalar1=-1.0, scalar2=None, op0=ALU.mult),
         [trf.t], [C.trineg.t])
    copy_op(C, "pool", C.masks[:], mf[:], [mf.t], [C.masks.t])
    P.op("pool", lambda e: e.memset(C.ones[:], 1.0), [], [C.ones.t])
    P.op("pool", lambda e: e.memset(C.onesneg[:], -1.0), [], [C.onesneg.t])
    P.barrier()
    C.release(m)


def phase_ffn(C, x_d, gain_d, wg_d, wu_d, wd_d):
    P = C.P
    m = C.mark()
    stg = [C.sb([128, 1024], F32), C.sb([128, 1024], F32)]
    Wg = load_weight(C, wg_d, D, DFF, stg)
    Wu = load_weight(C, wu_d, D, DFF, stg)
    Wd = load_weight(C, wd_d, DFF, D, stg)
    gain = load_bcast(C, gain_d, D)
    xs = [C.sb([128, D], F32) for _ in range(4)]
    hb = C.sb([128, D], BF16)
    junk = hb
    ss = C.sb([128, 8], F32)
    rs = C.sb([128, 8], F32)
    hT = C.sb([128, 8, 512], BF16)
    actT = C.sb([128, NFF, 512], BF16, ntr=NFF)
    sg = [C.sb([128, 512], F32), C.sb([128, 512], F32)]
    for g in range(C.NG):
        for t in range(4):
            r0 = g * 512 + t * 128
            P.dma("sp", xs[t][:], x_d[r0:r0 + 128, :], writes=[xs[t].t])
            norm_rows(C, xs[t], gain, hb, junk, ss, rs)
            transpose_cols(C, hb, hb.t, 8, lambda c0, c1, t=t: hT[:, c0:c1, t * 128:(t + 1) * 128], [hT.t], [0, 1])
        for f in range(NFF):
            pg = C.psum[2 + (f % 2)]
            pu = C.psum[4 + (f % 2)]
            for c in range(8):
                P.op("pe", lambda e, c=c, f=f, pg=pg: e.matmul(out=pg[:], lhsT=Wg[:, c, f * 128:(f + 1) * 128],
                                                             rhs=hT[:, c, :], start=(c == 0), stop=(c == 7)),
                     [Wg.t, hT.t], [pg.t])
            for c in range(8):
                P.op("pe", lambda e, c=c, f=f, pu=pu: e.matmul(out=pu[:], lhsT=Wu[:, c, f * 128:(f + 1) * 128],
                                                             rhs=hT[:, c, :], start=(c == 0), stop=(c == 7)),
                     [Wu.t, hT.t], [pu.t])
            s = sg[f % 2]
            P.op("act", lambda e, s=s, pg=pg: e.activation(out=s[:], in_=pg[:], func=AF.Silu), [pg.t], [s.t])
            P.op("dve", lambda e, s=s, pu=pu, f=f: e.tensor_tensor(out=actT[:, f, :], in0=pu[:], in1=s[:], op=ALU.mult),
                 [pu.t, s.t], [actT.tr[f]])
        for t in range(4):
            for h in range(2):
                py = C.psum[6 + h]
                for f in range(NFF):
                    P.op("pe", lambda e, f=f, t=t, h=h, py=py: e.matmul(out=py[:], lhsT=actT[:, f, t * 128:(t + 1) * 128],
                                                                        rhs=Wd[:, f, h * 512:(h + 1) * 512],
                                                                        start=(f == 0), stop=(f == NFF - 1)),
                         [actT.tr[f], Wd.t], [py.t])
                P.op("dve", lambda e, t=t, h=h, py=py: e.scalar_tensor_tensor(
                    out=xs[t][:, h * 512:(h + 1) * 512], in0=py[:], scalar=0.5, in1=xs[t][:, h * 512:(h + 1) * 512],
                    op0=ALU.mult, op1=ALU.add), [py.t, xs[t].t], [xs[t].t])
            r0 = g * 512 + t * 128
            P.dma("sp", x_d[r0:r0 + 128, :], xs[t][:], reads=[xs[t].t])
    P.barrier()
    C.release(m)


def headnorm_rope(C, raw, nh, dh, gain_b, cs_ap, off, half, outb, w1, w2, ss, rs, cs_tr=None):
    P = C.P
    n = nh * dh
    r3 = raw[:, 0:n].rearrange("p (h d) -> p h d", h=nh)
    o3 = outb[:, 0:n].rearrange("p (h d) -> p h d", h=nh)
    a3 = w1[:, 0:n].rearrange("p (h d) -> p h d", h=nh)
    b3 = w2[:, 0:n].rearrange("p (h d) -> p h d", h=nh)
    P.op("act", lambda e: e.activation(out=w1[:, 0:n], in_=raw[:, 0:n], func=AF.Square), [raw.t], [w1.t])
    P.op("dve", lambda e: e.tensor_reduce(out=ss[:, 0:nh], in_=a3, axis=AX.X, op=ALU.add), [w1.t], [ss.t])
    rstd_from_ss(C, ss, rs, nh, dh)
    P.op("dve", lambda e: e.tensor_tensor(out=r3, in0=r3, in1=rs[:, 0:nh].unsqueeze(2).to_broadcast([128, nh, dh]),
                                          op=ALU.mult), [raw.t, rs.t], [raw.t])
    P.op("pool", lambda e: e.tensor_tensor(out=r3, in0=r3, in1=gain_b[:, 0:dh].unsqueeze(1).to_broadcast([128, nh, dh]),
                                           op=ALU.mult), [raw.t, gain_b.t], [raw.t])
    if cs_ap is None:
        copy_op(C, "act", outb[:, 0:n], raw[:, 0:n], [raw.t], [outb.t])
        return
    cos = cs_ap[:, 0:half].unsqueeze(1).to_broadcast([128, nh, half])
    sin = cs_ap[:, half:2 * half].unsqueeze(1).to_broadcast([128, nh, half])
    x1 = r3[:, :, off:off + half]
    x2 = r3[:, :, off + half:off + 2 * half]
    if off > 0:
        copy_op(C, "act", o3[:, :, 0:off], r3[:, :, 0:off], [raw.t, outb.t], [outb.t])
    P.op("pool", lambda e: e.tensor_tensor(out=a3[:, :, 0:half], in0=x1, in1=cos, op=ALU.mult), [raw.t, w1.t, cs_tr], [w1.t])
    P.op("dve", lambda e: e.tensor_tensor(out=b3[:, :, 0:half], in0=x2, in1=sin, op=ALU.mult), [raw.t, w2.t, cs_tr], [w2.t])
    P.op("pool", lambda e: e.tensor_tensor(out=a3[:, :, half:2 * half], in0=x1, in1=sin, op=ALU.mult), [raw.t, w1.t, cs_tr], [w1.t])
    P.op("dve", lambda e: e.tensor_tensor(out=b3[:, :, half:2 * half], in0=x2, in1=cos, op=ALU.mult), [raw.t, w2.t, cs_tr], [w2.t])
    P.op("dve", lambda e: e.tensor_tensor(out=o3[:, :, off:off + half], in0=a3[:, :, 0:half], in1=b3[:, :, 0:half],
                                          op=ALU.subtract), [w1.t, w2.t, outb.t], [outb.t])
    P.op("pool", lambda e: e.tensor_tensor(out=o3[:, :, off + half:off + 2 * half], in0=a3[:, :, half:2 * half],
                                           in1=b3[:, :, half:2 * half], op=ALU.add), [w1.t, w2.t, outb.t], [outb.t])


def store_T(C, outb, ncols, dstT, t, tbuf):
    import os
    nchunk = ncols // 128
    transpose_cols(C, outb, outb.t, nchunk, lambda c0, c1: tbuf[:, c0:c1, :], [tbuf.t], [0, 1])
    if "s" in os.environ.get("MSKIP", ""):
        return
    for c0 in range(0, nchunk, 4):
        C.P.dma("sp", dstT.rearrange("(c p) s -> p c s", p=128)[:, c0:c0 + 4, t * 128:(t + 1) * 128],
                tbuf[:, c0:c0 + 4, :], reads=[tbuf.t])


def proj(C, hT, kc, W, c0, n, bank, tok0=0, ntok=128):
    ps = C.psum[bank]
    for c in range(kc):
        C.P.op("pe", lambda e, c=c: e.matmul(out=ps[0:ntok, 0:n], lhsT=hT[:, c, tok0:tok0 + ntok], rhs=W[:, c, c0:c0 + n],
                                             start=(c == 0), stop=(c == kc - 1)), [hT.t, W.t], [ps.t])
    return ps


def scaled_gain(C, vec_d, n, scale):
    g = load_bcast(C, vec_d, n)
    if scale != 1.0:
        C.P.op("dve", lambda e: e.tensor_scalar(out=g[:], in0=g[:], scalar1=float(scale), scalar2=None, op0=ALU.mult),
               [g.t], [g.t])
    return g


def phase_ab_proj(C, x_d, gain_d, w_in_d, qn_d, kn_d, cs_d, scr):
    P = C.P
    S = C.S
    m = C.mark()
    stg = [C.sb([128, 1024], F32), C.sb([128, 1024], F32)]
    W = load_weight(C, w_in_d, D, 3072, stg)
    gain = load_bcast(C, gain_d, D)
    gq = scaled_gain(C, qn_d, 64, 0.125)
    gk = scaled_gain(C, kn_d, 64, 1.0)
    cs = C.sb([128, C.NT, 64], F32)
    P.dma("sp", cs[:], cs_d.rearrange("(t p) c -> p t c", p=128), writes=[cs.t])
    xt = [C.sb([128, D], F32) for _ in range(2)]
    hb = C.sb([128, D], BF16)
    ss = C.sb([128, 16], F32)
    rs = C.sb([128, 16], F32)
    hT = C.sb([128, 8, 128], BF16)
    raw = C.sb([128, 512], F32)
    w1 = C.sb([128, 512], F32)
    w2 = C.sb([128, 512], F32)
    outb = [C.sb([128, 512], BF16) for _ in range(2)]
    tbuf = [C.sb([128, 4, 128], BF16) for _ in range(2)]
    vb = [C.sb([128, 512], BF16) for _ in range(2)]
    qaT, kaT, va, qbT, kbT, vbd = scr["qaT"], scr["kaT"], scr["va"], scr["qbT"], scr["kbT"], scr["vb"]
    k = 0
    for t in range(C.NT):
        x = xt[t % 2]
        P.dma("sp", x[:], x_d[t * 128:(t + 1) * 128, :], writes=[x.t])
        norm_rows(C, x, gain, hb, hb, ss, rs)
        transpose_cols(C, hb, hb.t, 8, lambda c0, c1: hT[:, c0:c1, :], [hT.t], [0, 1])
        for grp in range(6):
            ps = proj(C, hT, 8, W, grp * 512, 512, 2 + (k % 6))
            k += 1
            if grp in (0, 1):
                copy_op(C, "act", raw[:], ps[:], [ps.t], [raw.t])
                ob = outb[grp]
                headnorm_rope(C, raw, 8, 64, gq if grp == 0 else gk, cs[:, t, :], 0, 32, ob, w1, w2, ss, rs, cs.t)
                store_T(C, ob, 512, qaT if grp == 0 else kaT, t, tbuf[grp])
            elif grp in (2, 5):
                v = vb[0 if grp == 2 else 1]
                copy_op(C, "act", v[:], ps[:], [ps.t, v.t], [v.t])
                P.dma("sp", (va if grp == 2 else vbd)[t * 128:(t + 1) * 128, :], v[:], reads=[v.t])
            else:
                ob = outb[grp - 3]
                if grp == 3:
                    P.op("act", lambda e, ob=ob, ps=ps: e.activation(out=ob[:], in_=ps[:], func=AF.Copy, scale=0.125),
                         [ps.t, ob.t], [ob.t])
                else:
                    copy_op(C, "dve", ob[:], ps[:], [ps.t, ob.t], [ob.t])
                store_T(C, ob, 512, qbT if grp == 3 else kbT, t, tbuf[grp - 3])
    P.barrier()
    C.release(m)


def load_head(C, qT_d, kT_d, dk, QT, KT):
    C.P.dma("sp", QT[0:dk, :], qT_d, reads=[QT.t], writes=[QT.t])
    C.P.dma("sp", KT[0:dk, :], kT_d, reads=[KT.t], writes=[KT.t])


def softmax_block(C, QT, KT, dk, Vl, qs, mask0, pbufs, bank_o, bank_d, k_ctr, sbanks=(0, 1)):
    P = C.P
    nj = 4 * qs + 4
    po = C.psum[bank_o]
    pd = C.psum[bank_d] if bank_d is not None else None
    st = {}

    def stage_a(j):
        ps = C.psum[sbanks[k_ctr[0] % len(sbanks)]]
        pt = pbufs[k_ctr[0] % len(pbufs)]
        k_ctr[0] += 1
        st[j] = (ps, pt)
        P.op("pe", lambda e, j=j, ps=ps: e.matmul(out=ps[:], lhsT=KT[0:dk, j * 128:(j + 1) * 128],
                                                 rhs=QT[0:dk, qs * 512:(qs + 1) * 512], start=True, stop=True),
             [KT.t, QT.t], [ps.t])

    def stage_b(j):
        ps, pt = st.pop(j)
        P.op("act", lambda e, ps=ps, pt=pt: e.activation(out=pt[:], in_=ps[:], func=AF.Exp), [ps.t], [pt.t])
        if j >= 4 * qs:
            mi = mask0 + j - 4 * qs
            P.op("pool", lambda e, pt=pt, mi=mi: e.tensor_tensor(out=pt[:], in0=pt[:], in1=C.masks[:, mi, :], op=ALU.mult),
                 [pt.t, C.masks.t], [pt.t])
        P.op("pe", lambda e, j=j, pt=pt: e.matmul(out=po[:], lhsT=Vl(j), rhs=pt[:], start=(j == 0), stop=(j == nj - 1)),
             [pt.t, Vl.tr], [po.t])
        if pd is not None:
            P.op("pe", lambda e, j=j, pt=pt: e.matmul(out=pd[:], lhsT=C.ones[:], rhs=pt[:], start=(j == 0),
                                                     stop=(j == nj - 1)), [pt.t, C.ones.t], [pd.t])
    la = min(2, len(sbanks) - 1)
    for j in range(min(la, nj)):
        stage_a(j)
    for j in range(nj):
        if j + la < nj:
            stage_a(j + la)
        stage_b(j)


class VL:
    def __init__(self, fn, tr):
        self.fn = fn
        self.tr = tr

    def __call__(self, j):
        return self.fn(j)


def phase_diff_attn(C, scr, lam_d, subln_d, lambda_init):
    P = C.P
    S = C.S
    m = C.mark()
    QT = [C.sb([128, S], BF16) for _ in range(2)]
    KT = [C.sb([128, S], BF16) for _ in range(2)]
    V = [C.sb([128, C.NT, 128], BF16) for _ in range(2)]
    pb = [C.sb([128, 512], BF16) for _ in range(4)]
    A = C.sb([128, 512], F32)
    B = C.sb([128, 512], F32)
    R = C.sb([128, 512], F32)
    ob = [C.sb([128, 512], BF16) for _ in range(2)]
    Bh = C.sb([128, 512], BF16)
    Bl = C.sb([128, 512], BF16)
    lv = C.sb([128, 4, 64], F32)
    P.dma("sp", lv[:].rearrange("p a b -> p (a b)"), lam_d.partition_broadcast(128), writes=[lv.t])
    lt = C.sb([128, 2, 64], F32)
    ls = C.sb([128, 4], F32)
    P.op("dve", lambda e: e.tensor_tensor(out=lt[:], in0=lv[:, 0:4:2, :], in1=lv[:, 1:4:2, :], op=ALU.mult), [lv.t], [lt.t])
    P.op("dve", lambda e: e.tensor_reduce(out=ls[:, 0:2], in_=lt[:], axis=AX.X, op=ALU.add), [lt.t], [ls.t])
    P.op("act", lambda e: e.activation(out=ls[:, 0:2], in_=ls[:, 0:2], func=AF.Exp), [ls.t], [ls.t])
    P.op("dve", lambda e: e.tensor_tensor(out=ls[:, 2:3], in0=ls[:, 1:2], in1=ls[:, 0:1], op=ALU.subtract), [ls.t], [ls.t])
    P.op("dve", lambda e: e.tensor_scalar(out=ls[:, 3:4], in0=ls[:, 2:3], scalar1=-float(lambda_init), scalar2=None,
                                          op0=ALU.add), [ls.t], [ls.t])
    sub = C.sb([128, 1], F32)
    P.dma("sp", sub[:], subln_d.rearrange("(p o) -> p o", o=1), writes=[sub.t])
    P.op("dve", lambda e: e.tensor_scalar(out=sub[:], in0=sub[:], scalar1=float(1.0 - lambda_init), scalar2=None,
                                          op0=ALU.mult), [sub.t], [sub.t])
    kc = [0]
    for h in range(4):
        Vh = V[h % 2]
        for t0 in range(0, C.NT, 8):
            P.dma("sp", Vh[:, t0:min(t0 + 8, C.NT), :], scr["va"].rearrange("(t p) c -> p t c", p=128)[:, t0:min(t0 + 8, C.NT), h * 128:(h + 1) * 128],
                  reads=[Vh.t], writes=[Vh.t])
        for mp in range(2):
            hh = h + 4 * mp
            load_head(C, scr["qaT"][hh * 64:(hh + 1) * 64, :], scr["kaT"][hh * 64:(hh + 1) * 64, :], 64, QT[mp], KT[mp])
        for qs in range(C.NG):
            for mp in range(2):
                softmax_block(C, QT[mp], KT[mp], 64, VL(lambda j, Vh=Vh: Vh[:, j, :], Vh.t), qs, 0, pb, 2 + 2 * mp, 3 + 2 * mp, kc, sbanks=(0, 1, 7))
                po, pd = C.psum[2 + 2 * mp], C.psum[3 + 2 * mp]
                P.op("dve", lambda e, pd=pd: e.reciprocal(out=R[:], in_=pd[:]), [pd.t], [R.t])
                dst = A if mp == 0 else B
                P.op("dve", lambda e, po=po, dst=dst: e.tensor_tensor(out=dst[:], in0=po[:], in1=R[:], op=ALU.mult),
                     [po.t, R.t], [dst.t])
            P.op("dve", lambda e: e.scalar_tensor_tensor(out=A[:], in0=B[:], scalar=ls[:, 3:4], in1=A[:], op0=ALU.mult,
                                                         op1=ALU.add), [A.t, B.t, ls.t], [A.t])
            P.op("act", lambda e: e.activation(out=B[:], in_=A[:], func=AF.Square), [A.t], [B.t])
            pq = C.psum[6]
            P.op("dve", lambda e: e.tensor_copy(out=Bh[:], in_=B[:]), [B.t, Bh.t], [Bh.t])
            P.op("dve", lambda e: e.tensor_tensor(out=Bl[:], in0=B[:], in1=Bh[:], op=ALU.subtract), [B.t, Bh.t, Bl.t], [Bl.t])
            P.op("pe", lambda e, pq=pq: e.matmul(out=pq[:], lhsT=C.ones[:], rhs=Bh[:], start=True, stop=False),
                 [C.ones.t, Bh.t], [pq.t])
            P.op("pe", lambda e, pq=pq: e.matmul(out=pq[:], lhsT=C.ones[:], rhs=Bl[:], start=False, stop=True),
                 [C.ones.t, Bl.t], [pq.t])
            P.op("dve", lambda e, pq=pq: e.tensor_scalar(out=R[:], in0=pq[:], scalar1=1.0 / 128, scalar2=EPS, op0=ALU.mult,
                                                         op1=ALU.add), [pq.t], [R.t])
            P.op("act", lambda e: e.activation(out=R[:], in_=R[:], func=AF.Sqrt), [R.t], [R.t])
            P.op("dve", lambda e: e.reciprocal(out=R[:], in_=R[:]), [R.t], [R.t])
            o = ob[qs % 2]
            P.op("dve", lambda e, o=o: e.scalar_tensor_tensor(out=o[:], in0=A[:], scalar=sub[:, 0:1], in1=R[:], op0=ALU.mult,
                                                              op1=ALU.mult), [A.t, R.t, sub.t, o.t], [o.t])
            P.dma("sp", scr["mixT"][h * 128:(h + 1) * 128, qs * 512:(qs + 1) * 512], o[:], reads=[o.t])
    P.barrier()
    C.release(m)


def phase_sb_attn(C, scr):
    P = C.P
    S = C.S
    m = C.mark()
    QT = [C.sb([128, S], BF16) for _ in range(2)]
    KT = [C.sb([128, S], BF16) for _ in range(2)]
    V = [C.sb([128, C.NT, 128], BF16) for _ in range(2)]
    for b_ in QT + KT:
        P.op("pool", lambda e, b_=b_: e.memset(b_[64:128, :], 0.0), [], [b_.t])
    for v in V:
        P.op("pool", lambda e, v=v: e.memset(v[:, :, 64:128], 0.0), [], [v.t])
    ef = [C.sb([128, 512], F32) for _ in range(3)]
    sp = [C.sb([128, 512], BF16) for _ in range(3)]
    wb = [C.sb([128, 512], BF16) for _ in range(3)]
    Ls = C.sb([128, 512], F32)
    Lb = [C.sb([128, 512], BF16) for _ in range(4)]
    ob = [C.sb([128, 512], BF16) for _ in range(2)]
    k = 0
    for h in range(8):
        Vh, Q, K = V[h % 2], QT[h % 2], KT[h % 2]
        for t0 in range(0, C.NT, 8):
            P.dma("sp", Vh[:, t0:min(t0 + 8, C.NT), 0:64], scr["vb"].rearrange("(t p) c -> p t c", p=128)[:, t0:min(t0 + 8, C.NT), h * 64:(h + 1) * 64],
                  reads=[Vh.t], writes=[Vh.t])
        load_head(C, scr["qbT"][h * 64:(h + 1) * 64, :], scr["kbT"][h * 64:(h + 1) * 64, :], 64, Q, K)
        for qs in range(C.NG):
            nj = 4 * qs + 4
            po = C.psum[6 + (qs % 2)]
            js = list(range(nj - 1, -1, -1))
            st = {}

            def stage_a(idx, js=js, qs=qs, Q=Q, K=K, st=st):
                nonlocal k
                j = js[idx]
                pz = C.psum[k % 3]
                pc = C.psum[3 + (k % 3)]
                e_, s_, w_ = ef[k % 3], sp[k % 3], wb[k % 3]
                lb_in = Lb[idx % 4]
                lb_out = Lb[(idx + 1) % 4]
                k += 1
                st[idx] = (j, pc, s_, w_, lb_in)
                diag = j >= 4 * qs
                mi = 4 + j - 4 * qs
                for pp in (pz, pc):
                    P.op("pe", lambda e, j=j, pp=pp, last=(pp is pz): e.matmul(
                        out=pp[:], lhsT=K[:, j * 128:(j + 1) * 128], rhs=Q[:, qs * 512:(qs + 1) * 512],
                        start=True, stop=last), [K.t, Q.t], [pp.t])
                P.op("act", lambda e, pz=pz, e_=e_: e.activation(out=e_[:], in_=pz[:], func=AF.Exp), [pz.t], [e_.t])
                P.op("act", lambda e, e_=e_, s_=s_: e.activation(out=s_[:], in_=e_[:], func=AF.Ln, bias=1.0, scale=1.0),
                     [e_.t], [s_.t])
                if diag:
                    P.op("pool", lambda e, s_=s_, mi=mi: e.tensor_tensor(out=s_[:], in0=s_[:], in1=C.masks[:, mi, :],
                                                                        op=ALU.mult), [s_.t, C.masks.t], [s_.t])
                if idx + 1 < len(js):
                    if idx == 0:
                        copy_op(C, "dve", Ls[:], s_[:], [s_.t, Ls.t], [Ls.t])
                    else:
                        P.op("dve", lambda e, s_=s_: e.tensor_tensor(out=Ls[:], in0=Ls[:], in1=s_[:], op=ALU.add),
                             [Ls.t, s_.t], [Ls.t])
                    copy_op(C, "dve", lb_out[:], Ls[:], [Ls.t, lb_out.t], [lb_out.t])

            def stage_b(idx, js=js, qs=qs, st=st, po=po, Vh=Vh):
                j, pc, s_, w_, lb = st.pop(idx)
                first = idx == 0
                diag = j >= 4 * qs
                mi = 4 + j - 4 * qs
                P.op("pe", lambda e, pc=pc, s_=s_, first=first: e.matmul(out=pc[:], lhsT=C.trineg[:], rhs=s_[:], start=False,
                                                                        stop=first), [C.trineg.t, s_.t], [pc.t])
                if not first:
                    P.op("pe", lambda e, pc=pc, lb=lb: e.matmul(out=pc[:], lhsT=C.onesneg[:], rhs=lb[:], start=False, stop=True),
                         [C.onesneg.t, lb.t], [pc.t])
                P.op("act", lambda e, pc=pc, w_=w_: e.activation(out=w_[:], in_=pc[:], func=AF.Exp), [pc.t], [w_.t])
                if diag:
                    P.op("pool", lambda e, w_=w_, mi=mi: e.tensor_tensor(out=w_[:], in0=w_[:], in1=C.masks[:, mi, :],
                                                                        op=ALU.mult), [w_.t, C.masks.t], [w_.t])
                P.op("pe", lambda e, j=j, w_=w_, first=first, po=po, Vh=Vh: e.matmul(out=po[:], lhsT=Vh[:, j, :], rhs=w_[:],
                                                                                    start=first, stop=(j == 0)), [Vh.t, w_.t], [po.t])
            stage_a(0)
            stage_a(1)
            for idx in range(nj):
                if idx + 2 < nj:
                    stage_a(idx + 2)
                stage_b(idx)
            o = ob[qs % 2]
            copy_op(C, "dve", o[0:64, :], po[0:64, :], [po.t, o.t], [o.t])
            P.dma("sp", scr["mixT"][512 + h * 64:512 + (h + 1) * 64, qs * 512:(qs + 1) * 512], o[0:64, :], reads=[o.t])
    P.barrier()
    C.release(m)


def phase_outproj(C, x_d, mixT_d, w_d):
    P = C.P
    m = C.mark()
    stg = [C.sb([128, 1024], F32), C.sb([128, 1024], F32)]
    W = load_weight(C, w_d, D, D, stg)
    mT = [C.sb([128, 8, 512], BF16) for _ in range(2)]
    xs = [C.sb([128, D], F32) for _ in range(3)]
    k = 0
    for g in range(C.NG):
        mt = mT[g % 2]
        P.dma("sp", mt[:], mixT_d.rearrange("(c p) s -> p c s", p=128)[:, :, g * 512:(g + 1) * 512], writes=[mt.t])
        for t in range(4):
            r0 = g * 512 + t * 128
            x = xs[k % 3]
            k += 1
            P.dma("sp", x[:], x_d[r0:r0 + 128, :], writes=[x.t])
            for h in range(2):
                py = C.psum[2 * (k % 2) + h]
                for c in range(8):
                    P.op("pe", lambda e, c=c, t=t, h=h, py=py, mt=mt: e.matmul(
                        out=py[:], lhsT=mt[:, c, t * 128:(t + 1) * 128], rhs=W[:, c, h * 512:(h + 1) * 512],
                        start=(c == 0), stop=(c == 7)), [mt.t, W.t], [py.t])
                P.op("dve", lambda e, h=h, py=py, x=x: e.tensor_tensor(out=x[:, h * 512:(h + 1) * 512], in0=py[:],
                                                                     in1=x[:, h * 512:(h + 1) * 512], op=ALU.add),
                     [py.t, x.t], [x.t])
            P.dma("sp", x_d[r0:r0 + 128, :], x[:], reads=[x.t])
    P.barrier()
    C.release(m)


def phase_xm(C, x_d, mem_d, xn_d, mn_d, wq_d, wkv_d, qn_d, kn_d, wo_d):
    P = C.P
    m = C.mark()
    stg = [C.sb([128, 1024], F32), C.sb([128, 1024], F32)]
    Wkv = load_weight(C, wkv_d, D, D, stg)
    gm = load_bcast(C, mn_d, D)
    gk = scaled_gain(C, kn_d, 128, 1.0)
    gq = scaled_gain(C, qn_d, 128, 128 ** -0.5)
    hb = C.sb([128, D], BF16)
    ss = C.sb([128, 8], F32)
    rs = C.sb([128, 8], F32)
    hT = C.sb([128, 8, 128], BF16)
    raw = C.sb([128, 512], F32)
    w1 = C.sb([128, 512], F32)
    outb = C.sb([128, 512], BF16)
    KT = C.sb([128, 4, MEM], BF16)
    Vm = C.sb([128, 2, 512], BF16)
    xs = [C.sb([128, D], F32) for _ in range(4)]
    for t in range(2):
        x = xs[t]
        P.dma("sp", x[:], mem_d[t * 128:(t + 1) * 128, :], writes=[x.t])
        norm_rows(C, x, gm, hb, hb, ss, rs)
        transpose_cols(C, hb, hb.t, 8, lambda c0, c1: hT[:, c0:c1, :], [hT.t], [0, 1])
        ps = proj(C, hT, 8, Wkv, 0, 512, 2)
        copy_op(C, "act", raw[:], ps[:], [ps.t], [raw.t])
        headnorm_rope(C, raw, 4, 128, gk, None, 0, 0, outb, w1, w1, ss, rs)
        transpose_cols(C, outb, outb.t, 4, lambda c0, c1, t=t: KT[:, c0:c1, t * 128:(t + 1) * 128], [KT.t], [0, 1])
        ps = proj(C, hT, 8, Wkv, 512, 512, 3)
        copy_op(C, "act", Vm[:, t, :], ps[:], [ps.t, Vm.t], [Vm.t])
    Wq = load_weight(C, wq_d, D, 512, stg)
    Wo = load_weight(C, wo_d, 512, D, stg)
    gx = load_bcast(C, xn_d, D)
    hT4 = C.sb([128, 8, 512], BF16)
    qT = C.sb([128, 4, 512], BF16)
    xoT = C.sb([128, 4, 512], BF16, ntr=4)
    pb = [C.sb([128, 512], BF16) for _ in range(3)]
    R = C.sb([128, 512], F32)
    kk = 0
    for g in range(C.NG):
        for t in range(4):
            r0 = g * 512 + t * 128
            x = xs[t]
            P.dma("sp", x[:], x_d[r0:r0 + 128, :], writes=[x.t])
            norm_rows(C, x, gx, hb, hb, ss, rs)
            transpose_cols(C, hb, hb.t, 8, lambda c0, c1, t=t: hT4[:, c0:c1, t * 128:(t + 1) * 128], [hT4.t], [0, 1])
            ps = proj(C, hT4, 8, Wq, 0, 512, 2 + (t % 2), tok0=t * 128)
            copy_op(C, "act", raw[:], ps[:], [ps.t], [raw.t])
            headnorm_rope(C, raw, 4, 128, gq, None, 0, 0, outb, w1, w1, ss, rs)
            transpose_cols(C, outb, outb.t, 4, lambda c0, c1, t=t: qT[:, c0:c1, t * 128:(t + 1) * 128], [qT.t], [0, 1])
        for h in range(4):
            po, pd = C.psum[4], C.psum[5]
            for mt in range(2):
                ps = C.psum[2 + (kk % 2)]
                pt = pb[kk % 3]
                kk += 1
                P.op("pe", lambda e, h=h, mt=mt, ps=ps: e.matmul(out=ps[:], lhsT=KT[:, h, mt * 128:(mt + 1) * 128],
                                                               rhs=qT[:, h, :], start=True, stop=True), [KT.t, qT.t], [ps.t])
                P.op("act", lambda e, ps=ps, pt=pt: e.activation(out=pt[:], in_=ps[:], func=AF.Exp), [ps.t], [pt.t])
                P.op("pe", lambda e, h=h, mt=mt, pt=pt: e.matmul(out=po[:], lhsT=Vm[:, mt, h * 128:(h + 1) * 128], rhs=pt[:],
                                                               start=(mt == 0), stop=(mt == 1)), [Vm.t, pt.t], [po.t])
                P.op("pe", lambda e, mt=mt, pt=pt: e.matmul(out=pd[:], lhsT=C.ones[:], rhs=pt[:], start=(mt == 0),
                                                          stop=(mt == 1)), [C.ones.t, pt.t], [pd.t])
            P.op("dve", lambda e, pd=pd: e.reciprocal(out=R[:], in_=pd[:]), [pd.t], [R.t])
            P.op("dve", lambda e, po=po, h=h: e.tensor_tensor(out=xoT[:, h, :], in0=po[:], in1=R[:], op=ALU.mult),
                 [po.t, R.t], [xoT.tr[h]])
        for t in range(4):
            r0 = g * 512 + t * 128
            x = xs[t]
            for hf in range(2):
                py = C.psum[6 + hf]
                for h in range(4):
                    P.op("pe", lambda e, h=h, t=t, hf=hf, py=py: e.matmul(
                        out=py[:], lhsT=xoT[:, h, t * 128:(t + 1) * 128], rhs=Wo[:, h, hf * 512:(hf + 1) * 512],
                        start=(h == 0), stop=(h == 3)), [xoT.tr[h], Wo.t], [py.t])
                P.op("dve", lambda e, hf=hf, py=py, x=x: e.tensor_tensor(out=x[:, hf * 512:(hf + 1) * 512], in0=py[:],
                                                                       in1=x[:, hf * 512:(hf + 1) * 512], op=ALU.add),
                     [py.t, x.t], [x.t])
            P.dma("sp", x_d[r0:r0 + 128, :], x[:], reads=[x.t])
    P.barrier()
    C.release(m)


def phase_mla_proj(C, x_d, gain_d, wdq_d, qln_d, wuq_d, wdkv_d, kvln_d, wukv_d, nq_d, nk_d, cs_d, scr):
    P = C.P
    m = C.mark()
    stg = [C.sb([128, 1024], F32), C.sb([128, 1024], F32)]
    Wdq = load_weight(C, wdq_d, D, 512, stg)
    Wuq = load_weight(C, wuq_d, 512, 1536, stg)
    Wdkv = load_weight(C, wdkv_d, D, 288, stg)
    Wukv = load_weight(C, wukv_d, 256, 2048, stg)
    gain = load_bcast(C, gain_d, D)
    gql = load_bcast(C, qln_d, 512)
    gkvl = load_bcast(C, kvln_d, 256)
    gq = scaled_gain(C, nq_d, 96, 96 ** -0.5)
    gk = scaled_gain(C, nk_d, 96, 1.0)
    import os
    MS = os.environ.get("MSKIP", "")
    cs = C.sb([128, C.NT, 32], F32)
    if "c" not in MS:
        P.dma("sp", cs[:], cs_d.rearrange("(t p) c -> p t c", p=128), writes=[cs.t])
    xt = [C.sb([128, D], F32) for _ in range(2)]
    hb = C.sb([128, D], BF16)
    ss = C.sb([128, 16], F32)
    rs = C.sb([128, 16], F32)
    hT = C.sb([128, 8, 128], BF16)
    cq = C.sb([128, 512], F32)
    cqb = C.sb([128, 512], BF16)
    cqT = C.sb([128, 4, 128], BF16)
    dkv = C.sb([128, 288], F32)
    ckb = C.sb([128, 256], BF16)
    ckT = C.sb([128, 2, 128], BF16)
    raw = C.sb([128, 1536], F32)
    w1 = C.sb([128, 1536], F32)
    w2 = C.sb([128, 1536], F32)
    outb = [C.sb([128, 1536], BF16) for _ in range(2)]
    tbuf = [C.sb([128, 12, 128], BF16) for _ in range(2)]
    vb = C.sb([128, 16, 64], BF16)
    kvs = [C.sb([128, 512], F32) for _ in range(2)]
    qT_d, kT_d, v_d = scr["mqT"], scr["mkT"], scr["mv"]
    for t in range(C.NT):
        x = xt[t % 2]
        P.dma("sp", x[:], x_d[t * 128:(t + 1) * 128, :], writes=[x.t])
        norm_rows(C, x, gain, hb, hb, ss, rs)
        transpose_cols(C, hb, hb.t, 8, lambda c0, c1: hT[:, c0:c1, :], [hT.t], [0, 1])
        ps = proj(C, hT, 8, Wdq, 0, 512, 2)
        copy_op(C, "act", cq[:], ps[:], [ps.t], [cq.t])
        P.op("act", lambda e: e.activation(out=w1[:, 0:512], in_=cq[:], func=AF.Square, accum_out=ss[:, 0:1]),
             [cq.t], [w1.t, ss.t])
        rstd_from_ss(C, ss, rs, 1, 512)
        P.op("dve", lambda e: e.scalar_tensor_tensor(out=cqb[:], in0=cq[:], scalar=rs[:, 0:1], in1=gql[:], op0=ALU.mult,
                                                     op1=ALU.mult), [cq.t, rs.t, gql.t], [cqb.t])
        transpose_cols(C, cqb, cqb.t, 4, lambda c0, c1: cqT[:, c0:c1, :], [cqT.t], [0, 1])
        ps = proj(C, hT, 8, Wdkv, 0, 288, 3)
        copy_op(C, "act", dkv[:], ps[:, 0:288], [ps.t], [dkv.t])
        P.op("act", lambda e: e.activation(out=w1[:, 0:256], in_=dkv[:, 0:256], func=AF.Square, accum_out=ss[:, 0:1]),
             [dkv.t, ss.t], [w1.t, ss.t])
        rstd_from_ss(C, ss, rs, 1, 256)
        P.op("dve", lambda e: e.scalar_tensor_tensor(out=ckb[:], in0=dkv[:, 0:256], scalar=rs[:, 0:1], in1=gkvl[:],
                                                     op0=ALU.mult, op1=ALU.mult), [dkv.t, rs.t, gkvl.t], [ckb.t])
        transpose_cols(C, ckb, ckb.t, 2, lambda c0, c1: ckT[:, c0:c1, :], [ckT.t], [0, 1])
        r3 = raw[:].rearrange("p (h d) -> p h d", h=16)
        if "Q" in MS:
            continue
        for g4 in range(4):
            ps = proj(C, cqT, 4, Wuq, g4 * 384, 384, 4 + g4)
            copy_op(C, "act" if g4 % 2 else "dve", raw[:, g4 * 384:(g4 + 1) * 384], ps[:, 0:384], [ps.t, raw.t], [raw.t])
        import os
        MS = os.environ.get("MSKIP", "")
        headnorm_rope(C, raw, 16, 96, gq, None if "r" in MS else cs[:, t, :], 64, 16, outb[0], w1, w2, ss, rs, cs.t)
        store_T(C, outb[0], 1536, qT_d, t, tbuf[0])
        if "K" in MS:
            continue
        for g4 in range(4):
            ps = proj(C, ckT, 2, Wukv, g4 * 512, 512, 4 + g4)
            kv = kvs[g4 % 2]
            copy_op(C, "act", kv[:], ps[:], [ps.t], [kv.t])
            p3 = kv[:].rearrange("p (h d) -> p h d", h=4)
            copy_op(C, "pool", r3[:, g4 * 4:(g4 + 1) * 4, 0:64], p3[:, :, 0:64], [kv.t, raw.t], [raw.t])
            copy_op(C, "dve", vb[:, g4 * 4:(g4 + 1) * 4, :], p3[:, :, 64:128], [kv.t, vb.t], [vb.t])
        copy_op(C, "dve" if "b" in MS else "pool", r3[:, :, 64:96], dkv[:, 256:288].unsqueeze(1).to_broadcast([128, 16, 32]), [dkv.t, raw.t], [raw.t])
        if "v" not in MS:
            P.dma("sp", v_d[t * 128:(t + 1) * 128, :], vb[:].rearrange("p h d -> p (h d)"), reads=[vb.t])
        headnorm_rope(C, raw, 16, 96, gk, None if "r" in MS else cs[:, t, :], 64, 16, outb[1], w1, w2, ss, rs, cs.t)
        store_T(C, outb[1], 1536, kT_d, t, tbuf[1])
    P.barrier()
    C.release(m)


def phase_mla_attn(C, scr):
    P = C.P
    S = C.S
    m = C.mark()
    QT = [C.sb([128, S], BF16) for _ in range(2)]
    KT = [C.sb([128, S], BF16) for _ in range(2)]
    V = [C.sb([128, C.NT, 128], BF16) for _ in range(2)]
    for v in V:
        P.op("pool", lambda e, v=v: e.memset(v[:, :, 64:128], 1.0), [], [v.t])
    pb = [C.sb([128, 512], BF16) for _ in range(5)]
    R = C.sb([128, 512], F32)
    ob = [C.sb([128, 512], BF16) for _ in range(2)]
    kc = [0]
    k = 0
    for h in range(16):
        Vh, Q, K = V[h % 2], QT[h % 2], KT[h % 2]
        for t0 in range(0, C.NT, 8):
            P.dma("sp", Vh[:, t0:min(t0 + 8, C.NT), 0:64], scr["mv"].rearrange("(t p) c -> p t c", p=128)[:, t0:min(t0 + 8, C.NT), h * 64:(h + 1) * 64],
                  reads=[Vh.t], writes=[Vh.t])
        load_head(C, scr["mqT"][h * 96:(h + 1) * 96, :], scr["mkT"][h * 96:(h + 1) * 96, :], 96, Q, K)
        for qs in range(C.NG):
            bo = 2 + (k % 2)
            k += 1
            softmax_block(C, Q, K, 96, VL(lambda j, Vh=Vh: Vh[:, j, :], Vh.t), qs, 0, pb, bo, None, kc, sbanks=(0, 1, 4, 5))
            po = C.psum[bo]
            P.op("dve", lambda e, po=po: e.reciprocal(out=R[0:64, :], in_=po[64:128, :]), [po.t], [R.t])
            o = ob[k % 2]
            P.op("dve", lambda e, po=po, o=o: e.tensor_tensor(out=o[0:64, :], in0=po[0:64, :], in1=R[0:64, :], op=ALU.mult),
                 [po.t, R.t, o.t], [o.t])
            P.dma("sp", scr["mixT"][h * 64:(h + 1) * 64, qs * 512:(qs + 1) * 512], o[0:64, :], reads=[o.t])
    P.barrier()
    C.release(m)


WNAMES = ["ffn_norm", "ffn_w_gate", "ffn_w_up", "ffn_w_down", "mix_norm", "ab_w_in", "ab_w_out", "diff_q_norm",
          "diff_k_norm", "diff_subln", "mla_w_dq", "mla_q_norm", "mla_w_uq", "mla_w_dkv", "mla_kv_norm", "mla_w_ukv",
          "mla_qk_norm_q", "mla_qk_norm_k", "mla_w_o", "xm_norm", "xm_mem_norm", "xm_w_q", "xm_w_kv", "xm_q_norm",
          "xm_k_norm", "xm_w_o"]


def build(S, shapes, phases=None):
    nc = bass.Bass("TRN2", target_bir_lowering=False)
    ins = {}
    for name, shp in shapes.items():
        ins[name] = nc.dram_tensor(name, list(shp), F32, kind="ExternalInput").ap()
    out = nc.dram_tensor("out", [S, D], F32, kind="ExternalOutput").ap()
    scr_t = nc.dram_tensor("scr", [5120 * S], BF16).ap()

    def sv(off, rows, cols):
        return scr_t[off * S:(off + rows * cols // S) * S].rearrange("(r c) -> r c", c=cols)
    scr = {"qaT": sv(0, 512, S), "kaT": sv(512, 512, S), "va": sv(1024, S, 512), "qbT": sv(1536, 512, S),
           "kbT": sv(2048, 512, S), "vb": sv(2560, S, 512),
           "mqT": sv(0, 1536, S), "mkT": sv(1536, 1536, S), "mv": sv(3072, S, 1024), "mixT": sv(4096, 1024, S)}
    C = Ctx(nc, S)
    P = C.P
    setup_consts(C, ins["c_ident"], ins["c_masks"], ins["c_tri"])
    xin = Tr()
    P.dma("sp", out, ins["x"], writes=[xin])
    P.barrier()
    w = ins
    allp = ["ffn00", "abproj", "diff", "sb", "out0", "xm0", "ffn01", "ffn10", "mlaproj", "mlaattn", "out1", "xm1", "ffn11"]
    for ph in (allp if phases is None else phases):
        if ph.startswith("ffn"):
            l, i = int(ph[3]), int(ph[4])
            phase_ffn(C, out, w["ffn_norm"][l, i], w["ffn_w_gate"][l, i], w["ffn_w_up"][l, i], w["ffn_w_down"][l, i])
        elif ph == "abproj":
            phase_ab_proj(C, out, w["mix_norm"][0], w["ab_w_in"][0], w["diff_q_norm"][0], w["diff_k_norm"][0], w["c_cs64"], scr)
        elif ph == "diff":
            phase_diff_attn(C, scr, w["c_lam"], w["diff_subln"][0], 0.8 - 0.6 * math.exp(0.0))
        elif ph == "sb":
            phase_sb_attn(C, scr)
        elif ph == "out0":
            phase_outproj(C, out, scr["mixT"], w["ab_w_out"][0])
        elif ph == "out1":
            phase_outproj(C, out, scr["mixT"], w["mla_w_o"][0])
        elif ph.startswith("xm"):
            l = int(ph[2])
            phase_xm(C, out, w["mem"], w["xm_norm"][l], w["xm_mem_norm"][l], w["xm_w_q"][l], w["xm_w_kv"][l],
                     w["xm_q_norm"][l], w["xm_k_norm"][l], w["xm_w_o"][l])
        elif ph == "mlaproj":
            phase_mla_proj(C, out, w["mix_norm"][1], w["mla_w_dq"][0], w["mla_q_norm"][0], w["mla_w_uq"][0], w["mla_w_dkv"][0],
                           w["mla_kv_norm"][0], w["mla_w_ukv"][0], w["mla_qk_norm_q"][0], w["mla_qk_norm_k"][0], w["c_cs32"], scr)
        elif ph == "mlaattn":
            phase_mla_attn(C, scr)
    P.emit(final_wait_ops=list(C.P.dma_hist[-N_DMA_SEMS:]))
    return nc


def make_consts(S):
    c = {}
    c["c_ident"] = np.eye(128, dtype=np.float32)
    kk = np.arange(128)[:, None]
    qq = np.arange(512)[None, :]
    masks = np.zeros((8, 128, 512), np.float32)
    for j in range(4):
        k = 128 * j + kk
        masks[j] = ((k // 64) <= (qq // 64)).astype(np.float32)
        masks[4 + j] = (k < qq).astype(np.float32)
    c["c_masks"] = masks
    c["c_tri"] = (np.arange(128)[:, None] >= np.arange(128)[None, :]).astype(np.float32)
    pos = np.arange(S, dtype=np.float32)[:, None]
    for d, nm in ((64, "c_cs64"), (32, "c_cs32")):
        inv = (1.0 / (np.float32(10000.0) ** (np.arange(0, d, 2, dtype=np.float32) / np.float32(d)))).astype(np.float32)
        ang = (pos * inv[None, :]).astype(np.float32)
        c[nm] = np.concatenate([np.cos(ang), np.sin(ang)], axis=1).astype(np.float32)
    return c


def prep_inputs(inputs, S):
    shared = {k: np.ascontiguousarray(np.asarray(inputs[k], dtype=np.float32)) for k in WNAMES}
    shared["c_lam"] = np.ascontiguousarray(np.concatenate([np.asarray(inputs[k], np.float32).reshape(-1) for k in
                                           ("diff_lambda_q1", "diff_lambda_k1", "diff_lambda_q2", "diff_lambda_k2")]))
    shared.update(make_consts(S))
    x = np.asarray(inputs["x"], np.float32)
    mem = np.asarray(inputs["mem"], np.float32)
    maps = []
    for b in range(x.shape[0]):
        mmap = dict(shared)
        mmap["x"] = np.ascontiguousarray(x[b])
        mmap["mem"] = np.ascontiguousarray(mem[b])
        maps.append(mmap)
    return maps


def kernel(**inputs):
    S = inputs["x"].shape[1]
    maps = prep_inputs(inputs, S)
    shapes = {k: v.shape for k, v in maps[0].items()}
    nc = build(S, shapes)
    res = run_bass_kernel_spmd(nc, maps, core_ids=list(range(len(maps))))
    return np.stack([np.asarray(r["out"], dtype=np.float32) for r in res.results], axis=0)
```

```python
import math
import numpy as np
import concourse.bass as bass
import concourse.mybir as mybir
from concourse.bass_utils import run_bass_kernel_spmd

F32 = mybir.dt.float32
BF16 = mybir.dt.bfloat16
AF = mybir.ActivationFunctionType
ALU = mybir.AluOpType
AX = mybir.AxisListType
N_DMA_SEMS = 48
EPS = 1e-6
D = 1024
DFF = 2816
NFF = 22
MEM = 256


class Tr:
    __slots__ = ("lw", "rd")

    def __init__(self):
        self.lw = None
        self.rd = []


class Op:
    __slots__ = ("eng", "fn", "deps", "needed", "cnt", "is_dma", "dslot", "dval")

    def __init__(self, eng, fn, is_dma=False):
        self.eng = eng
        self.fn = fn
        self.deps = set()
        self.needed = False
        self.cnt = 0
        self.is_dma = is_dma
        self.dslot = -1
        self.dval = 0


class Prog:
    ENGS = ("pe", "act", "dve", "pool", "sp")

    def __init__(self, nc):
        self.nc = nc
        self.streams = {e: [] for e in self.ENGS}
        self.n_dma = 0
        self.dma_hist = []
        self.since_barrier = []
        self.order = []

    def _add_deps(self, op, reads, writes):
        for t in reads:
            if t.lw is not None:
                op.deps.add(t.lw)
        for t in writes:
            if t.lw is not None:
                op.deps.add(t.lw)
            for r in t.rd:
                op.deps.add(r)
        for t in reads:
            if not op.is_dma:
                t.rd = [r for r in t.rd if r.is_dma or r.eng != op.eng]
            t.rd.append(op)
        for t in writes:
            t.lw = op
            t.rd = []
        op.deps.discard(op)

    def op(self, eng, fn, reads=(), writes=()):
        o = Op(eng, fn)
        self._add_deps(o, reads, writes)
        self.streams[eng].append(o)
        self.order.append(o)
        return o

    def dma(self, queue, out_ap, in_ap, reads=(), writes=()):
        def fn(e, out_ap=out_ap, in_ap=in_ap):
            return e.dma_start(out=out_ap, in_=in_ap)
        o = Op(queue, fn, is_dma=True)
        i = self.n_dma
        self.n_dma += 1
        o.dslot = i % N_DMA_SEMS
        o.dval = 16 * (i // N_DMA_SEMS + 1)
        if i >= N_DMA_SEMS:
            o.deps.add(self.dma_hist[i - N_DMA_SEMS])
        self.dma_hist.append(o)
        self.since_barrier.append(o)
        self._add_deps(o, reads, writes)
        self.streams[queue].append(o)
        self.order.append(o)
        return o

    def barrier(self):
        lasts = []
        for e in self.ENGS:
            for o in reversed(self.streams[e]):
                if not o.is_dma:
                    lasts.append(o)
                    break
        dmas = list(self.since_barrier)
        self.since_barrier = []
        for e in self.ENGS:
            o = Op(e, lambda eng: eng.nop())
            o.deps.update(lasts)
            o.deps.update(dmas)
            self.streams[e].append(o)
            self.order.append(o)

    def emit(self, final_wait_ops=()):
        nc = self.nc
        for e in self.ENGS:
            for o in self.streams[e]:
                for d in o.deps:
                    if not d.is_dma:
                        if d.eng == "pe" and o.eng == "pe" and not o.is_dma:
                            continue
                        d.needed = True
        for e in self.ENGS:
            c = 0
            for o in self.streams[e]:
                if (not o.is_dma) and o.needed:
                    c += 1
                    o.cnt = c
        esem = {e: nc.semaphore("s_" + e).__enter__() for e in self.ENGS}
        dsem = [nc.semaphore("d%d" % i).__enter__() for i in range(N_DMA_SEMS)]
        engobj = {"pe": nc.tensor, "act": nc.scalar, "dve": nc.vector, "pool": nc.gpsimd, "sp": nc.sync}
        seen = {e: {} for e in self.ENGS}

        def do_waits(ename, deps):
            eng = engobj[ename]
            waits = {}
            for d in deps:
                if d.is_dma:
                    key = ("d", d.dslot)
                    val = d.dval
                else:
                    if d.eng == "pe" and ename == "pe":
                        continue
                    key = ("e", d.eng)
                    val = d.cnt
                if waits.get(key, 0) < val:
                    waits[key] = val
            sn = seen[ename]
            for key, val in waits.items():
                if sn.get(key, 0) >= val:
                    continue
                sn[key] = val
                sm = dsem[key[1]] if key[0] == "d" else esem[key[1]]
                eng.wait_ge(sm, val)
        for o in self.order:
            do_waits(o.eng, o.deps)
            ins = o.fn(engobj[o.eng])
            if o.is_dma:
                ins.then_inc(dsem[o.dslot], 16)
            elif o.needed:
                ins.then_inc(esem[o.eng], 1)
        do_waits("sp", final_wait_ops)
        import os
        if os.environ.get("KDEBUG"):
            print("SEMCOUNTS", {e: max([o.cnt for o in self.streams[e]] + [0]) for e in self.ENGS}, "ndma", self.n_dma, "nops", len(self.order))


class Buf:
    def __init__(self, handle, ntr=1):
        self.h = handle
        self.tr = [Tr() for _ in range(ntr)]

    def __getitem__(self, k):
        return self.h[k]

    @property
    def t(self):
        return self.tr[0]


SB_BASE = 16640
SB_TOP = 229376


class Ctx:
    def __init__(self, nc, S):
        self.nc = nc
        self.S = S
        self.NT = S // 128
        self.NG = S // 512
        self.P = Prog(nc)
        self.off = SB_BASE
        self.uid = 0
        self.psum = [Buf(nc.alloc_psum_tensor("ps%d" % i, [128, 512], F32)) for i in range(8)]
        self.rr = 0

    def sb(self, shape, dtype, ntr=1):
        n = 1
        for s in shape[1:]:
            n *= s
        nbytes = n * (4 if dtype == F32 else 2)
        nbytes = (nbytes + 63) // 64 * 64
        assert self.off + nbytes <= SB_TOP, ("SBUF overflow", self.off, nbytes)
        self.uid += 1
        h = self.nc.alloc_sbuf_tensor_at("t%d" % self.uid, list(shape), dtype, offset=self.off)
        self.off += nbytes
        return Buf(h, ntr)

    def mark(self):
        return self.off

    def release(self, m):
        self.off = m

    def eng_rr(self, engs=("dve", "pool")):
        self.rr += 1
        return engs[self.rr % len(engs)]


def copy_op(C, eng, out_ap, in_ap, reads, writes):
    if eng == "act":
        return C.P.op("act", lambda e: e.copy(out=out_ap, in_=in_ap), reads, writes)
    return C.P.op(eng, lambda e: e.tensor_copy(out=out_ap, in_=in_ap), reads, writes)


def load_weight(C, w_dram, K, N, stg, col0=0, ncols=None, engs=("dve", "pool")):
    ncols = N if ncols is None else ncols
    kc = (K + 127) // 128
    W = C.sb([128, kc, ncols], BF16)
    CH = stg[0].h.shape[1]
    i = 0
    for c in range(kc):
        rows = min(128, K - c * 128)
        for n0 in range(0, ncols, CH):
            n1 = min(ncols, n0 + CH)
            s = stg[C.rr % len(stg)]
            C.P.dma("sp", s[0:rows, 0:n1 - n0], w_dram[c * 128:c * 128 + rows, col0 + n0:col0 + n1], writes=[s.t])
            copy_op(C, C.eng_rr(engs), W[0:rows, c, n0:n1], s[0:rows, 0:n1 - n0], [s.t, W.t], [W.t])
    return W


def load_bcast(C, vec_dram, n):
    b = C.sb([128, n], F32)
    C.P.dma("sp", b[:], vec_dram.partition_broadcast(128), writes=[b.t])
    return b


def rstd_from_ss(C, ss, rs, n, dim):
    P = C.P
    P.op("dve", lambda e: e.tensor_scalar(out=ss[:, 0:n], in0=ss[:, 0:n], scalar1=1.0 / dim, scalar2=EPS,
                                          op0=ALU.mult, op1=ALU.add), [ss.t], [ss.t])
    P.op("act", lambda e: e.activation(out=ss[:, 0:n], in_=ss[:, 0:n], func=AF.Sqrt), [ss.t], [ss.t])
    P.op("dve", lambda e: e.reciprocal(out=rs[:, 0:n], in_=ss[:, 0:n]), [ss.t], [rs.t])


def norm_rows(C, xt, gain, hb, junk, ss, rs):
    P = C.P
    P.op("act", lambda e: e.activation(out=junk[:], in_=xt[:], func=AF.Square, accum_out=ss[:, 0:1]),
         [xt.t], [junk.t, ss.t])
    rstd_from_ss(C, ss, rs, 1, D)
    P.op("dve", lambda e: e.scalar_tensor_tensor(out=hb[:], in0=xt[:], scalar=rs[:, 0:1], in1=gain[:],
                                                 op0=ALU.mult, op1=ALU.mult), [xt.t, rs.t, gain.t], [hb.t])


def transpose_cols(C, src, src_tr, ncol, dst_fn, dst_tr, banks, ceng=("act", "dve")):
    P = C.P
    for b0 in range(0, ncol, 4):
        b1 = min(ncol, b0 + 4)
        ps = C.psum[banks[(b0 // 4) % len(banks)]]
        for c in range(b0, b1):
            P.op("pe", lambda e, c=c, ps=ps, b0=b0: e.matmul(out=ps[:, (c - b0) * 128:(c - b0 + 1) * 128],
                                                           lhsT=src[:, c * 128:(c + 1) * 128], rhs=C.ident[:],
                                                           start=True, stop=True),
                 [src_tr, C.ident.t], [ps.t])
        n = b1 - b0
        copy_op(C, C.eng_rr(ceng), dst_fn(b0, b1), ps[:, 0:n * 128].rearrange("p (c t) -> p c t", c=n),
                [ps.t] + list(dst_tr), list(dst_tr))


def setup_consts(C, ident_d, masks_d, tri_d):
    P = C.P
    C.ident = C.sb([128, 128], BF16)
    C.ones = C.sb([128, 128], BF16)
    C.masks = C.sb([128, 8, 512], BF16)
    C.trineg = C.sb([128, 128], BF16)
    C.onesneg = C.sb([128, 128], BF16)
    m = C.mark()
    idf = C.sb([128, 128], F32)
    trf = C.sb([128, 128], F32)
    mf = C.sb([128, 8, 512], F32)
    P.dma("sp", idf[:], ident_d, writes=[idf.t])
    P.dma("sp", trf[:], tri_d, writes=[trf.t])
    P.dma("sp", mf[:], masks_d.rearrange("m p q -> p m q"), writes=[mf.t])
    copy_op(C, "dve", C.ident[:], idf[:], [idf.t], [C.ident.t])
    P.op("dve", lambda e: e.tensor_scalar(out=C.trineg[:], in0=trf[:], scalar1=-1.0, scalar2=None, op0=ALU.mult),
         [trf.t], [C.trineg.t])
    copy_op(C, "pool", C.masks[:], mf[:], [mf.t], [C.masks.t])
    P.op("pool", lambda e: e.memset(C.ones[:], 1.0), [], [C.ones.t])
    P.op("pool", lambda e: e.memset(C.onesneg[:], -1.0), [], [C.onesneg.t])
    P.barrier()
    C.release(m)


def phase_ffn(C, x_d, gain_d, wg_d, wu_d, wd_d):
    P = C.P
    m = C.mark()
    stg = [C.sb([128, 1024], F32), C.sb([128, 1024], F32)]
    Wg = load_weight(C, wg_d, D, DFF, stg)
    Wu = load_weight(C, wu_d, D, DFF, stg)
    Wd = load_weight(C, wd_d, DFF, D, stg)
    gain = load_bcast(C, gain_d, D)
    xs = [C.sb([128, D], F32) for _ in range(4)]
    hb = C.sb([128, D], BF16)
    junk = hb
    ss = C.sb([128, 8], F32)
    rs = C.sb([128, 8], F32)
    hT = C.sb([128, 8, 512], BF16)
    actT = C.sb([128, NFF, 512], BF16, ntr=NFF)
    sg = [C.sb([128, 512], F32), C.sb([128, 512], F32)]
    for g in range(C.NG):
        for t in range(4):
            r0 = g * 512 + t * 128
            P.dma("sp", xs[t][:], x_d[r0:r0 + 128, :], writes=[xs[t].t])
            norm_rows(C, xs[t], gain, hb, junk, ss, rs)
            transpose_cols(C, hb, hb.t, 8, lambda c0, c1, t=t: hT[:, c0:c1, t * 128:(t + 1) * 128], [hT.t], [0, 1])
        for f in range(NFF):
            pg = C.psum[2 + (f % 2)]
            pu = C.psum[4 + (f % 2)]
            for c in range(8):
                P.op("pe", lambda e, c=c, f=f, pg=pg: e.matmul(out=pg[:], lhsT=Wg[:, c, f * 128:(f + 1) * 128],
                                                             rhs=hT[:, c, :], start=(c == 0), stop=(c == 7)),
                     [Wg.t, hT.t], [pg.t])
            for c in range(8):
                P.op("pe", lambda e, c=c, f=f, pu=pu: e.matmul(out=pu[:], lhsT=Wu[:, c, f * 128:(f + 1) * 128],
                                                             rhs=hT[:, c, :], start=(c == 0), stop=(c == 7)),
                     [Wu.t, hT.t], [pu.t])
            s = sg[f % 2]
            P.op("act", lambda e, s=s, pg=pg: e.activation(out=s[:], in_=pg[:], func=AF.Silu), [pg.t], [s.t])
            P.op("dve", lambda e, s=s, pu=pu, f=f: e.tensor_tensor(out=actT[:, f, :], in0=pu[:], in1=s[:], op=ALU.mult),
                 [pu.t, s.t], [actT.tr[f]])
        for t in range(4):
            for h in range(2):
                py = C.psum[6 + h]
                for f in range(NFF):
                    P.op("pe", lambda e, f=f, t=t, h=h, py=py: e.matmul(out=py[:], lhsT=actT[:, f, t * 128:(t + 1) * 128],
                                                                        rhs=Wd[:, f, h * 512:(h + 1) * 512],
                                                                        start=(f == 0), stop=(f == NFF - 1)),
                         [actT.tr[f], Wd.t], [py.t])
                P.op("dve", lambda e, t=t, h=h, py=py: e.scalar_tensor_tensor(
                    out=xs[t][:, h * 512:(h + 1) * 512], in0=py[:], scalar=0.5, in1=xs[t][:, h * 512:(h + 1) * 512],
                    op0=ALU.mult, op1=ALU.add), [py.t, xs[t].t], [xs[t].t])
            r0 = g * 512 + t * 128
            P.dma("sp", x_d[r0:r0 + 128, :], xs[t][:], reads=[xs[t].t])
    P.barrier()
    C.release(m)


def headnorm_rope(C, raw, nh, dh, gain_b, cs_ap, off, half, outb, w1, w2, ss, rs, cs_tr=None):
    P = C.P
    n = nh * dh
    r3 = raw[:, 0:n].rearrange("p (h d) -> p h d", h=nh)
    o3 = outb[:, 0:n].rearrange("p (h d) -> p h d", h=nh)
    a3 = w1[:, 0:n].rearrange("p (h d) -> p h d", h=nh)
    b3 = w2[:, 0:n].rearrange("p (h d) -> p h d", h=nh)
    P.op("act", lambda e: e.activation(out=w1[:, 0:n], in_=raw[:, 0:n], func=AF.Square), [raw.t], [w1.t])
    P.op("dve", lambda e: e.tensor_reduce(out=ss[:, 0:nh], in_=a3, axis=AX.X, op=ALU.add), [w1.t], [ss.t])
    rstd_from_ss(C, ss, rs, nh, dh)
    P.op("dve", lambda e: e.tensor_tensor(out=r3, in0=r3, in1=rs[:, 0:nh].unsqueeze(2).to_broadcast([128, nh, dh]),
                                          op=ALU.mult), [raw.t, rs.t], [raw.t])
    P.op("pool", lambda e: e.tensor_tensor(out=r3, in0=r3, in1=gain_b[:, 0:dh].unsqueeze(1).to_broadcast([128, nh, dh]),
                                           op=ALU.mult), [raw.t, gain_b.t], [raw.t])
    if cs_ap is None:
        copy_op(C, "act", outb[:, 0:n], raw[:, 0:n], [raw.t], [outb.t])
        return
    cos = cs_ap[:, 0:half].unsqueeze(1).to_broadcast([128, nh, half])
    sin = cs_ap[:, half:2 * half].unsqueeze(1).to_broadcast([128, nh, half])
    x1 = r3[:, :, off:off + half]
    x2 = r3[:, :, off + half:off + 2 * half]
    if off > 0:
        copy_op(C, "act", o3[:, :, 0:off], r3[:, :, 0:off], [raw.t, outb.t], [outb.t])
    P.op("pool", lambda e: e.tensor_tensor(out=a3[:, :, 0:half], in0=x1, in1=cos, op=ALU.mult), [raw.t, w1.t, cs_tr], [w1.t])
    P.op("dve", lambda e: e.tensor_tensor(out=b3[:, :, 0:half], in0=x2, in1=sin, op=ALU.mult), [raw.t, w2.t, cs_tr], [w2.t])
    P.op("pool", lambda e: e.tensor_tensor(out=a3[:, :, half:2 * half], in0=x1, in1=sin, op=ALU.mult), [raw.t, w1.t, cs_tr], [w1.t])
    P.op("dve", lambda e: e.tensor_tensor(out=b3[:, :, half:2 * half], in0=x2, in1=cos, op=ALU.mult), [raw.t, w2.t, cs_tr], [w2.t])
    P.op("dve", lambda e: e.tensor_tensor(out=o3[:, :, off:off + half], in0=a3[:, :, 0:half], in1=b3[:, :, 0:half],
                                          op=ALU.subtract), [w1.t, w2.t, outb.t], [outb.t])
    P.op("pool", lambda e: e.tensor_tensor(out=o3[:, :, off + half:off + 2 * half], in0=a3[:, :, half:2 * half],
                                           in1=b3[:, :, half:2 * half], op=ALU.add), [w1.t, w2.t, outb.t], [outb.t])


def store_T(C, outb, ncols, dstT, t, tbuf):
    import os
    nchunk = ncols // 128
    transpose_cols(C, outb, outb.t, nchunk, lambda c0, c1: tbuf[:, c0:c1, :], [tbuf.t], [0, 1])
    if "s" in os.environ.get("MSKIP", ""):
        return
    for c0 in range(0, nchunk, 4):
        C.P.dma("sp", dstT.rearrange("(c p) s -> p c s", p=128)[:, c0:c0 + 4, t * 128:(t + 1) * 128],
                tbuf[:, c0:c0 + 4, :], reads=[tbuf.t])


def proj(C, hT, kc, W, c0, n, bank, tok0=0, ntok=128):
    ps = C.psum[bank]
    for c in range(kc):
        C.P.op("pe", lambda e, c=c: e.matmul(out=ps[0:ntok, 0:n], lhsT=hT[:, c, tok0:tok0 + ntok], rhs=W[:, c, c0:c0 + n],
                                             start=(c == 0), stop=(c == kc - 1)), [hT.t, W.t], [ps.t])
    return ps


def scaled_gain(C, vec_d, n, scale):
    g = load_bcast(C, vec_d, n)
    if scale != 1.0:
        C.P.op("dve", lambda e: e.tensor_scalar(out=g[:], in0=g[:], scalar1=float(scale), scalar2=None, op0=ALU.mult),
               [g.t], [g.t])
    return g


def phase_ab_proj(C, x_d, gain_d, w_in_d, qn_d, kn_d, cs_d, scr):
    P = C.P
    S = C.S
    m = C.mark()
    stg = [C.sb([128, 1024], F32), C.sb([128, 1024], F32)]
    W = load_weight(C, w_in_d, D, 3072, stg)
    gain = load_bcast(C, gain_d, D)
    gq = scaled_gain(C, qn_d, 64, 0.125)
    gk = scaled_gain(C, kn_d, 64, 1.0)
    cs = C.sb([128, C.NT, 64], F32)
    P.dma("sp", cs[:], cs_d.rearrange("(t p) c -> p t c", p=128), writes=[cs.t])
    xt = [C.sb([128, D], F32) for _ in range(2)]
    hbL = [C.sb([128, D], BF16) for _ in range(2)]
    ssL = [C.sb([128, 16], F32) for _ in range(2)]
    rsL = [C.sb([128, 16], F32) for _ in range(2)]
    hTL = [C.sb([128, 8, 128], BF16) for _ in range(2)]
    sets = [[(C.sb([128, 512], F32), C.sb([128, 512], F32), C.sb([128, 512], F32), C.sb([128, 16], F32), C.sb([128, 16], F32))
             for _ in range(2)] for _ in range(2)]
    outbL = [[C.sb([128, 512], BF16) for _ in range(2)] for _ in range(2)]
    tbufL = [[C.sb([128, 4, 128], BF16) for _ in range(2)] for _ in range(2)]
    vb = [C.sb([128, 512], BF16) for _ in range(2)]
    qaT, kaT, va, qbT, kbT, vbd = scr["qaT"], scr["kaT"], scr["va"], scr["qbT"], scr["kbT"], scr["vb"]
    k = 0
    for t in range(C.NT):
        x = xt[t % 2]
        pp_ = t % 2
        hb, ss, rs, hT = hbL[pp_], ssL[pp_], rsL[pp_], hTL[pp_]
        outb, tbuf = [outbL[0][pp_], outbL[1][pp_]], [tbufL[0][pp_], tbufL[1][pp_]]
        P.dma("sp", x[:], x_d[t * 128:(t + 1) * 128, :], writes=[x.t])
        norm_rows(C, x, gain, hb, hb, ss, rs)
        transpose_cols(C, hb, hb.t, 8, lambda c0, c1: hT[:, c0:c1, :], [hT.t], [0, 1])
        for grp in range(6):
            ps = proj(C, hT, 8, W, grp * 512, 512, 2 + (k % 6))
            k += 1
            if grp in (0, 1):
                raw, w1, w2, ss2, rs2 = sets[grp][pp_]
                copy_op(C, "act", raw[:], ps[:], [ps.t], [raw.t])
                ob = outb[grp]
                headnorm_rope(C, raw, 8, 64, gq if grp == 0 else gk, cs[:, t, :], 0, 32, ob, w1, w2, ss2, rs2, cs.t)
                store_T(C, ob, 512, qaT if grp == 0 else kaT, t, tbuf[grp])
            elif grp in (2, 5):
                v = vb[0 if grp == 2 else 1]
                copy_op(C, "act", v[:], ps[:], [ps.t, v.t], [v.t])
                P.dma("sp", (va if grp == 2 else vbd)[t * 128:(t + 1) * 128, :], v[:], reads=[v.t])
            else:
                ob = outb[grp - 3]
                if grp == 3:
                    P.op("act", lambda e, ob=ob, ps=ps: e.activation(out=ob[:], in_=ps[:], func=AF.Copy, scale=0.125),
                         [ps.t, ob.t], [ob.t])
                else:
                    copy_op(C, "dve", ob[:], ps[:], [ps.t, ob.t], [ob.t])
                store_T(C, ob, 512, qbT if grp == 3 else kbT, t, tbuf[grp - 3])
    P.barrier()
    C.release(m)


def load_head(C, qT_d, kT_d, dk, QT, KT):
    C.P.dma("sp", QT[0:dk, :], qT_d, reads=[QT.t], writes=[QT.t])
    C.P.dma("sp", KT[0:dk, :], kT_d, reads=[KT.t], writes=[KT.t])


def softmax_block(C, QT, KT, dk, Vl, qs, mask0, pbufs, bank_o, bank_d, k_ctr, sbanks=(0, 1)):
    P = C.P
    nj = 4 * qs + 4
    po = C.psum[bank_o]
    pd = C.psum[bank_d] if bank_d is not None else None
    st = {}

    def stage_a(j):
        ps = C.psum[sbanks[k_ctr[0] % len(sbanks)]]
        pt = pbufs[k_ctr[0] % len(pbufs)]
        k_ctr[0] += 1
        st[j] = (ps, pt)
        P.op("pe", lambda e, j=j, ps=ps: e.matmul(out=ps[:], lhsT=KT[0:dk, j * 128:(j + 1) * 128],
                                                 rhs=QT[0:dk, qs * 512:(qs + 1) * 512], start=True, stop=True),
             [KT.t, QT.t], [ps.t])

    def stage_b(j):
        ps, pt = st.pop(j)
        P.op("act", lambda e, ps=ps, pt=pt: e.activation(out=pt[:], in_=ps[:], func=AF.Exp), [ps.t], [pt.t])
        if j >= 4 * qs:
            mi = mask0 + j - 4 * qs
            P.op("pool", lambda e, pt=pt, mi=mi: e.tensor_tensor(out=pt[:], in0=pt[:], in1=C.masks[:, mi, :], op=ALU.mult),
                 [pt.t, C.masks.t], [pt.t])
        P.op("pe", lambda e, j=j, pt=pt: e.matmul(out=po[:], lhsT=Vl(j), rhs=pt[:], start=(j == 0), stop=(j == nj - 1)),
             [pt.t, Vl.tr], [po.t])
        if pd is not None:
            P.op("pe", lambda e, j=j, pt=pt: e.matmul(out=pd[:], lhsT=C.ones[:], rhs=pt[:], start=(j == 0),
                                                     stop=(j == nj - 1)), [pt.t, C.ones.t], [pd.t])
    la = min(2, len(sbanks) - 1)
    for j in range(min(la, nj)):
        stage_a(j)
    for j in range(nj):
        if j + la < nj:
            stage_a(j + la)
        stage_b(j)


class VL:
    def __init__(self, fn, tr):
        self.fn = fn
        self.tr = tr

    def __call__(self, j):
        return self.fn(j)


def phase_diff_attn(C, scr, lam_d, subln_d, lambda_init):
    P = C.P
    S = C.S
    m = C.mark()
    QT = [C.sb([128, S], BF16) for _ in range(2)]
    KT = [C.sb([128, S], BF16) for _ in range(2)]
    V = [C.sb([128, C.NT, 128], BF16) for _ in range(2)]
    pb = [C.sb([128, 512], BF16) for _ in range(4)]
    A = C.sb([128, 512], F32)
    B = C.sb([128, 512], F32)
    R = C.sb([128, 512], F32)
    ob = [C.sb([128, 512], BF16) for _ in range(2)]
    Bh = C.sb([128, 512], BF16)
    Bl = C.sb([128, 512], BF16)
    lv = C.sb([128, 4, 64], F32)
    P.dma("sp", lv[:].rearrange("p a b -> p (a b)"), lam_d.partition_broadcast(128), writes=[lv.t])
    lt = C.sb([128, 2, 64], F32)
    ls = C.sb([128, 4], F32)
    P.op("dve", lambda e: e.tensor_tensor(out=lt[:], in0=lv[:, 0:4:2, :], in1=lv[:, 1:4:2, :], op=ALU.mult), [lv.t], [lt.t])
    P.op("dve", lambda e: e.tensor_reduce(out=ls[:, 0:2], in_=lt[:], axis=AX.X, op=ALU.add), [lt.t], [ls.t])
    P.op("act", lambda e: e.activation(out=ls[:, 0:2], in_=ls[:, 0:2], func=AF.Exp), [ls.t], [ls.t])
    P.op("dve", lambda e: e.tensor_tensor(out=ls[:, 2:3], in0=ls[:, 1:2], in1=ls[:, 0:1], op=ALU.subtract), [ls.t], [ls.t])
    P.op("dve", lambda e: e.tensor_scalar(out=ls[:, 3:4], in0=ls[:, 2:3], scalar1=-float(lambda_init), scalar2=None,
                                          op0=ALU.add), [ls.t], [ls.t])
    sub = C.sb([128, 1], F32)
    P.dma("sp", sub[:], subln_d.rearrange("(p o) -> p o", o=1), writes=[sub.t])
    P.op("dve", lambda e: e.tensor_scalar(out=sub[:], in0=sub[:], scalar1=float(1.0 - lambda_init), scalar2=None,
                                          op0=ALU.mult), [sub.t], [sub.t])
    kc = [0]
    for h in range(4):
        Vh = V[h % 2]
        for t0 in range(0, C.NT, 8):
            P.dma("sp", Vh[:, t0:min(t0 + 8, C.NT), :], scr["va"].rearrange("(t p) c -> p t c", p=128)[:, t0:min(t0 + 8, C.NT), h * 128:(h + 1) * 128],
                  reads=[Vh.t], writes=[Vh.t])
        for mp in range(2):
            hh = h + 4 * mp
            load_head(C, scr["qaT"][hh * 64:(hh + 1) * 64, :], scr["kaT"][hh * 64:(hh + 1) * 64, :], 64, QT[mp], KT[mp])
        for qs in range(C.NG):
            for mp in range(2):
                softmax_block(C, QT[mp], KT[mp], 64, VL(lambda j, Vh=Vh: Vh[:, j, :], Vh.t), qs, 0, pb, 2 + 2 * mp, 3 + 2 * mp, kc, sbanks=(0, 1, 7))
                po, pd = C.psum[2 + 2 * mp], C.psum[3 + 2 * mp]
                P.op("dve", lambda e, pd=pd: e.reciprocal(out=R[:], in_=pd[:]), [pd.t], [R.t])
                dst = A if mp == 0 else B
                P.op("dve", lambda e, po=po, dst=dst: e.tensor_tensor(out=dst[:], in0=po[:], in1=R[:], op=ALU.mult),
                     [po.t, R.t], [dst.t])
            P.op("dve", lambda e: e.scalar_tensor_tensor(out=A[:], in0=B[:], scalar=ls[:, 3:4], in1=A[:], op0=ALU.mult,
                                                         op1=ALU.add), [A.t, B.t, ls.t], [A.t])
            P.op("act", lambda e: e.activation(out=B[:], in_=A[:], func=AF.Square), [A.t], [B.t])
            pq = C.psum[6]
            P.op("dve", lambda e: e.tensor_copy(out=Bh[:], in_=B[:]), [B.t, Bh.t], [Bh.t])
            P.op("dve", lambda e: e.tensor_tensor(out=Bl[:], in0=B[:], in1=Bh[:], op=ALU.subtract), [B.t, Bh.t, Bl.t], [Bl.t])
            P.op("pe", lambda e, pq=pq: e.matmul(out=pq[:], lhsT=C.ones[:], rhs=Bh[:], start=True, stop=False),
                 [C.ones.t, Bh.t], [pq.t])
            P.op("pe", lambda e, pq=pq: e.matmul(out=pq[:], lhsT=C.ones[:], rhs=Bl[:], start=False, stop=True),
                 [C.ones.t, Bl.t], [pq.t])
            P.op("dve", lambda e, pq=pq: e.tensor_scalar(out=R[:], in0=pq[:], scalar1=1.0 / 128, scalar2=EPS, op0=ALU.mult,
                                                         op1=ALU.add), [pq.t], [R.t])
            P.op("act", lambda e: e.activation(out=R[:], in_=R[:], func=AF.Sqrt), [R.t], [R.t])
            P.op("dve", lambda e: e.reciprocal(out=R[:], in_=R[:]), [R.t], [R.t])
            o = ob[qs % 2]
            P.op("dve", lambda e, o=o: e.scalar_tensor_tensor(out=o[:], in0=A[:], scalar=sub[:, 0:1], in1=R[:], op0=ALU.mult,
                                                              op1=ALU.mult), [A.t, R.t, sub.t, o.t], [o.t])
            P.dma("sp", scr["mixT"][h * 128:(h + 1) * 128, qs * 512:(qs + 1) * 512], o[:], reads=[o.t])
    P.barrier()
    C.release(m)


def phase_sb_attn(C, scr):
    P = C.P
    S = C.S
    m = C.mark()
    QT = [C.sb([128, S], BF16) for _ in range(2)]
    KT = [C.sb([128, S], BF16) for _ in range(2)]
    V = [C.sb([128, C.NT, 128], BF16) for _ in range(2)]
    for b_ in QT + KT:
        P.op("pool", lambda e, b_=b_: e.memset(b_[64:128, :], 0.0), [], [b_.t])
    for v in V:
        P.op("pool", lambda e, v=v: e.memset(v[:, :, 64:128], 0.0), [], [v.t])
    ef = [C.sb([128, 512], F32) for _ in range(3)]
    sp = [C.sb([128, 512], BF16) for _ in range(3)]
    wb = [C.sb([128, 512], BF16) for _ in range(3)]
    Ls = C.sb([128, 512], F32)
    Lb = [C.sb([128, 512], BF16) for _ in range(4)]
    ob = [C.sb([128, 512], BF16) for _ in range(2)]
    k = 0
    for h in range(8):
        Vh, Q, K = V[h % 2], QT[h % 2], KT[h % 2]
        for t0 in range(0, C.NT, 8):
            P.dma("sp", Vh[:, t0:min(t0 + 8, C.NT), 0:64], scr["vb"].rearrange("(t p) c -> p t c", p=128)[:, t0:min(t0 + 8, C.NT), h * 64:(h + 1) * 64],
                  reads=[Vh.t], writes=[Vh.t])
        load_head(C, scr["qbT"][h * 64:(h + 1) * 64, :], scr["kbT"][h * 64:(h + 1) * 64, :], 64, Q, K)
        for qs in range(C.NG):
            nj = 4 * qs + 4
            po = C.psum[6 + (qs % 2)]
            js = list(range(nj - 1, -1, -1))
            st = {}

            def stage_a(idx, js=js, qs=qs, Q=Q, K=K, st=st):
                nonlocal k
                j = js[idx]
                pz = C.psum[k % 3]
                pc = C.psum[3 + (k % 3)]
                e_, s_, w_ = ef[k % 3], sp[k % 3], wb[k % 3]
                lb_in = Lb[idx % 4]
                lb_out = Lb[(idx + 1) % 4]
                k += 1
                st[idx] = (j, pc, s_, w_, lb_in)
                diag = j >= 4 * qs
                mi = 4 + j - 4 * qs
                for pp in (pz, pc):
                    P.op("pe", lambda e, j=j, pp=pp, last=(pp is pz): e.matmul(
                        out=pp[:], lhsT=K[:, j * 128:(j + 1) * 128], rhs=Q[:, qs * 512:(qs + 1) * 512],
                        start=True, stop=last), [K.t, Q.t], [pp.t])
                P.op("act", lambda e, pz=pz, e_=e_: e.activation(out=e_[:], in_=pz[:], func=AF.Exp), [pz.t], [e_.t])
                P.op("act", lambda e, e_=e_, s_=s_: e.activation(out=s_[:], in_=e_[:], func=AF.Ln, bias=1.0, scale=1.0),
                     [e_.t], [s_.t])
                if diag:
                    P.op("pool", lambda e, s_=s_, mi=mi: e.tensor_tensor(out=s_[:], in0=s_[:], in1=C.masks[:, mi, :],
                                                                        op=ALU.mult), [s_.t, C.masks.t], [s_.t])
                if idx + 1 < len(js):
                    if idx == 0:
                        copy_op(C, "dve", Ls[:], s_[:], [s_.t, Ls.t], [Ls.t])
                    else:
                        P.op("dve", lambda e, s_=s_: e.tensor_tensor(out=Ls[:], in0=Ls[:], in1=s_[:], op=ALU.add),
                             [Ls.t, s_.t], [Ls.t])
                    copy_op(C, "dve", lb_out[:], Ls[:], [Ls.t, lb_out.t], [lb_out.t])

            def stage_b(idx, js=js, qs=qs, st=st, po=po, Vh=Vh):
                j, pc, s_, w_, lb = st.pop(idx)
                first = idx == 0
                diag = j >= 4 * qs
                mi = 4 + j - 4 * qs
                P.op("pe", lambda e, pc=pc, s_=s_, first=first: e.matmul(out=pc[:], lhsT=C.trineg[:], rhs=s_[:], start=False,
                                                                        stop=first), [C.trineg.t, s_.t], [pc.t])
                if not first:
                    P.op("pe", lambda e, pc=pc, lb=lb: e.matmul(out=pc[:], lhsT=C.onesneg[:], rhs=lb[:], start=False, stop=True),
                         [C.onesneg.t, lb.t], [pc.t])
                P.op("act", lambda e, pc=pc, w_=w_: e.activation(out=w_[:], in_=pc[:], func=AF.Exp), [pc.t], [w_.t])
                if diag:
                    P.op("pool", lambda e, w_=w_, mi=mi: e.tensor_tensor(out=w_[:], in0=w_[:], in1=C.masks[:, mi, :],
                                                                        op=ALU.mult), [w_.t, C.masks.t], [w_.t])
                P.op("pe", lambda e, j=j, w_=w_, first=first, po=po, Vh=Vh: e.matmul(out=po[:], lhsT=Vh[:, j, :], rhs=w_[:],
                                                                                    start=first, stop=(j == 0)), [Vh.t, w_.t], [po.t])
            stage_a(0)
            stage_a(1)
            for idx in range(nj):
                if idx + 2 < nj:
                    stage_a(idx + 2)
                stage_b(idx)
            o = ob[qs % 2]
            copy_op(C, "dve", o[0:64, :], po[0:64, :], [po.t, o.t], [o.t])
            P.dma("sp", scr["mixT"][512 + h * 64:512 + (h + 1) * 64, qs * 512:(qs + 1) * 512], o[0:64, :], reads=[o.t])
    P.barrier()
    C.release(m)


def phase_outproj(C, x_d, mixT_d, w_d):
    P = C.P
    m = C.mark()
    stg = [C.sb([128, 1024], F32), C.sb([128, 1024], F32)]
    W = load_weight(C, w_d, D, D, stg)
    mT = [C.sb([128, 8, 512], BF16) for _ in range(2)]
    xs = [C.sb([128, D], F32) for _ in range(3)]
    k = 0
    for g in range(C.NG):
        mt = mT[g % 2]
        P.dma("sp", mt[:], mixT_d.rearrange("(c p) s -> p c s", p=128)[:, :, g * 512:(g + 1) * 512], writes=[mt.t])
        for t in range(4):
            r0 = g * 512 + t * 128
            x = xs[k % 3]
            k += 1
            P.dma("sp", x[:], x_d[r0:r0 + 128, :], writes=[x.t])
            for h in range(2):
                py = C.psum[2 * (k % 2) + h]
                for c in range(8):
                    P.op("pe", lambda e, c=c, t=t, h=h, py=py, mt=mt: e.matmul(
                        out=py[:], lhsT=mt[:, c, t * 128:(t + 1) * 128], rhs=W[:, c, h * 512:(h + 1) * 512],
                        start=(c == 0), stop=(c == 7)), [mt.t, W.t], [py.t])
                P.op("dve", lambda e, h=h, py=py, x=x: e.tensor_tensor(out=x[:, h * 512:(h + 1) * 512], in0=py[:],
                                                                     in1=x[:, h * 512:(h + 1) * 512], op=ALU.add),
                     [py.t, x.t], [x.t])
            P.dma("sp", x_d[r0:r0 + 128, :], x[:], reads=[x.t])
    P.barrier()
    C.release(m)


def phase_xm(C, x_d, mem_d, xn_d, mn_d, wq_d, wkv_d, qn_d, kn_d, wo_d):
    P = C.P
    m = C.mark()
    stg = [C.sb([128, 1024], F32), C.sb([128, 1024], F32)]
    Wkv = load_weight(C, wkv_d, D, D, stg)
    gm = load_bcast(C, mn_d, D)
    gk = scaled_gain(C, kn_d, 128, 1.0)
    gq = scaled_gain(C, qn_d, 128, 128 ** -0.5)
    hb = C.sb([128, D], BF16)
    ss = C.sb([128, 8], F32)
    rs = C.sb([128, 8], F32)
    hT = C.sb([128, 8, 128], BF16)
    raw = C.sb([128, 512], F32)
    w1 = C.sb([128, 512], F32)
    outb = C.sb([128, 512], BF16)
    KT = C.sb([128, 4, MEM], BF16)
    Vm = C.sb([128, 2, 512], BF16)
    xs = [C.sb([128, D], F32) for _ in range(4)]
    for t in range(2):
        x = xs[t]
        P.dma("sp", x[:], mem_d[t * 128:(t + 1) * 128, :], writes=[x.t])
        norm_rows(C, x, gm, hb, hb, ss, rs)
        transpose_cols(C, hb, hb.t, 8, lambda c0, c1: hT[:, c0:c1, :], [hT.t], [0, 1])
        ps = proj(C, hT, 8, Wkv, 0, 512, 2)
        copy_op(C, "act", raw[:], ps[:], [ps.t], [raw.t])
        headnorm_rope(C, raw, 4, 128, gk, None, 0, 0, outb, w1, w1, ss, rs)
        transpose_cols(C, outb, outb.t, 4, lambda c0, c1, t=t: KT[:, c0:c1, t * 128:(t + 1) * 128], [KT.t], [0, 1])
        ps = proj(C, hT, 8, Wkv, 512, 512, 3)
        copy_op(C, "act", Vm[:, t, :], ps[:], [ps.t, Vm.t], [Vm.t])
    Wq = load_weight(C, wq_d, D, 512, stg)
    Wo = load_weight(C, wo_d, 512, D, stg)
    gx = load_bcast(C, xn_d, D)
    hT4 = C.sb([128, 8, 512], BF16)
    qT = C.sb([128, 4, 512], BF16)
    xoT = C.sb([128, 4, 512], BF16, ntr=4)
    pb = [C.sb([128, 512], BF16) for _ in range(3)]
    R = C.sb([128, 512], F32)
    kk = 0
    for g in range(C.NG):
        for t in range(4):
            r0 = g * 512 + t * 128
            x = xs[t]
            P.dma("sp", x[:], x_d[r0:r0 + 128, :], writes=[x.t])
            norm_rows(C, x, gx, hb, hb, ss, rs)
            transpose_cols(C, hb, hb.t, 8, lambda c0, c1, t=t: hT4[:, c0:c1, t * 128:(t + 1) * 128], [hT4.t], [0, 1])
            ps = proj(C, hT4, 8, Wq, 0, 512, 2 + (t % 2), tok0=t * 128)
            copy_op(C, "act", raw[:], ps[:], [ps.t], [raw.t])
            headnorm_rope(C, raw, 4, 128, gq, None, 0, 0, outb, w1, w1, ss, rs)
            transpose_cols(C, outb, outb.t, 4, lambda c0, c1, t=t: qT[:, c0:c1, t * 128:(t + 1) * 128], [qT.t], [0, 1])
        for h in range(4):
            po, pd = C.psum[4], C.psum[5]
            for mt in range(2):
                ps = C.psum[2 + (kk % 2)]
                pt = pb[kk % 3]
                kk += 1
                P.op("pe", lambda e, h=h, mt=mt, ps=ps: e.matmul(out=ps[:], lhsT=KT[:, h, mt * 128:(mt + 1) * 128],
                                                               rhs=qT[:, h, :], start=True, stop=True), [KT.t, qT.t], [ps.t])
                P.op("act", lambda e, ps=ps, pt=pt: e.activation(out=pt[:], in_=ps[:], func=AF.Exp), [ps.t], [pt.t])
                P.op("pe", lambda e, h=h, mt=mt, pt=pt: e.matmul(out=po[:], lhsT=Vm[:, mt, h * 128:(h + 1) * 128], rhs=pt[:],
                                                               start=(mt == 0), stop=(mt == 1)), [Vm.t, pt.t], [po.t])
                P.op("pe", lambda e, mt=mt, pt=pt: e.matmul(out=pd[:], lhsT=C.ones[:], rhs=pt[:], start=(mt == 0),
                                                          stop=(mt == 1)), [C.ones.t, pt.t], [pd.t])
            P.op("dve", lambda e, pd=pd: e.reciprocal(out=R[:], in_=pd[:]), [pd.t], [R.t])
            P.op("dve", lambda e, po=po, h=h: e.tensor_tensor(out=xoT[:, h, :], in0=po[:], in1=R[:], op=ALU.mult),
                 [po.t, R.t], [xoT.tr[h]])
        for t in range(4):
            r0 = g * 512 + t * 128
            x = xs[t]
            for hf in range(2):
                py = C.psum[6 + hf]
                for h in range(4):
                    P.op("pe", lambda e, h=h, t=t, hf=hf, py=py: e.matmul(
                        out=py[:], lhsT=xoT[:, h, t * 128:(t + 1) * 128], rhs=Wo[:, h, hf * 512:(hf + 1) * 512],
                        start=(h == 0), stop=(h == 3)), [xoT.tr[h], Wo.t], [py.t])
                P.op("dve", lambda e, hf=hf, py=py, x=x: e.tensor_tensor(out=x[:, hf * 512:(hf + 1) * 512], in0=py[:],
                                                                       in1=x[:, hf * 512:(hf + 1) * 512], op=ALU.add),
                     [py.t, x.t], [x.t])
            P.dma("sp", x_d[r0:r0 + 128, :], x[:], reads=[x.t])
    P.barrier()
    C.release(m)


def phase_mla_proj(C, x_d, gain_d, wdq_d, qln_d, wuq_d, wdkv_d, kvln_d, wukv_d, nq_d, nk_d, cs_d, scr):
    P = C.P
    m = C.mark()
    stg = [C.sb([128, 1024], F32), C.sb([128, 1024], F32)]
    Wdq = load_weight(C, wdq_d, D, 512, stg)
    Wuq = load_weight(C, wuq_d, 512, 1536, stg)
    Wdkv = load_weight(C, wdkv_d, D, 288, stg)
    Wukv = load_weight(C, wukv_d, 256, 2048, stg)
    gain = load_bcast(C, gain_d, D)
    gql = load_bcast(C, qln_d, 512)
    gkvl = load_bcast(C, kvln_d, 256)
    gq = scaled_gain(C, nq_d, 96, 96 ** -0.5)
    gk = scaled_gain(C, nk_d, 96, 1.0)
    import os
    MS = os.environ.get("MSKIP", "")
    cs = C.sb([128, C.NT, 32], F32)
    if "c" not in MS:
        P.dma("sp", cs[:], cs_d.rearrange("(t p) c -> p t c", p=128), writes=[cs.t])
    xt = [C.sb([128, D], F32) for _ in range(2)]
    hb = C.sb([128, D], BF16)
    ss = C.sb([128, 16], F32)
    rs = C.sb([128, 16], F32)
    hT = C.sb([128, 8, 128], BF16)
    cq = C.sb([128, 512], F32)
    cqb = C.sb([128, 512], BF16)
    cqT = C.sb([128, 4, 128], BF16)
    dkv = C.sb([128, 288], F32)
    ckb = C.sb([128, 256], BF16)
    ckT = C.sb([128, 2, 128], BF16)
    raw = C.sb([128, 1536], F32)
    w1 = C.sb([128, 1536], F32)
    w2 = C.sb([128, 1536], F32)
    outb = [C.sb([128, 1536], BF16) for _ in range(2)]
    tbuf = [C.sb([128, 12, 128], BF16) for _ in range(2)]
    vb = C.sb([128, 16, 64], BF16)
    kvs = [C.sb([128, 512], F32) for _ in range(2)]
    qT_d, kT_d, v_d = scr["mqT"], scr["mkT"], scr["mv"]
    for t in range(C.NT):
        x = xt[t % 2]
        P.dma("sp", x[:], x_d[t * 128:(t + 1) * 128, :], writes=[x.t])
        norm_rows(C, x, gain, hb, hb, ss, rs)
        transpose_cols(C, hb, hb.t, 8, lambda c0, c1: hT[:, c0:c1, :], [hT.t], [0, 1])
        ps = proj(C, hT, 8, Wdq, 0, 512, 2)
        copy_op(C, "act", cq[:], ps[:], [ps.t], [cq.t])
        P.op("act", lambda e: e.activation(out=w1[:, 0:512], in_=cq[:], func=AF.Square, accum_out=ss[:, 0:1]),
             [cq.t], [w1.t, ss.t])
        rstd_from_ss(C, ss, rs, 1, 512)
        P.op("dve", lambda e: e.scalar_tensor_tensor(out=cqb[:], in0=cq[:], scalar=rs[:, 0:1], in1=gql[:], op0=ALU.mult,
                                                     op1=ALU.mult), [cq.t, rs.t, gql.t], [cqb.t])
        transpose_cols(C, cqb, cqb.t, 4, lambda c0, c1: cqT[:, c0:c1, :], [cqT.t], [0, 1])
        ps = proj(C, hT, 8, Wdkv, 0, 288, 3)
        copy_op(C, "act", dkv[:], ps[:, 0:288], [ps.t], [dkv.t])
        P.op("act", lambda e: e.activation(out=w1[:, 0:256], in_=dkv[:, 0:256], func=AF.Square, accum_out=ss[:, 0:1]),
             [dkv.t, ss.t], [w1.t, ss.t])
        rstd_from_ss(C, ss, rs, 1, 256)
        P.op("dve", lambda e: e.scalar_tensor_tensor(out=ckb[:], in0=dkv[:, 0:256], scalar=rs[:, 0:1], in1=gkvl[:],
                                                     op0=ALU.mult, op1=ALU.mult), [dkv.t, rs.t, gkvl.t], [ckb.t])
        transpose_cols(C, ckb, ckb.t, 2, lambda c0, c1: ckT[:, c0:c1, :], [ckT.t], [0, 1])
        r3 = raw[:].rearrange("p (h d) -> p h d", h=16)
        if "Q" in MS:
            continue
        for g4 in range(4):
            ps = proj(C, cqT, 4, Wuq, g4 * 384, 384, 4 + g4)
            copy_op(C, "act" if g4 % 2 else "dve", raw[:, g4 * 384:(g4 + 1) * 384], ps[:, 0:384], [ps.t, raw.t], [raw.t])
        import os
        MS = os.environ.get("MSKIP", "")
        headnorm_rope(C, raw, 16, 96, gq, None if "r" in MS else cs[:, t, :], 64, 16, outb[0], w1, w2, ss, rs, cs.t)
        store_T(C, outb[0], 1536, qT_d, t, tbuf[0])
        if "K" in MS:
            continue
        for g4 in range(4):
            ps = proj(C, ckT, 2, Wukv, g4 * 512, 512, 4 + g4)
            kv = kvs[g4 % 2]
            copy_op(C, "act", kv[:], ps[:], [ps.t], [kv.t])
            p3 = kv[:].rearrange("p (h d) -> p h d", h=4)
            copy_op(C, "pool", r3[:, g4 * 4:(g4 + 1) * 4, 0:64], p3[:, :, 0:64], [kv.t, raw.t], [raw.t])
            copy_op(C, "dve", vb[:, g4 * 4:(g4 + 1) * 4, :], p3[:, :, 64:128], [kv.t, vb.t], [vb.t])
        copy_op(C, "dve" if "b" in MS else "pool", r3[:, :, 64:96], dkv[:, 256:288].unsqueeze(1).to_broadcast([128, 16, 32]), [dkv.t, raw.t], [raw.t])
        if "v" not in MS:
            P.dma("sp", v_d[t * 128:(t + 1) * 128, :], vb[:].rearrange("p h d -> p (h d)"), reads=[vb.t])
        headnorm_rope(C, raw, 16, 96, gk, None if "r" in MS else cs[:, t, :], 64, 16, outb[1], w1, w2, ss, rs, cs.t)
        store_T(C, outb[1], 1536, kT_d, t, tbuf[1])
    P.barrier()
    C.release(m)


def phase_mla_attn(C, scr):
    P = C.P
    S = C.S
    m = C.mark()
    QT = [C.sb([128, S], BF16) for _ in range(2)]
    KT = [C.sb([128, S], BF16) for _ in range(2)]
    V = [C.sb([128, C.NT, 128], BF16) for _ in range(2)]
    for v in V:
        P.op("pool", lambda e, v=v: e.memset(v[:, :, 64:128], 1.0), [], [v.t])
    pb = [C.sb([128, 512], BF16) for _ in range(5)]
    R = C.sb([128, 512], F32)
    ob = [C.sb([128, 512], BF16) for _ in range(2)]
    kc = [0]
    k = 0
    for h in range(16):
        Vh, Q, K = V[h % 2], QT[h % 2], KT[h % 2]
        for t0 in range(0, C.NT, 8):
            P.dma("sp", Vh[:, t0:min(t0 + 8, C.NT), 0:64], scr["mv"].rearrange("(t p) c -> p t c", p=128)[:, t0:min(t0 + 8, C.NT), h * 64:(h + 1) * 64],
                  reads=[Vh.t], writes=[Vh.t])
        load_head(C, scr["mqT"][h * 96:(h + 1) * 96, :], scr["mkT"][h * 96:(h + 1) * 96, :], 96, Q, K)
        for qs in range(C.NG):
            bo = 2 + (k % 2)
            k += 1
            softmax_block(C, Q, K, 96, VL(lambda j, Vh=Vh: Vh[:, j, :], Vh.t), qs, 0, pb, bo, None, kc, sbanks=(0, 1, 4, 5))
            po = C.psum[bo]
            P.op("dve", lambda e, po=po: e.reciprocal(out=R[0:64, :], in_=po[64:128, :]), [po.t], [R.t])
            o = ob[k % 2]
            P.op("dve", lambda e, po=po, o=o: e.tensor_tensor(out=o[0:64, :], in0=po[0:64, :], in1=R[0:64, :], op=ALU.mult),
                 [po.t, R.t, o.t], [o.t])
            P.dma("sp", scr["mixT"][h * 64:(h + 1) * 64, qs * 512:(qs + 1) * 512], o[0:64, :], reads=[o.t])
    P.barrier()
    C.release(m)


WNAMES = ["ffn_norm", "ffn_w_gate", "ffn_w_up", "ffn_w_down", "mix_norm", "ab_w_in", "ab_w_out", "diff_q_norm",
          "diff_k_norm", "diff_subln", "mla_w_dq", "mla_q_norm", "mla_w_uq", "mla_w_dkv", "mla_kv_norm", "mla_w_ukv",
          "mla_qk_norm_q", "mla_qk_norm_k", "mla_w_o", "xm_norm", "xm_mem_norm", "xm_w_q", "xm_w_kv", "xm_q_norm",
          "xm_k_norm", "xm_w_o"]


def build(S, shapes, phases=None):
    nc = bass.Bass("TRN2", target_bir_lowering=False)
    ins = {}
    for name, shp in shapes.items():
        ins[name] = nc.dram_tensor(name, list(shp), F32, kind="ExternalInput").ap()
    out = nc.dram_tensor("out", [S, D], F32, kind="ExternalOutput").ap()
    scr_t = nc.dram_tensor("scr", [5120 * S], BF16).ap()

    def sv(off, rows, cols):
        return scr_t[off * S:(off + rows * cols // S) * S].rearrange("(r c) -> r c", c=cols)
    scr = {"qaT": sv(0, 512, S), "kaT": sv(512, 512, S), "va": sv(1024, S, 512), "qbT": sv(1536, 512, S),
           "kbT": sv(2048, 512, S), "vb": sv(2560, S, 512),
           "mqT": sv(0, 1536, S), "mkT": sv(1536, 1536, S), "mv": sv(3072, S, 1024), "mixT": sv(4096, 1024, S)}
    C = Ctx(nc, S)
    P = C.P
    setup_consts(C, ins["c_ident"], ins["c_masks"], ins["c_tri"])
    xin = Tr()
    P.dma("sp", out, ins["x"], writes=[xin])
    P.barrier()
    w = ins
    allp = ["ffn00", "abproj", "diff", "sb", "out0", "xm0", "ffn01", "ffn10", "mlaproj", "mlaattn", "out1", "xm1", "ffn11"]
    for ph in (allp if phases is None else phases):
        if ph.startswith("ffn"):
            l, i = int(ph[3]), int(ph[4])
            phase_ffn(C, out, w["ffn_norm"][l, i], w["ffn_w_gate"][l, i], w["ffn_w_up"][l, i], w["ffn_w_down"][l, i])
        elif ph == "abproj":
            phase_ab_proj(C, out, w["mix_norm"][0], w["ab_w_in"][0], w["diff_q_norm"][0], w["diff_k_norm"][0], w["c_cs64"], scr)
        elif ph == "diff":
            phase_diff_attn(C, scr, w["c_lam"], w["diff_subln"][0], 0.8 - 0.6 * math.exp(0.0))
        elif ph == "sb":
            phase_sb_attn(C, scr)
        elif ph == "out0":
            phase_outproj(C, out, scr["mixT"], w["ab_w_out"][0])
        elif ph == "out1":
            phase_outproj(C, out, scr["mixT"], w["mla_w_o"][0])
        elif ph.startswith("xm"):
            l = int(ph[2])
            phase_xm(C, out, w["mem"], w["xm_norm"][l], w["xm_mem_norm"][l], w["xm_w_q"][l], w["xm_w_kv"][l],
                     w["xm_q_norm"][l], w["xm_k_norm"][l], w["xm_w_o"][l])
        elif ph == "mlaproj":
            phase_mla_proj(C, out, w["mix_norm"][1], w["mla_w_dq"][0], w["mla_q_norm"][0], w["mla_w_uq"][0], w["mla_w_dkv"][0],
                           w["mla_kv_norm"][0], w["mla_w_ukv"][0], w["mla_qk_norm_q"][0], w["mla_qk_norm_k"][0], w["c_cs32"], scr)
        elif ph == "mlaattn":
            phase_mla_attn(C, scr)
    P.emit(final_wait_ops=list(C.P.dma_hist[-N_DMA_SEMS:]))
    return nc


def make_consts(S):
    c = {}
    c["c_ident"] = np.eye(128, dtype=np.float32)
    kk = np.arange(128)[:, None]
    qq = np.arange(512)[None, :]
    masks = np.zeros((8, 128, 512), np.float32)
    for j in range(4):
        k = 128 * j + kk
        masks[j] = ((k // 64) <= (qq // 64)).astype(np.float32)
        masks[4 + j] = (k < qq).astype(np.float32)
    c["c_masks"] = masks
    c["c_tri"] = (np.arange(128)[:, None] >= np.arange(128)[None, :]).astype(np.float32)
    pos = np.arange(S, dtype=np.float32)[:, None]
    for d, nm in ((64, "c_cs64"), (32, "c_cs32")):
        inv = (1.0 / (np.float32(10000.0) ** (np.arange(0, d, 2, dtype=np.float32) / np.float32(d)))).astype(np.float32)
        ang = (pos * inv[None, :]).astype(np.float32)
        c[nm] = np.concatenate([np.cos(ang), np.sin(ang)], axis=1).astype(np.float32)
    return c


def prep_inputs(inputs, S):
    shared = {k: np.ascontiguousarray(np.asarray(inputs[k], dtype=np.float32)) for k in WNAMES}
    shared["c_lam"] = np.ascontiguousarray(np.concatenate([np.asarray(inputs[k], np.float32).reshape(-1) for k in
                                           ("diff_lambda_q1", "diff_lambda_k1", "diff_lambda_q2", "diff_lambda_k2")]))
    shared.update(make_consts(S))
    x = np.asarray(inputs["x"], np.float32)
    mem = np.asarray(inputs["mem"], np.float32)
    maps = []
    for b in range(x.shape[0]):
        mmap = dict(shared)
        mmap["x"] = np.ascontiguousarray(x[b])
        mmap["mem"] = np.ascontiguousarray(mem[b])
        maps.append(mmap)
    return maps


def kernel(**inputs):
    S = inputs["x"].shape[1]
    maps = prep_inputs(inputs, S)
    shapes = {k: v.shape for k, v in maps[0].items()}
    nc = build(S, shapes)
    res = run_bass_kernel_spmd(nc, maps, core_ids=list(range(len(maps))))
    return np.stack([np.asarray(r["out"], dtype=np.float32) for r in res.results], axis=0)
```

```python
import math
import numpy as np
import concourse.bass as bass
import concourse.mybir as mybir
from concourse.bass_utils import run_bass_kernel_spmd

F32 = mybir.dt.float32
BF16 = mybir.dt.bfloat16
AF = mybir.ActivationFunctionType
ALU = mybir.AluOpType
AX = mybir.AxisListType
N_DMA_SEMS = 48
EPS = 1e-6
D = 1024
DFF = 2816
NFF = 22
MEM = 256


class Tr:
    __slots__ = ("lw", "rd")

    def __init__(self):
        self.lw = None
        self.rd = []


class Op:
    __slots__ = ("eng", "fn", "deps", "needed", "cnt", "is_dma", "dslot", "dval")

    def __init__(self, eng, fn, is_dma=False):
        self.eng = eng
        self.fn = fn
        self.deps = set()
        self.needed = False
        self.cnt = 0
        self.is_dma = is_dma
        self.dslot = -1
        self.dval = 0


class Prog:
    ENGS = ("pe", "act", "dve", "pool", "sp")

    def __init__(self, nc):
        self.nc = nc
        self.streams = {e: [] for e in self.ENGS}
        self.n_dma = 0
        self.dma_hist = []
        self.since_barrier = []
        self.order = []

    def _add_deps(self, op, reads, writes):
        for t in reads:
            if t.lw is not None:
                op.deps.add(t.lw)
        for t in writes:
            if t.lw is not None:
                op.deps.add(t.lw)
            for r in t.rd:
                op.deps.add(r)
        for t in reads:
            if not op.is_dma:
                t.rd = [r for r in t.rd if r.is_dma or r.eng != op.eng]
            t.rd.append(op)
        for t in writes:
            t.lw = op
            t.rd = []
        op.deps.discard(op)

    def op(self, eng, fn, reads=(), writes=()):
        o = Op(eng, fn)
        self._add_deps(o, reads, writes)
        self.streams[eng].append(o)
        self.order.append(o)
        return o

    def dma(self, queue, out_ap, in_ap, reads=(), writes=()):
        def fn(e, out_ap=out_ap, in_ap=in_ap):
            return e.dma_start(out=out_ap, in_=in_ap)
        o = Op(queue, fn, is_dma=True)
        i = self.n_dma
        self.n_dma += 1
        o.dslot = i % N_DMA_SEMS
        o.dval = 16 * (i // N_DMA_SEMS + 1)
        if i >= N_DMA_SEMS:
            o.deps.add(self.dma_hist[i - N_DMA_SEMS])
        self.dma_hist.append(o)
        self.since_barrier.append(o)
        self._add_deps(o, reads, writes)
        self.streams[queue].append(o)
        self.order.append(o)
        return o

    def barrier(self):
        lasts = []
        for e in self.ENGS:
            for o in reversed(self.streams[e]):
                if not o.is_dma:
                    lasts.append(o)
                    break
        dmas = list(self.since_barrier)
        self.since_barrier = []
        for e in self.ENGS:
            o = Op(e, lambda eng: eng.nop())
            o.deps.update(lasts)
            o.deps.update(dmas)
            self.streams[e].append(o)
            self.order.append(o)

    def emit(self, final_wait_ops=()):
        nc = self.nc
        for e in self.ENGS:
            for o in self.streams[e]:
                for d in o.deps:
                    if not d.is_dma:
                        if d.eng == "pe" and o.eng == "pe" and not o.is_dma:
                            continue
                        d.needed = True
        for e in self.ENGS:
            c = 0
            for o in self.streams[e]:
                if (not o.is_dma) and o.needed:
                    c += 1
                    o.cnt = c
        esem = {e: nc.semaphore("s_" + e).__enter__() for e in self.ENGS}
        dsem = [nc.semaphore("d%d" % i).__enter__() for i in range(N_DMA_SEMS)]
        engobj = {"pe": nc.tensor, "act": nc.scalar, "dve": nc.vector, "pool": nc.gpsimd, "sp": nc.sync}
        seen = {e: {} for e in self.ENGS}

        def do_waits(ename, deps):
            eng = engobj[ename]
            waits = {}
            for d in deps:
                if d.is_dma:
                    key = ("d", d.dslot)
                    val = d.dval
                else:
                    if d.eng == "pe" and ename == "pe":
                        continue
                    key = ("e", d.eng)
                    val = d.cnt
                if waits.get(key, 0) < val:
                    waits[key] = val
            sn = seen[ename]
            for key, val in waits.items():
                if sn.get(key, 0) >= val:
                    continue
                sn[key] = val
                sm = dsem[key[1]] if key[0] == "d" else esem[key[1]]
                eng.wait_ge(sm, val)
        for o in self.order:
            do_waits(o.eng, o.deps)
            ins = o.fn(engobj[o.eng])
            if o.is_dma:
                ins.then_inc(dsem[o.dslot], 16)
            elif o.needed:
                ins.then_inc(esem[o.eng], 1)
        do_waits("sp", final_wait_ops)
        import os
        if os.environ.get("KDEBUG"):
            print("SEMCOUNTS", {e: max([o.cnt for o in self.streams[e]] + [0]) for e in self.ENGS}, "ndma", self.n_dma, "nops", len(self.order))


class Buf:
    def __init__(self, handle, ntr=1):
        self.h = handle
        self.tr = [Tr() for _ in range(ntr)]

    def __getitem__(self, k):
        return self.h[k]

    @property
    def t(self):
        return self.tr[0]


SB_BASE = 16640
SB_TOP = 229376


class Ctx:
    def __init__(self, nc, S):
        self.nc = nc
        self.S = S
        self.NT = S // 128
        self.NG = S // 512
        self.P = Prog(nc)
        self.off = SB_BASE
        self.uid = 0
        self.psum = [Buf(nc.alloc_psum_tensor("ps%d" % i, [128, 512], F32)) for i in range(8)]
        self.rr = 0

    def sb(self, shape, dtype, ntr=1):
        n = 1
        for s in shape[1:]:
            n *= s
        nbytes = n * (4 if dtype == F32 else 2)
        nbytes = (nbytes + 63) // 64 * 64
        assert self.off + nbytes <= SB_TOP, ("SBUF overflow", self.off, nbytes)
        self.uid += 1
        h = self.nc.alloc_sbuf_tensor_at("t%d" % self.uid, list(shape), dtype, offset=self.off)
        self.off += nbytes
        return Buf(h, ntr)

    def mark(self):
        return self.off

    def release(self, m):
        self.off = m

    def eng_rr(self, engs=("dve", "pool")):
        self.rr += 1
        return engs[self.rr % len(engs)]


def copy_op(C, eng, out_ap, in_ap, reads, writes):
    if eng == "act":
        return C.P.op("act", lambda e: e.copy(out=out_ap, in_=in_ap), reads, writes)
    return C.P.op(eng, lambda e: e.tensor_copy(out=out_ap, in_=in_ap), reads, writes)


def load_weight(C, w_dram, K, N, stg, col0=0, ncols=None, engs=("dve", "pool")):
    ncols = N if ncols is None else ncols
    kc = (K + 127) // 128
    W = C.sb([128, kc, ncols], BF16)
    CH = stg[0].h.shape[1]
    i = 0
    for c in range(kc):
        rows = min(128, K - c * 128)
        for n0 in range(0, ncols, CH):
            n1 = min(ncols, n0 + CH)
            s = stg[C.rr % len(stg)]
            C.P.dma("sp", s[0:rows, 0:n1 - n0], w_dram[c * 128:c * 128 + rows, col0 + n0:col0 + n1], writes=[s.t])
            copy_op(C, C.eng_rr(engs), W[0:rows, c, n0:n1], s[0:rows, 0:n1 - n0], [s.t, W.t], [W.t])
    return W


def load_bcast(C, vec_dram, n):
    b = C.sb([128, n], F32)
    C.P.dma("sp", b[:], vec_dram.partition_broadcast(128), writes=[b.t])
    return b


def rstd_from_ss(C, ss, rs, n, dim):
    P = C.P
    P.op("dve", lambda e: e.tensor_scalar(out=ss[:, 0:n], in0=ss[:, 0:n], scalar1=1.0 / dim, scalar2=EPS,
                                          op0=ALU.mult, op1=ALU.add), [ss.t], [ss.t])
    P.op("act", lambda e: e.activation(out=ss[:, 0:n], in_=ss[:, 0:n], func=AF.Sqrt), [ss.t], [ss.t])
    P.op("dve", lambda e: e.reciprocal(out=rs[:, 0:n], in_=ss[:, 0:n]), [ss.t], [rs.t])


def norm_rows(C, xt, gain, hb, junk, ss, rs):
    P = C.P
    P.op("act", lambda e: e.activation(out=junk[:], in_=xt[:], func=AF.Square, accum_out=ss[:, 0:1]),
         [xt.t], [junk.t, ss.t])
    rstd_from_ss(C, ss, rs, 1, D)
    P.op("dve", lambda e: e.scalar_tensor_tensor(out=hb[:], in0=xt[:], scalar=rs[:, 0:1], in1=gain[:],
                                                 op0=ALU.mult, op1=ALU.mult), [xt.t, rs.t, gain.t], [hb.t])


def transpose_cols(C, src, src_tr, ncol, dst_fn, dst_tr, banks, ceng=("act", "dve")):
    P = C.P
    for b0 in range(0, ncol, 4):
        b1 = min(ncol, b0 + 4)
        ps = C.psum[banks[(b0 // 4) % len(banks)]]
        for c in range(b0, b1):
            P.op("pe", lambda e, c=c, ps=ps, b0=b0: e.matmul(out=ps[:, (c - b0) * 128:(c - b0 + 1) * 128],
                                                           lhsT=src[:, c * 128:(c + 1) * 128], rhs=C.ident[:],
                                                           start=True, stop=True),
                 [src_tr, C.ident.t], [ps.t])
        n = b1 - b0
        copy_op(C, C.eng_rr(ceng), dst_fn(b0, b1), ps[:, 0:n * 128].rearrange("p (c t) -> p c t", c=n),
                [ps.t] + list(dst_tr), list(dst_tr))


def setup_consts(C, ident_d, masks_d, tri_d):
    P = C.P
    C.ident = C.sb([128, 128], BF16)
    C.ones = C.sb([128, 128], BF16)
    C.masks = C.sb([128, 8, 512], BF16)
    C.trineg = C.sb([128, 128], BF16)
    C.onesneg = C.sb([128, 128], BF16)
    m = C.mark()
    idf = C.sb([128, 128], F32)
    trf = C.sb([128, 128], F32)
    mf = C.sb([128, 8, 512], F32)
    P.dma("sp", idf[:], ident_d, writes=[idf.t])
    P.dma("sp", trf[:], tri_d, writes=[trf.t])
    P.dma("sp", mf[:], masks_d.rearrange("m p q -> p m q"), writes=[mf.t])
    copy_op(C, "dve", C.ident[:], idf[:], [idf.t], [C.ident.t])
    P.op("dve", lambda e: e.tensor_scalar(out=C.trineg[:], in0=trf[:], scalar1=-1.0, scalar2=None, op0=ALU.mult),
         [trf.t], [C.trineg.t])
    copy_op(C, "pool", C.masks[:], mf[:], [mf.t], [C.masks.t])
    P.op("pool", lambda e: e.memset(C.ones[:], 1.0), [], [C.ones.t])
    P.op("pool", lambda e: e.memset(C.onesneg[:], -1.0), [], [C.onesneg.t])
    P.barrier()
    C.release(m)


def phase_ffn(C, x_d, gain_d, wg_d, wu_d, wd_d):
    P = C.P
    m = C.mark()
    stg = [C.sb([128, 1024], F32), C.sb([128, 1024], F32)]
    Wg = load_weight(C, wg_d, D, DFF, stg)
    Wu = load_weight(C, wu_d, D, DFF, stg)
    Wd = load_weight(C, wd_d, DFF, D, stg)
    gain = load_bcast(C, gain_d, D)
    xs = [C.sb([128, D], F32) for _ in range(4)]
    hb = C.sb([128, D], BF16)
    junk = hb
    ss = C.sb([128, 8], F32)
    rs = C.sb([128, 8], F32)
    hT = C.sb([128, 8, 512], BF16)
    actT = C.sb([128, NFF, 512], BF16, ntr=NFF)
    sg = [C.sb([128, 512], F32), C.sb([128, 512], F32)]
    for g in range(C.NG):
        for t in range(4):
            r0 = g * 512 + t * 128
            P.dma("sp", xs[t][:], x_d[r0:r0 + 128, :], writes=[xs[t].t])
            norm_rows(C, xs[t], gain, hb, junk, ss, rs)
            transpose_cols(C, hb, hb.t, 8, lambda c0, c1, t=t: hT[:, c0:c1, t * 128:(t + 1) * 128], [hT.t], [0, 1])
        for f in range(NFF):
            pg = C.psum[2 + (f % 2)]
            pu = C.psum[4 + (f % 2)]
            for c in range(8):
                P.op("pe", lambda e, c=c, f=f, pg=pg: e.matmul(out=pg[:], lhsT=Wg[:, c, f * 128:(f + 1) * 128],
                                                             rhs=hT[:, c, :], start=(c == 0), stop=(c == 7)),
                     [Wg.t, hT.t], [pg.t])
            for c in range(8):
                P.op("pe", lambda e, c=c, f=f, pu=pu: e.matmul(out=pu[:], lhsT=Wu[:, c, f * 128:(f + 1) * 128],
                                                             rhs=hT[:, c, :], start=(c == 0), stop=(c == 7)),
                     [Wu.t, hT.t], [pu.t])
            s = sg[f % 2]
            P.op("act", lambda e, s=s, pg=pg: e.activation(out=s[:], in_=pg[:], func=AF.Silu), [pg.t], [s.t])
            P.op("dve", lambda e, s=s, pu=pu, f=f: e.tensor_tensor(out=actT[:, f, :], in0=pu[:], in1=s[:], op=ALU.mult),
                 [pu.t, s.t], [actT.tr[f]])
        for t in range(4):
            for h in range(2):
                py = C.psum[6 + h]
                for f in range(NFF):
                    P.op("pe", lambda e, f=f, t=t, h=h, py=py: e.matmul(out=py[:], lhsT=actT[:, f, t * 128:(t + 1) * 128],
                                                                        rhs=Wd[:, f, h * 512:(h + 1) * 512],
                                                                        start=(f == 0), stop=(f == NFF - 1)),
                         [actT.tr[f], Wd.t], [py.t])
                P.op("dve", lambda e, t=t, h=h, py=py: e.scalar_tensor_tensor(
                    out=xs[t][:, h * 512:(h + 1) * 512], in0=py[:], scalar=0.5, in1=xs[t][:, h * 512:(h + 1) * 512],
                    op0=ALU.mult, op1=ALU.add), [py.t, xs[t].t], [xs[t].t])
            r0 = g * 512 + t * 128
            P.dma("sp", x_d[r0:r0 + 128, :], xs[t][:], reads=[xs[t].t])
    P.barrier()
    C.release(m)


def headnorm_rope(C, raw, nh, dh, gain_b, cs_ap, off, half, outb, w1, w2, ss, rs, cs_tr=None):
    P = C.P
    n = nh * dh
    r3 = raw[:, 0:n].rearrange("p (h d) -> p h d", h=nh)
    o3 = outb[:, 0:n].rearrange("p (h d) -> p h d", h=nh)
    a3 = w1[:, 0:n].rearrange("p (h d) -> p h d", h=nh)
    b3 = w2[:, 0:n].rearrange("p (h d) -> p h d", h=nh)
    P.op("act", lambda e: e.activation(out=w1[:, 0:n], in_=raw[:, 0:n], func=AF.Square), [raw.t], [w1.t])
    P.op("dve", lambda e: e.tensor_reduce(out=ss[:, 0:nh], in_=a3, axis=AX.X, op=ALU.add), [w1.t], [ss.t])
    rstd_from_ss(C, ss, rs, nh, dh)
    P.op("dve", lambda e: e.tensor_tensor(out=r3, in0=r3, in1=rs[:, 0:nh].unsqueeze(2).to_broadcast([128, nh, dh]),
                                          op=ALU.mult), [raw.t, rs.t], [raw.t])
    P.op("pool", lambda e: e.tensor_tensor(out=r3, in0=r3, in1=gain_b[:, 0:dh].unsqueeze(1).to_broadcast([128, nh, dh]),
                                           op=ALU.mult), [raw.t, gain_b.t], [raw.t])
    if cs_ap is None:
        copy_op(C, "act", outb[:, 0:n], raw[:, 0:n], [raw.t], [outb.t])
        return
    cos = cs_ap[:, 0:half].unsqueeze(1).to_broadcast([128, nh, half])
    sin = cs_ap[:, half:2 * half].unsqueeze(1).to_broadcast([128, nh, half])
    x1 = r3[:, :, off:off + half]
    x2 = r3[:, :, off + half:off + 2 * half]
    if off > 0:
        copy_op(C, "act", o3[:, :, 0:off], r3[:, :, 0:off], [raw.t, outb.t], [outb.t])
    P.op("pool", lambda e: e.tensor_tensor(out=a3[:, :, 0:half], in0=x1, in1=cos, op=ALU.mult), [raw.t, w1.t, cs_tr], [w1.t])
    P.op("dve", lambda e: e.tensor_tensor(out=b3[:, :, 0:half], in0=x2, in1=sin, op=ALU.mult), [raw.t, w2.t, cs_tr], [w2.t])
    P.op("pool", lambda e: e.tensor_tensor(out=a3[:, :, half:2 * half], in0=x1, in1=sin, op=ALU.mult), [raw.t, w1.t, cs_tr], [w1.t])
    P.op("dve", lambda e: e.tensor_tensor(out=b3[:, :, half:2 * half], in0=x2, in1=cos, op=ALU.mult), [raw.t, w2.t, cs_tr], [w2.t])
    P.op("dve", lambda e: e.tensor_tensor(out=o3[:, :, off:off + half], in0=a3[:, :, 0:half], in1=b3[:, :, 0:half],
                                          op=ALU.subtract), [w1.t, w2.t, outb.t], [outb.t])
    P.op("pool", lambda e: e.tensor_tensor(out=o3[:, :, off + half:off + 2 * half], in0=a3[:, :, half:2 * half],
                                           in1=b3[:, :, half:2 * half], op=ALU.add), [w1.t, w2.t, outb.t], [outb.t])


def store_T(C, outb, ncols, dstT, t, tbuf):
    import os
    nchunk = ncols // 128
    transpose_cols(C, outb, outb.t, nchunk, lambda c0, c1: tbuf[:, c0:c1, :], [tbuf.t], [0, 1])
    if "s" in os.environ.get("MSKIP", ""):
        return
    for c0 in range(0, nchunk, 4):
        C.P.dma("sp", dstT.rearrange("(c p) s -> p c s", p=128)[:, c0:c0 + 4, t * 128:(t + 1) * 128],
                tbuf[:, c0:c0 + 4, :], reads=[tbuf.t])


def proj(C, hT, kc, W, c0, n, bank, tok0=0, ntok=128):
    ps = C.psum[bank]
    for c in range(kc):
        C.P.op("pe", lambda e, c=c: e.matmul(out=ps[0:ntok, 0:n], lhsT=hT[:, c, tok0:tok0 + ntok], rhs=W[:, c, c0:c0 + n],
                                             start=(c == 0), stop=(c == kc - 1)), [hT.t, W.t], [ps.t])
    return ps


def scaled_gain(C, vec_d, n, scale):
    g = load_bcast(C, vec_d, n)
    if scale != 1.0:
        C.P.op("dve", lambda e: e.tensor_scalar(out=g[:], in0=g[:], scalar1=float(scale), scalar2=None, op0=ALU.mult),
               [g.t], [g.t])
    return g


def phase_ab_proj(C, x_d, gain_d, w_in_d, qn_d, kn_d, cs_d, scr):
    P = C.P
    S = C.S
    m = C.mark()
    stg = [C.sb([128, 1024], F32), C.sb([128, 1024], F32)]
    W = load_weight(C, w_in_d, D, 3072, stg)
    gain = load_bcast(C, gain_d, D)
    gq = scaled_gain(C, qn_d, 64, 0.125)
    gk = scaled_gain(C, kn_d, 64, 1.0)
    cs = C.sb([128, C.NT, 64], F32)
    P.dma("sp", cs[:], cs_d.rearrange("(t p) c -> p t c", p=128), writes=[cs.t])
    xt = [C.sb([128, D], F32) for _ in range(2)]
    hbL = [C.sb([128, D], BF16) for _ in range(2)]
    ssL = [C.sb([128, 16], F32) for _ in range(2)]
    rsL = [C.sb([128, 16], F32) for _ in range(2)]
    hTL = [C.sb([128, 8, 128], BF16) for _ in range(2)]
    sets = [[(C.sb([128, 512], F32), C.sb([128, 512], F32), C.sb([128, 512], F32), C.sb([128, 16], F32), C.sb([128, 16], F32))
             for _ in range(2)] for _ in range(2)]
    outbL = [[C.sb([128, 512], BF16) for _ in range(2)] for _ in range(2)]
    tbufL = [[C.sb([128, 4, 128], BF16) for _ in range(2)] for _ in range(2)]
    vb = [C.sb([128, 512], BF16) for _ in range(2)]
    outb2 = [[C.sb([128, 512], BF16) for _ in range(2)] for _ in range(2)]
    tbuf2 = [[C.sb([128, 4, 128], BF16) for _ in range(2)] for _ in range(2)]
    qaT, kaT, va, qbT, kbT, vbd = scr["qaT"], scr["kaT"], scr["va"], scr["qbT"], scr["kbT"], scr["vb"]
    kk = [0]

    def stage1(t):
        x = xt[t % 2]
        pp_ = t % 2
        hb, ss, rs, hT = hbL[pp_], ssL[pp_], rsL[pp_], hTL[pp_]
        P.dma("sp", x[:], x_d[t * 128:(t + 1) * 128, :], writes=[x.t])
        norm_rows(C, x, gain, hb, hb, ss, rs)
        transpose_cols(C, hb, hb.t, 8, lambda c0, c1: hT[:, c0:c1, :], [hT.t], [0, 1])

    def stage2(t):
        pp_ = t % 2
        hT = hTL[pp_]
        outb, tbuf = [outbL[0][pp_], outbL[1][pp_]], [tbufL[0][pp_], tbufL[1][pp_]]
        for grp in range(6):
            ps = proj(C, hT, 8, W, grp * 512, 512, 2 + (kk[0] % 6))
            kk[0] += 1
            if grp in (0, 1):
                raw, w1, w2, ss2, rs2 = sets[grp][pp_]
                copy_op(C, "act", raw[:], ps[:], [ps.t], [raw.t])
                ob = outb[grp]
                headnorm_rope(C, raw, 8, 64, gq if grp == 0 else gk, cs[:, t, :], 0, 32, ob, w1, w2, ss2, rs2, cs.t)
                store_T(C, ob, 512, qaT if grp == 0 else kaT, t, tbuf[grp])
            elif grp in (2, 5):
                v = vb[0 if grp == 2 else 1]
                copy_op(C, "act", v[:], ps[:], [ps.t, v.t], [v.t])
                P.dma("sp", (va if grp == 2 else vbd)[t * 128:(t + 1) * 128, :], v[:], reads=[v.t])
            else:
                ob = outb2[grp - 3][pp_]
                if grp == 3:
                    P.op("act", lambda e, ob=ob, ps=ps: e.activation(out=ob[:], in_=ps[:], func=AF.Copy, scale=0.125),
                         [ps.t, ob.t], [ob.t])
                else:
                    copy_op(C, "dve", ob[:], ps[:], [ps.t, ob.t], [ob.t])
                store_T(C, ob, 512, qbT if grp == 3 else kbT, t, tbuf2[grp - 3][pp_])
    for t in range(C.NT + 1):
        if t < C.NT:
            stage1(t)
        if t >= 1:
            stage2(t - 1)
    P.barrier()
    C.release(m)


def load_head(C, qT_d, kT_d, dk, QT, KT):
    C.P.dma("sp", QT[0:dk, :], qT_d, reads=[QT.t], writes=[QT.t])
    C.P.dma("sp", KT[0:dk, :], kT_d, reads=[KT.t], writes=[KT.t])


def softmax_block(C, QT, KT, dk, Vl, qs, mask0, pbufs, bank_o, bank_d, k_ctr, sbanks=(0, 1)):
    P = C.P
    nj = 4 * qs + 4
    po = C.psum[bank_o]
    pd = C.psum[bank_d] if bank_d is not None else None
    st = {}

    def stage_a(j):
        ps = C.psum[sbanks[k_ctr[0] % len(sbanks)]]
        pt = pbufs[k_ctr[0] % len(pbufs)]
        k_ctr[0] += 1
        st[j] = (ps, pt)
        P.op("pe", lambda e, j=j, ps=ps: e.matmul(out=ps[:], lhsT=KT[0:dk, j * 128:(j + 1) * 128],
                                                 rhs=QT[0:dk, qs * 512:(qs + 1) * 512], start=True, stop=True),
             [KT.t, QT.t], [ps.t])

    def stage_b(j):
        ps, pt = st.pop(j)
        P.op("act", lambda e, ps=ps, pt=pt: e.activation(out=pt[:], in_=ps[:], func=AF.Exp), [ps.t], [pt.t])
        if j >= 4 * qs:
            mi = mask0 + j - 4 * qs
            P.op("pool", lambda e, pt=pt, mi=mi: e.tensor_tensor(out=pt[:], in0=pt[:], in1=C.masks[:, mi, :], op=ALU.mult),
                 [pt.t, C.masks.t], [pt.t])
        P.op("pe", lambda e, j=j, pt=pt: e.matmul(out=po[:], lhsT=Vl(j), rhs=pt[:], start=(j == 0), stop=(j == nj - 1)),
             [pt.t, Vl.tr], [po.t])
        if pd is not None:
            P.op("pe", lambda e, j=j, pt=pt: e.matmul(out=pd[:], lhsT=C.ones[:], rhs=pt[:], start=(j == 0),
                                                     stop=(j == nj - 1)), [pt.t, C.ones.t], [pd.t])
    la = min(2, len(sbanks) - 1)
    for j in range(min(la, nj)):
        stage_a(j)
    for j in range(nj):
        if j + la < nj:
            stage_a(j + la)
        stage_b(j)


class VL:
    def __init__(self, fn, tr):
        self.fn = fn
        self.tr = tr

    def __call__(self, j):
        return self.fn(j)


def phase_diff_attn(C, scr, lam_d, subln_d, lambda_init):
    P = C.P
    S = C.S
    m = C.mark()
    QT = [C.sb([128, S], BF16) for _ in range(2)]
    KT = [C.sb([128, S], BF16) for _ in range(2)]
    V = [C.sb([128, C.NT, 128], BF16) for _ in range(2)]
    pb = [C.sb([128, 512], BF16) for _ in range(4)]
    A = C.sb([128, 512], F32)
    B = C.sb([128, 512], F32)
    R = C.sb([128, 512], F32)
    ob = [C.sb([128, 512], BF16) for _ in range(2)]
    Bh = C.sb([128, 512], BF16)
    Bl = C.sb([128, 512], BF16)
    lv = C.sb([128, 4, 64], F32)
    P.dma("sp", lv[:].rearrange("p a b -> p (a b)"), lam_d.partition_broadcast(128), writes=[lv.t])
    lt = C.sb([128, 2, 64], F32)
    ls = C.sb([128, 4], F32)
    P.op("dve", lambda e: e.tensor_tensor(out=lt[:], in0=lv[:, 0:4:2, :], in1=lv[:, 1:4:2, :], op=ALU.mult), [lv.t], [lt.t])
    P.op("dve", lambda e: e.tensor_reduce(out=ls[:, 0:2], in_=lt[:], axis=AX.X, op=ALU.add), [lt.t], [ls.t])
    P.op("act", lambda e: e.activation(out=ls[:, 0:2], in_=ls[:, 0:2], func=AF.Exp), [ls.t], [ls.t])
    P.op("dve", lambda e: e.tensor_tensor(out=ls[:, 2:3], in0=ls[:, 1:2], in1=ls[:, 0:1], op=ALU.subtract), [ls.t], [ls.t])
    P.op("dve", lambda e: e.tensor_scalar(out=ls[:, 3:4], in0=ls[:, 2:3], scalar1=-float(lambda_init), scalar2=None,
                                          op0=ALU.add), [ls.t], [ls.t])
    sub = C.sb([128, 1], F32)
    P.dma("sp", sub[:], subln_d.rearrange("(p o) -> p o", o=1), writes=[sub.t])
    P.op("dve", lambda e: e.tensor_scalar(out=sub[:], in0=sub[:], scalar1=float(1.0 - lambda_init), scalar2=None,
                                          op0=ALU.mult), [sub.t], [sub.t])
    kc = [0]
    for h in range(4):
        Vh = V[h % 2]
        for t0 in range(0, C.NT, 8):
            P.dma("sp", Vh[:, t0:min(t0 + 8, C.NT), :], scr["va"].rearrange("(t p) c -> p t c", p=128)[:, t0:min(t0 + 8, C.NT), h * 128:(h + 1) * 128],
                  reads=[Vh.t], writes=[Vh.t])
        for mp in range(2):
            hh = h + 4 * mp
            load_head(C, scr["qaT"][hh * 64:(hh + 1) * 64, :], scr["kaT"][hh * 64:(hh + 1) * 64, :], 64, QT[mp], KT[mp])
        for qs in range(C.NG):
            for mp in range(2):
                softmax_block(C, QT[mp], KT[mp], 64, VL(lambda j, Vh=Vh: Vh[:, j, :], Vh.t), qs, 0, pb, 2 + 2 * mp, 3 + 2 * mp, kc, sbanks=(0, 1, 7))
                po, pd = C.psum[2 + 2 * mp], C.psum[3 + 2 * mp]
                P.op("dve", lambda e, pd=pd: e.reciprocal(out=R[:], in_=pd[:]), [pd.t], [R.t])
                dst = A if mp == 0 else B
                P.op("dve", lambda e, po=po, dst=dst: e.tensor_tensor(out=dst[:], in0=po[:], in1=R[:], op=ALU.mult),
                     [po.t, R.t], [dst.t])
            P.op("dve", lambda e: e.scalar_tensor_tensor(out=A[:], in0=B[:], scalar=ls[:, 3:4], in1=A[:], op0=ALU.mult,
                                                         op1=ALU.add), [A.t, B.t, ls.t], [A.t])
            P.op("act", lambda e: e.activation(out=B[:], in_=A[:], func=AF.Square), [A.t], [B.t])
            pq = C.psum[6]
            P.op("dve", lambda e: e.tensor_copy(out=Bh[:], in_=B[:]), [B.t, Bh.t], [Bh.t])
            P.op("dve", lambda e: e.tensor_tensor(out=Bl[:], in0=B[:], in1=Bh[:], op=ALU.subtract), [B.t, Bh.t, Bl.t], [Bl.t])
            P.op("pe", lambda e, pq=pq: e.matmul(out=pq[:], lhsT=C.ones[:], rhs=Bh[:], start=True, stop=False),
                 [C.ones.t, Bh.t], [pq.t])
            P.op("pe", lambda e, pq=pq: e.matmul(out=pq[:], lhsT=C.ones[:], rhs=Bl[:], start=False, stop=True),
                 [C.ones.t, Bl.t], [pq.t])
            P.op("dve", lambda e, pq=pq: e.tensor_scalar(out=R[:], in0=pq[:], scalar1=1.0 / 128, scalar2=EPS, op0=ALU.mult,
                                                         op1=ALU.add), [pq.t], [R.t])
            P.op("act", lambda e: e.activation(out=R[:], in_=R[:], func=AF.Sqrt), [R.t], [R.t])
            P.op("dve", lambda e: e.reciprocal(out=R[:], in_=R[:]), [R.t], [R.t])
            o = ob[qs % 2]
            P.op("dve", lambda e, o=o: e.scalar_tensor_tensor(out=o[:], in0=A[:], scalar=sub[:, 0:1], in1=R[:], op0=ALU.mult,
                                                              op1=ALU.mult), [A.t, R.t, sub.t, o.t], [o.t])
            P.dma("sp", scr["mixT"][h * 128:(h + 1) * 128, qs * 512:(qs + 1) * 512], o[:], reads=[o.t])
    P.barrier()
    C.release(m)


def phase_sb_attn(C, scr):
    P = C.P
    S = C.S
    m = C.mark()
    QT = [C.sb([128, S], BF16) for _ in range(2)]
    KT = [C.sb([128, S], BF16) for _ in range(2)]
    V = [C.sb([128, C.NT, 128], BF16) for _ in range(2)]
    for b_ in QT + KT:
        P.op("pool", lambda e, b_=b_: e.memset(b_[64:128, :], 0.0), [], [b_.t])
    for v in V:
        P.op("pool", lambda e, v=v: e.memset(v[:, :, 64:128], 0.0), [], [v.t])
    ef = [C.sb([128, 512], F32) for _ in range(3)]
    sp = [C.sb([128, 512], BF16) for _ in range(3)]
    wb = [C.sb([128, 512], BF16) for _ in range(3)]
    Ls = C.sb([128, 512], F32)
    Lb = [C.sb([128, 512], BF16) for _ in range(4)]
    ob = [C.sb([128, 512], BF16) for _ in range(2)]
    k = 0
    for h in range(8):
        Vh, Q, K = V[h % 2], QT[h % 2], KT[h % 2]
        for t0 in range(0, C.NT, 8):
            P.dma("sp", Vh[:, t0:min(t0 + 8, C.NT), 0:64], scr["vb"].rearrange("(t p) c -> p t c", p=128)[:, t0:min(t0 + 8, C.NT), h * 64:(h + 1) * 64],
                  reads=[Vh.t], writes=[Vh.t])
        load_head(C, scr["qbT"][h * 64:(h + 1) * 64, :], scr["kbT"][h * 64:(h + 1) * 64, :], 64, Q, K)
        for qs in range(C.NG):
            nj = 4 * qs + 4
            po = C.psum[6 + (qs % 2)]
            js = list(range(nj - 1, -1, -1))
            st = {}

            def stage_a(idx, js=js, qs=qs, Q=Q, K=K, st=st):
                nonlocal k
                j = js[idx]
                pz = C.psum[k % 3]
                pc = C.psum[3 + (k % 3)]
                e_, s_, w_ = ef[k % 3], sp[k % 3], wb[k % 3]
                lb_in = Lb[idx % 4]
                lb_out = Lb[(idx + 1) % 4]
                k += 1
                st[idx] = (j, pc, s_, w_, lb_in)
                diag = j >= 4 * qs
                mi = 4 + j - 4 * qs
                for pp in (pz, pc):
                    P.op("pe", lambda e, j=j, pp=pp, last=(pp is pz): e.matmul(
                        out=pp[:], lhsT=K[:, j * 128:(j + 1) * 128], rhs=Q[:, qs * 512:(qs + 1) * 512],
                        start=True, stop=last), [K.t, Q.t], [pp.t])
                P.op("act", lambda e, pz=pz, e_=e_: e.activation(out=e_[:], in_=pz[:], func=AF.Exp), [pz.t], [e_.t])
                P.op("act", lambda e, e_=e_, s_=s_: e.activation(out=s_[:], in_=e_[:], func=AF.Ln, bias=1.0, scale=1.0),
                     [e_.t], [s_.t])
                if diag:
                    P.op("pool", lambda e, s_=s_, mi=mi: e.tensor_tensor(out=s_[:], in0=s_[:], in1=C.masks[:, mi, :],
                                                                        op=ALU.mult), [s_.t, C.masks.t], [s_.t])
                if idx + 1 < len(js):
                    if idx == 0:
                        copy_op(C, "dve", Ls[:], s_[:], [s_.t, Ls.t], [Ls.t])
                    else:
                        P.op("dve", lambda e, s_=s_: e.tensor_tensor(out=Ls[:], in0=Ls[:], in1=s_[:], op=ALU.add),
                             [Ls.t, s_.t], [Ls.t])
                    copy_op(C, "dve", lb_out[:], Ls[:], [Ls.t, lb_out.t], [lb_out.t])

            def stage_b(idx, js=js, qs=qs, st=st, po=po, Vh=Vh):
                j, pc, s_, w_, lb = st.pop(idx)
                first = idx == 0
                diag = j >= 4 * qs
                mi = 4 + j - 4 * qs
                P.op("pe", lambda e, pc=pc, s_=s_, first=first: e.matmul(out=pc[:], lhsT=C.trineg[:], rhs=s_[:], start=False,
                                                                        stop=first), [C.trineg.t, s_.t], [pc.t])
                if not first:
                    P.op("pe", lambda e, pc=pc, lb=lb: e.matmul(out=pc[:], lhsT=C.onesneg[:], rhs=lb[:], start=False, stop=True),
                         [C.onesneg.t, lb.t], [pc.t])
                P.op("act", lambda e, pc=pc, w_=w_: e.activation(out=w_[:], in_=pc[:], func=AF.Exp), [pc.t], [w_.t])
                if diag:
                    P.op("pool", lambda e, w_=w_, mi=mi: e.tensor_tensor(out=w_[:], in0=w_[:], in1=C.masks[:, mi, :],
                                                                        op=ALU.mult), [w_.t, C.masks.t], [w_.t])
                P.op("pe", lambda e, j=j, w_=w_, first=first, po=po, Vh=Vh: e.matmul(out=po[:], lhsT=Vh[:, j, :], rhs=w_[:],
                                                                                    start=first, stop=(j == 0)), [Vh.t, w_.t], [po.t])
            stage_a(0)
            stage_a(1)
            for idx in range(nj):
                if idx + 2 < nj:
                    stage_a(idx + 2)
                stage_b(idx)
            o = ob[qs % 2]
            copy_op(C, "dve", o[0:64, :], po[0:64, :], [po.t, o.t], [o.t])
            P.dma("sp", scr["mixT"][512 + h * 64:512 + (h + 1) * 64, qs * 512:(qs + 1) * 512], o[0:64, :], reads=[o.t])
    P.barrier()
    C.release(m)


def phase_outproj(C, x_d, mixT_d, w_d):
    P = C.P
    m = C.mark()
    stg = [C.sb([128, 1024], F32), C.sb([128, 1024], F32)]
    W = load_weight(C, w_d, D, D, stg)
    mT = [C.sb([128, 8, 512], BF16) for _ in range(2)]
    xs = [C.sb([128, D], F32) for _ in range(3)]
    k = 0
    for g in range(C.NG):
        mt = mT[g % 2]
        P.dma("sp", mt[:], mixT_d.rearrange("(c p) s -> p c s", p=128)[:, :, g * 512:(g + 1) * 512], writes=[mt.t])
        for t in range(4):
            r0 = g * 512 + t * 128
            x = xs[k % 3]
            k += 1
            P.dma("sp", x[:], x_d[r0:r0 + 128, :], writes=[x.t])
            for h in range(2):
                py = C.psum[2 * (k % 2) + h]
                for c in range(8):
                    P.op("pe", lambda e, c=c, t=t, h=h, py=py, mt=mt: e.matmul(
                        out=py[:], lhsT=mt[:, c, t * 128:(t + 1) * 128], rhs=W[:, c, h * 512:(h + 1) * 512],
                        start=(c == 0), stop=(c == 7)), [mt.t, W.t], [py.t])
                P.op("dve", lambda e, h=h, py=py, x=x: e.tensor_tensor(out=x[:, h * 512:(h + 1) * 512], in0=py[:],
                                                                     in1=x[:, h * 512:(h + 1) * 512], op=ALU.add),
                     [py.t, x.t], [x.t])
            P.dma("sp", x_d[r0:r0 + 128, :], x[:], reads=[x.t])
    P.barrier()
    C.release(m)


def phase_xm(C, x_d, mem_d, xn_d, mn_d, wq_d, wkv_d, qn_d, kn_d, wo_d):
    P = C.P
    m = C.mark()
    stg = [C.sb([128, 1024], F32), C.sb([128, 1024], F32)]
    Wkv = load_weight(C, wkv_d, D, D, stg)
    gm = load_bcast(C, mn_d, D)
    gk = scaled_gain(C, kn_d, 128, 1.0)
    gq = scaled_gain(C, qn_d, 128, 128 ** -0.5)
    hb = C.sb([128, D], BF16)
    ss = C.sb([128, 8], F32)
    rs = C.sb([128, 8], F32)
    hT = C.sb([128, 8, 128], BF16)
    raw = C.sb([128, 512], F32)
    w1 = C.sb([128, 512], F32)
    outb = C.sb([128, 512], BF16)
    KT = C.sb([128, 4, MEM], BF16)
    Vm = C.sb([128, 2, 512], BF16)
    xs = [C.sb([128, D], F32) for _ in range(4)]
    for t in range(2):
        x = xs[t]
        P.dma("sp", x[:], mem_d[t * 128:(t + 1) * 128, :], writes=[x.t])
        norm_rows(C, x, gm, hb, hb, ss, rs)
        transpose_cols(C, hb, hb.t, 8, lambda c0, c1: hT[:, c0:c1, :], [hT.t], [0, 1])
        ps = proj(C, hT, 8, Wkv, 0, 512, 2)
        copy_op(C, "act", raw[:], ps[:], [ps.t], [raw.t])
        headnorm_rope(C, raw, 4, 128, gk, None, 0, 0, outb, w1, w1, ss, rs)
        transpose_cols(C, outb, outb.t, 4, lambda c0, c1, t=t: KT[:, c0:c1, t * 128:(t + 1) * 128], [KT.t], [0, 1])
        ps = proj(C, hT, 8, Wkv, 512, 512, 3)
        copy_op(C, "act", Vm[:, t, :], ps[:], [ps.t, Vm.t], [Vm.t])
    Wq = load_weight(C, wq_d, D, 512, stg)
    Wo = load_weight(C, wo_d, 512, D, stg)
    gx = load_bcast(C, xn_d, D)
    hT4 = C.sb([128, 8, 512], BF16)
    qT = C.sb([128, 4, 512], BF16)
    xoT = C.sb([128, 4, 512], BF16, ntr=4)
    pb = [C.sb([128, 512], BF16) for _ in range(3)]
    R = C.sb([128, 512], F32)
    kk = 0
    for g in range(C.NG):
        for t in range(4):
            r0 = g * 512 + t * 128
            x = xs[t]
            P.dma("sp", x[:], x_d[r0:r0 + 128, :], writes=[x.t])
            norm_rows(C, x, gx, hb, hb, ss, rs)
            transpose_cols(C, hb, hb.t, 8, lambda c0, c1, t=t: hT4[:, c0:c1, t * 128:(t + 1) * 128], [hT4.t], [0, 1])
            ps = proj(C, hT4, 8, Wq, 0, 512, 2 + (t % 2), tok0=t * 128)
            copy_op(C, "act", raw[:], ps[:], [ps.t], [raw.t])
            headnorm_rope(C, raw, 4, 128, gq, None, 0, 0, outb, w1, w1, ss, rs)
            transpose_cols(C, outb, outb.t, 4, lambda c0, c1, t=t: qT[:, c0:c1, t * 128:(t + 1) * 128], [qT.t], [0, 1])
        for h in range(4):
            po, pd = C.psum[4], C.psum[5]
            for mt in range(2):
                ps = C.psum[2 + (kk % 2)]
                pt = pb[kk % 3]
                kk += 1
                P.op("pe", lambda e, h=h, mt=mt, ps=ps: e.matmul(out=ps[:], lhsT=KT[:, h, mt * 128:(mt + 1) * 128],
                                                               rhs=qT[:, h, :], start=True, stop=True), [KT.t, qT.t], [ps.t])
                P.op("act", lambda e, ps=ps, pt=pt: e.activation(out=pt[:], in_=ps[:], func=AF.Exp), [ps.t], [pt.t])
                P.op("pe", lambda e, h=h, mt=mt, pt=pt: e.matmul(out=po[:], lhsT=Vm[:, mt, h * 128:(h + 1) * 128], rhs=pt[:],
                                                               start=(mt == 0), stop=(mt == 1)), [Vm.t, pt.t], [po.t])
                P.op("pe", lambda e, mt=mt, pt=pt: e.matmul(out=pd[:], lhsT=C.ones[:], rhs=pt[:], start=(mt == 0),
                                                          stop=(mt == 1)), [C.ones.t, pt.t], [pd.t])
            P.op("dve", lambda e, pd=pd: e.reciprocal(out=R[:], in_=pd[:]), [pd.t], [R.t])
            P.op("dve", lambda e, po=po, h=h: e.tensor_tensor(out=xoT[:, h, :], in0=po[:], in1=R[:], op=ALU.mult),
                 [po.t, R.t], [xoT.tr[h]])
        for t in range(4):
            r0 = g * 512 + t * 128
            x = xs[t]
            for hf in range(2):
                py = C.psum[6 + hf]
                for h in range(4):
                    P.op("pe", lambda e, h=h, t=t, hf=hf, py=py: e.matmul(
                        out=py[:], lhsT=xoT[:, h, t * 128:(t + 1) * 128], rhs=Wo[:, h, hf * 512:(hf + 1) * 512],
                        start=(h == 0), stop=(h == 3)), [xoT.tr[h], Wo.t], [py.t])
                P.op("dve", lambda e, hf=hf, py=py, x=x: e.tensor_tensor(out=x[:, hf * 512:(hf + 1) * 512], in0=py[:],
                                                                       in1=x[:, hf * 512:(hf + 1) * 512], op=ALU.add),
                     [py.t, x.t], [x.t])
            P.dma("sp", x_d[r0:r0 + 128, :], x[:], reads=[x.t])
    P.barrier()
    C.release(m)


def phase_mla_proj(C, x_d, gain_d, wdq_d, qln_d, wuq_d, wdkv_d, kvln_d, wukv_d, nq_d, nk_d, cs_d, scr):
    P = C.P
    m = C.mark()
    stg = [C.sb([128, 1024], F32), C.sb([128, 1024], F32)]
    Wdq = load_weight(C, wdq_d, D, 512, stg)
    Wuq = load_weight(C, wuq_d, 512, 1536, stg)
    Wdkv = load_weight(C, wdkv_d, D, 288, stg)
    Wukv = load_weight(C, wukv_d, 256, 2048, stg)
    gain = load_bcast(C, gain_d, D)
    gql = load_bcast(C, qln_d, 512)
    gkvl = load_bcast(C, kvln_d, 256)
    gq = scaled_gain(C, nq_d, 96, 96 ** -0.5)
    gk = scaled_gain(C, nk_d, 96, 1.0)
    import os
    MS = os.environ.get("MSKIP", "")
    cs = C.sb([128, C.NT, 32], F32)
    if "c" not in MS:
        P.dma("sp", cs[:], cs_d.rearrange("(t p) c -> p t c", p=128), writes=[cs.t])
    xt = [C.sb([128, D], F32) for _ in range(2)]
    hb = C.sb([128, D], BF16)
    ss = C.sb([128, 16], F32)
    rs = C.sb([128, 16], F32)
    hT = C.sb([128, 8, 128], BF16)
    cq = C.sb([128, 512], F32)
    cqb = C.sb([128, 512], BF16)
    cqT = C.sb([128, 4, 128], BF16)
    dkv = C.sb([128, 288], F32)
    ckb = C.sb([128, 256], BF16)
    ckT = C.sb([128, 2, 128], BF16)
    raw = C.sb([128, 1536], F32)
    w1 = C.sb([128, 1536], F32)
    w2 = C.sb([128, 1536], F32)
    outb = [C.sb([128, 1536], BF16) for _ in range(2)]
    tbuf = [C.sb([128, 12, 128], BF16) for _ in range(2)]
    vb = C.sb([128, 16, 64], BF16)
    kvs = [C.sb([128, 512], F32) for _ in range(2)]
    qT_d, kT_d, v_d = scr["mqT"], scr["mkT"], scr["mv"]
    for t in range(C.NT):
        x = xt[t % 2]
        P.dma("sp", x[:], x_d[t * 128:(t + 1) * 128, :], writes=[x.t])
        norm_rows(C, x, gain, hb, hb, ss, rs)
        transpose_cols(C, hb, hb.t, 8, lambda c0, c1: hT[:, c0:c1, :], [hT.t], [0, 1])
        ps = proj(C, hT, 8, Wdq, 0, 512, 2)
        copy_op(C, "act", cq[:], ps[:], [ps.t], [cq.t])
        P.op("act", lambda e: e.activation(out=w1[:, 0:512], in_=cq[:], func=AF.Square, accum_out=ss[:, 0:1]),
             [cq.t], [w1.t, ss.t])
        rstd_from_ss(C, ss, rs, 1, 512)
        P.op("dve", lambda e: e.scalar_tensor_tensor(out=cqb[:], in0=cq[:], scalar=rs[:, 0:1], in1=gql[:], op0=ALU.mult,
                                                     op1=ALU.mult), [cq.t, rs.t, gql.t], [cqb.t])
        transpose_cols(C, cqb, cqb.t, 4, lambda c0, c1: cqT[:, c0:c1, :], [cqT.t], [0, 1])
        ps = proj(C, hT, 8, Wdkv, 0, 288, 3)
        copy_op(C, "act", dkv[:], ps[:, 0:288], [ps.t], [dkv.t])
        P.op("act", lambda e: e.activation(out=w1[:, 0:256], in_=dkv[:, 0:256], func=AF.Square, accum_out=ss[:, 0:1]),
             [dkv.t, ss.t], [w1.t, ss.t])
        rstd_from_ss(C, ss, rs, 1, 256)
        P.op("dve", lambda e: e.scalar_tensor_tensor(out=ckb[:], in0=dkv[:, 0:256], scalar=rs[:, 0:1], in1=gkvl[:],
                                                     op0=ALU.mult, op1=ALU.mult), [dkv.t, rs.t, gkvl.t], [ckb.t])
        transpose_cols(C, ckb, ckb.t, 2, lambda c0, c1: ckT[:, c0:c1, :], [ckT.t], [0, 1])
        r3 = raw[:].rearrange("p (h d) -> p h d", h=16)
        if "Q" in MS:
            continue
        for g4 in range(4):
            ps = proj(C, cqT, 4, Wuq, g4 * 384, 384, 4 + g4)
            copy_op(C, "act" if g4 % 2 else "dve", raw[:, g4 * 384:(g4 + 1) * 384], ps[:, 0:384], [ps.t, raw.t], [raw.t])
        import os
        MS = os.environ.get("MSKIP", "")
        headnorm_rope(C, raw, 16, 96, gq, None if "r" in MS else cs[:, t, :], 64, 16, outb[0], w1, w2, ss, rs, cs.t)
        store_T(C, outb[0], 1536, qT_d, t, tbuf[0])
        if "K" in MS:
            continue
        for g4 in range(4):
            ps = proj(C, ckT, 2, Wukv, g4 * 512, 512, 4 + g4)
            kv = kvs[g4 % 2]
            copy_op(C, "act", kv[:], ps[:], [ps.t], [kv.t])
            p3 = kv[:].rearrange("p (h d) -> p h d", h=4)
            copy_op(C, "pool", r3[:, g4 * 4:(g4 + 1) * 4, 0:64], p3[:, :, 0:64], [kv.t, raw.t], [raw.t])
            copy_op(C, "dve", vb[:, g4 * 4:(g4 + 1) * 4, :], p3[:, :, 64:128], [kv.t, vb.t], [vb.t])
        copy_op(C, "dve" if "b" in MS else "pool", r3[:, :, 64:96], dkv[:, 256:288].unsqueeze(1).to_broadcast([128, 16, 32]), [dkv.t, raw.t], [raw.t])
        if "v" not in MS:
            P.dma("sp", v_d[t * 128:(t + 1) * 128, :], vb[:].rearrange("p h d -> p (h d)"), reads=[vb.t])
        headnorm_rope(C, raw, 16, 96, gk, None if "r" in MS else cs[:, t, :], 64, 16, outb[1], w1, w2, ss, rs, cs.t)
        store_T(C, outb[1], 1536, kT_d, t, tbuf[1])
    P.barrier()
    C.release(m)


def phase_mla_attn(C, scr):
    P = C.P
    S = C.S
    m = C.mark()
    QT = [C.sb([128, S], BF16) for _ in range(2)]
    KT = [C.sb([128, S], BF16) for _ in range(2)]
    V = [C.sb([128, C.NT, 128], BF16) for _ in range(2)]
    for v in V:
        P.op("pool", lambda e, v=v: e.memset(v[:, :, 64:128], 1.0), [], [v.t])
    pb = [C.sb([128, 512], BF16) for _ in range(5)]
    R = C.sb([128, 512], F32)
    ob = [C.sb([128, 512], BF16) for _ in range(2)]
    kc = [0]
    k = 0
    for h in range(16):
        Vh, Q, K = V[h % 2], QT[h % 2], KT[h % 2]
        for t0 in range(0, C.NT, 8):
            P.dma("sp", Vh[:, t0:min(t0 + 8, C.NT), 0:64], scr["mv"].rearrange("(t p) c -> p t c", p=128)[:, t0:min(t0 + 8, C.NT), h * 64:(h + 1) * 64],
                  reads=[Vh.t], writes=[Vh.t])
        load_head(C, scr["mqT"][h * 96:(h + 1) * 96, :], scr["mkT"][h * 96:(h + 1) * 96, :], 96, Q, K)
        for qs in range(C.NG):
            bo = 2 + (k % 2)
            k += 1
            softmax_block(C, Q, K, 96, VL(lambda j, Vh=Vh: Vh[:, j, :], Vh.t), qs, 0, pb, bo, None, kc, sbanks=(0, 1, 4, 5))
            po = C.psum[bo]
            P.op("dve", lambda e, po=po: e.reciprocal(out=R[0:64, :], in_=po[64:128, :]), [po.t], [R.t])
            o = ob[k % 2]
            P.op("dve", lambda e, po=po, o=o: e.tensor_tensor(out=o[0:64, :], in0=po[0:64, :], in1=R[0:64, :], op=ALU.mult),
                 [po.t, R.t, o.t], [o.t])
            P.dma("sp", scr["mixT"][h * 64:(h + 1) * 64, qs * 512:(qs + 1) * 512], o[0:64, :], reads=[o.t])
    P.barrier()
    C.release(m)


WNAMES = ["ffn_norm", "ffn_w_gate", "ffn_w_up", "ffn_w_down", "mix_norm", "ab_w_in", "ab_w_out", "diff_q_norm",
          "diff_k_norm", "diff_subln", "mla_w_dq", "mla_q_norm", "mla_w_uq", "mla_w_dkv", "mla_kv_norm", "mla_w_ukv",
          "mla_qk_norm_q", "mla_qk_norm_k", "mla_w_o", "xm_norm", "xm_mem_norm", "xm_w_q", "xm_w_kv", "xm_q_norm",
          "xm_k_norm", "xm_w_o"]


def build(S, shapes, phases=None):
    nc = bass.Bass("TRN2", target_bir_lowering=False)
    ins = {}
    for name, shp in shapes.items():
        ins[name] = nc.dram_tensor(name, list(shp), F32, kind="ExternalInput").ap()
    out = nc.dram_tensor("out", [S, D], F32, kind="ExternalOutput").ap()
    scr_t = nc.dram_tensor("scr", [5120 * S], BF16).ap()

    def sv(off, rows, cols):
        return scr_t[off * S:(off + rows * cols // S) * S].rearrange("(r c) -> r c", c=cols)
    scr = {"qaT": sv(0, 512, S), "kaT": sv(512, 512, S), "va": sv(1024, S, 512), "qbT": sv(1536, 512, S),
           "kbT": sv(2048, 512, S), "vb": sv(2560, S, 512),
           "mqT": sv(0, 1536, S), "mkT": sv(1536, 1536, S), "mv": sv(3072, S, 1024), "mixT": sv(4096, 1024, S)}
    C = Ctx(nc, S)
    P = C.P
    setup_consts(C, ins["c_ident"], ins["c_masks"], ins["c_tri"])
    xin = Tr()
    P.dma("sp", out, ins["x"], writes=[xin])
    P.barrier()
    w = ins
    allp = ["ffn00", "abproj", "diff", "sb", "out0", "xm0", "ffn01", "ffn10", "mlaproj", "mlaattn", "out1", "xm1", "ffn11"]
    for ph in (allp if phases is None else phases):
        if ph.startswith("ffn"):
            l, i = int(ph[3]), int(ph[4])
            phase_ffn(C, out, w["ffn_norm"][l, i], w["ffn_w_gate"][l, i], w["ffn_w_up"][l, i], w["ffn_w_down"][l, i])
        elif ph == "abproj":
            phase_ab_proj(C, out, w["mix_norm"][0], w["ab_w_in"][0], w["diff_q_norm"][0], w["diff_k_norm"][0], w["c_cs64"], scr)
        elif ph == "diff":
            phase_diff_attn(C, scr, w["c_lam"], w["diff_subln"][0], 0.8 - 0.6 * math.exp(0.0))
        elif ph == "sb":
            phase_sb_attn(C, scr)
        elif ph == "out0":
            phase_outproj(C, out, scr["mixT"], w["ab_w_out"][0])
        elif ph == "out1":
            phase_outproj(C, out, scr["mixT"], w["mla_w_o"][0])
        elif ph.startswith("xm"):
            l = int(ph[2])
            phase_xm(C, out, w["mem"], w["xm_norm"][l], w["xm_mem_norm"][l], w["xm_w_q"][l], w["xm_w_kv"][l],
                     w["xm_q_norm"][l], w["xm_k_norm"][l], w["xm_w_o"][l])
        elif ph == "mlaproj":
            phase_mla_proj(C, out, w["mix_norm"][1], w["mla_w_dq"][0], w["mla_q_norm"][0], w["mla_w_uq"][0], w["mla_w_dkv"][0],
                           w["mla_kv_norm"][0], w["mla_w_ukv"][0], w["mla_qk_norm_q"][0], w["mla_qk_norm_k"][0], w["c_cs32"], scr)
        elif ph == "mlaattn":
            phase_mla_attn(C, scr)
    P.emit(final_wait_ops=list(C.P.dma_hist[-N_DMA_SEMS:]))
    return nc


def make_consts(S):
    c = {}
    c["c_ident"] = np.eye(128, dtype=np.float32)
    kk = np.arange(128)[:, None]
    qq = np.arange(512)[None, :]
    masks = np.zeros((8, 128, 512), np.float32)
    for j in range(4):
        k = 128 * j + kk
        masks[j] = ((k // 64) <= (qq // 64)).astype(np.float32)
        masks[4 + j] = (k < qq).astype(np.float32)
    c["c_masks"] = masks
    c["c_tri"] = (np.arange(128)[:, None] >= np.arange(128)[None, :]).astype(np.float32)
    pos = np.arange(S, dtype=np.float32)[:, None]
    for d, nm in ((64, "c_cs64"), (32, "c_cs32")):
        inv = (1.0 / (np.float32(10000.0) ** (np.arange(0, d, 2, dtype=np.float32) / np.float32(d)))).astype(np.float32)
        ang = (pos * inv[None, :]).astype(np.float32)
        c[nm] = np.concatenate([np.cos(ang), np.sin(ang)], axis=1).astype(np.float32)
    return c


def prep_inputs(inputs, S):
    shared = {k: np.ascontiguousarray(np.asarray(inputs[k], dtype=np.float32)) for k in WNAMES}
    shared["c_lam"] = np.ascontiguousarray(np.concatenate([np.asarray(inputs[k], np.float32).reshape(-1) for k in
                                           ("diff_lambda_q1", "diff_lambda_k1", "diff_lambda_q2", "diff_lambda_k2")]))
    shared.update(make_consts(S))
    x = np.asarray(inputs["x"], np.float32)
    mem = np.asarray(inputs["mem"], np.float32)
    maps = []
    for b in range(x.shape[0]):
        mmap = dict(shared)
        mmap["x"] = np.ascontiguousarray(x[b])
        mmap["mem"] = np.ascontiguousarray(mem[b])
        maps.append(mmap)
    return maps


def kernel(**inputs):
    S = inputs["x"].shape[1]
    maps = prep_inputs(inputs, S)
    shapes = {k: v.shape for k, v in maps[0].items()}
    nc = build(S, shapes)
    res = run_bass_kernel_spmd(nc, maps, core_ids=list(range(len(maps))))
    return np.stack([np.asarray(r["out"], dtype=np.float32) for r in res.results], axis=0)
```
